# Optimizing a Trainium2 kernel written in Bass

```python
import jax
import jax.numpy as jnp
from jax import lax
import numpy as np

D_MODEL = 1024
BATCH = 2
SEQ = 16384
DEPTH = 4

N_MIXERS = 3
N_ATTN = (DEPTH + 2) // 3
N_CONV = (DEPTH + 1) // 3
N_RWKV = DEPTH // 3
NORM_EPS = 1e-6

ATTN_HEADS = 16
ATTN_HEAD_DIM = 64
DILATED_GROUPS = ((128, 1), (512, 4), (2048, 16))
N_GROUPS = len(DILATED_GROUPS)
ATTN_BLOCK = 128
ROPE_THETA = 10000.0
NEG_INF = -1e30

CONV_WIDTH = 31
CONV_CH = D_MODEL

RWKV_HEAD_DIM = 64
RWKV_HEADS = D_MODEL // RWKV_HEAD_DIM
DECAY_LORA = 64
AAA_LORA = 64
GATE_LORA = 128
RWKV_GN_EPS = 1e-5 * RWKV_HEAD_DIM

FFN_HIDDEN = ((8 * D_MODEL // 3 + 255) // 256) * 256

kernel_name = "hybrid_dilated_conv_rwkv7_adaln_trunk"


def rms_norm(t, g, eps=NORM_EPS):
    t32 = t.astype(jnp.float32)
    return (t32 * lax.rsqrt(jnp.mean(t32 * t32, -1, keepdims=True) + eps) * g).astype(t.dtype)


def layer_norm(t, g, b, eps=NORM_EPS):
    t32 = t.astype(jnp.float32)
    mu = jnp.mean(t32, -1, keepdims=True)
    var = jnp.mean(jnp.square(t32 - mu), -1, keepdims=True)
    return ((t32 - mu) * lax.rsqrt(var + eps) * g + b).astype(t.dtype)


def rope(t, cos, sin):
    t1, t2 = jnp.split(t, 2, axis=-1)
    return jnp.concatenate([t1 * cos - t2 * sin, t2 * cos + t1 * sin], -1).astype(t.dtype)


def dilated_window_attention(q, k, v, dilation, steps):
    B, S, H, E = q.shape
    L = S // dilation
    nb = -(-L // ATTN_BLOCK)
    Lp = nb * ATTN_BLOCK

    def to_blocks(t):
        t = t.reshape(B, L, dilation, H, E).transpose(0, 2, 1, 3, 4)
        t = jnp.pad(t, ((0, 0), (0, 0), (0, Lp - L), (0, 0), (0, 0)))
        return t.reshape(B, dilation, nb, ATTN_BLOCK, H, E)

    def with_prev(t):
        prev = jnp.pad(t[:, :, :-1], ((0, 0), (0, 0), (1, 0), (0, 0), (0, 0), (0, 0)))
        return jnp.concatenate([prev, t], axis=3)

    qb = to_blocks(q)
    kb = with_prev(to_blocks(k))
    vb = with_prev(to_blocks(v))
    s = jnp.einsum('brnihe,brnjhe->brnhij', qb, kb).astype(jnp.float32) * (E ** -0.5)
    i = jnp.arange(ATTN_BLOCK)[:, None]
    j = jnp.arange(2 * ATTN_BLOCK)[None, :]
    dist = i + ATTN_BLOCK - j
    blk = jnp.arange(nb)[:, None, None]
    valid = (dist >= 0) & (dist <= steps) & (blk * ATTN_BLOCK + j >= ATTN_BLOCK)
    s = jnp.where(valid[None, None, :, None], s, NEG_INF)
    lse = jax.nn.logsumexp(s, axis=-1)
    p = jnp.exp(s - lse[..., None]).astype(v.dtype)
    o = jnp.einsum('brnhij,brnjhe->brnihe', p, vb)

    def from_blocks(t):
        t = t.reshape((B, dilation, Lp) + t.shape[4:])[:, :, :L]
        return jnp.swapaxes(t, 1, 2).reshape((B, S) + t.shape[3:])

    return from_blocks(o), from_blocks(jnp.swapaxes(lse, 3, 4))


def dilated_attention_mixer(h, cos, sin, w_qkv, q_gain, k_gain, w_o):
    B, S, _ = h.shape
    qkv = (h @ w_qkv).reshape(B, S, N_GROUPS, 3, ATTN_HEADS, ATTN_HEAD_DIM)
    q = rope(rms_norm(qkv[:, :, :, 0], q_gain), cos, sin)
    k = rope(rms_norm(qkv[:, :, :, 1], k_gain), cos, sin)
    v = qkv[:, :, :, 2]
    outs, lses = [], []
    for g, (window, dilation) in enumerate(DILATED_GROUPS):
        o_g, lse_g = dilated_window_attention(q[:, :, g], k[:, :, g], v[:, :, g], dilation, window // dilation)
        outs.append(o_g)
        lses.append(lse_g)
    wts = jax.nn.softmax(jnp.stack(lses), axis=0)
    o = jnp.einsum('gbsh,gbshe->bshe', wts.astype(v.dtype), jnp.stack(outs))
    return o.reshape(B, S, ATTN_HEADS * ATTN_HEAD_DIM) @ w_o


def conformer_conv_mixer(h, w_pw1, b_pw1, w_dw, b_dw, ln_g, ln_b, w_pw2, b_pw2):
    u = jax.nn.glu(h @ w_pw1 + b_pw1, axis=-1)
    u = lax.conv_general_dilated(
        u, w_dw[:, None, :].astype(u.dtype), window_strides=(1,),
        padding=((CONV_WIDTH - 1, 0),),
        dimension_numbers=('NWC', 'WIO', 'NWC'), feature_group_count=CONV_CH) + b_dw
    u = layer_norm(u, ln_g, ln_b)
    return jax.nn.silu(u) @ w_pw2 + b_pw2


def wkv7_scan(r, decay, k, v, a_vec, b_vec):
    B, S, H, N = r.shape

    def step(state, inp):
        r_t, d_t, k_t, v_t, a_t, b_t = inp
        sa = jnp.einsum('bhij,bhj->bhi', state, a_t)
        state = state * d_t[:, :, None, :] + sa[..., None] * b_t[:, :, None, :] + v_t[..., None] * k_t[:, :, None, :]
        return state, jnp.einsum('bhij,bhj->bhi', state, r_t)

    xs = tuple(jnp.moveaxis(t, 1, 0) for t in (r, decay, k, v, a_vec, b_vec))
    _, ys = lax.scan(step, jnp.zeros((B, H, N, N), jnp.float32), xs)
    return jnp.moveaxis(ys, 0, 1)


def rwkv7_mixer(h, mu, w_r, w_k, w_v, w_o, w0, w_lora_a, w_lora_b, a0, a_lora_a, a_lora_b,
                g_lora_a, g_lora_b, k_k, k_a, r_k, ln_g, ln_b):
    B, S, D = h.shape
    xx = jnp.pad(h[:, :-1], ((0, 0), (1, 0), (0, 0))) - h
    xr, xw, xk, xv, xa, xg = (h + xx * mu[m] for m in range(6))
    r = xr @ w_r
    k = xk @ w_k
    v = xv @ w_v
    w = -jax.nn.softplus(-(w0 + jnp.tanh(xw @ w_lora_a) @ w_lora_b)) - 0.5
    a = jax.nn.sigmoid(a0 + (xa @ a_lora_a) @ a_lora_b)
    g = jax.nn.sigmoid(xg @ g_lora_a) @ g_lora_b

    def heads(t):
        return t.reshape(B, S, RWKV_HEADS, RWKV_HEAD_DIM).astype(jnp.float32)

    kk = heads(k * k_k)
    kk = kk / jnp.maximum(jnp.sqrt(jnp.sum(kk * kk, -1, keepdims=True)), 1e-12)
    k = k * (1 + (a - 1) * k_a)
    r, k, v, a = heads(r), heads(k), heads(v), heads(a)
    decay = jnp.exp(-jnp.exp(heads(w)))
    y = wkv7_scan(r, decay, k, v, -kk, kk * a)
    mean = jnp.mean(y, -1, keepdims=True)
    var = jnp.mean(jnp.square(y - mean), -1, keepdims=True)
    y = ((y - mean) * lax.rsqrt(var + RWKV_GN_EPS)).reshape(B, S, D) * ln_g + ln_b
    bonus = (jnp.sum(r * k * r_k, -1, keepdims=True) * v).reshape(B, S, D)
    return ((y + bonus) * g).astype(h.dtype) @ w_o


def swiglu_ffn(h, w_gate, w_up, w_down):
    return (jax.nn.silu(h @ w_gate) * (h @ w_up)) @ w_down


def setup_inputs(seed: int = 0) -> dict:
    key = jax.random.key(seed)
    ks = iter(jax.random.split(key, 48))

    def nrm(shape, scale):
        return jax.random.normal(next(ks), shape, jnp.float32) * scale

    def gain(shape):
        return 1.0 + nrm(shape, 0.02)

    D, F = D_MODEL, FFN_HIDDEN
    HE = ATTN_HEADS * ATTN_HEAD_DIM
    x = nrm((BATCH, SEQ, D), 1.0)
    c = nrm((BATCH, D), 1.0)
    offset = jax.random.randint(next(ks), (BATCH, 1), 0, 4096, dtype=jnp.int32)
    positions = offset + jnp.arange(SEQ, dtype=jnp.int32)[None, :]
    return {
        "x": x,
        "c": c,
        "positions": positions,
        "ada_w": nrm((DEPTH, D, 6 * D), 0.02),
        "ada_b": nrm((DEPTH, 6 * D), 0.02),
        "norm_mix_g": gain((DEPTH, D)),
        "norm_ffn_g": gain((DEPTH, D)),
        "ffn_w_gate": nrm((DEPTH, D, F), D ** -0.5),
        "ffn_w_up": nrm((DEPTH, D, F), D ** -0.5),
        "ffn_w_down": nrm((DEPTH, F, D), F ** -0.5),
        "attn_w_qkv": nrm((N_ATTN, D, N_GROUPS * 3 * HE), D ** -0.5),
        "attn_q_gain": gain((N_ATTN, ATTN_HEAD_DIM)),
        "attn_k_gain": gain((N_ATTN, ATTN_HEAD_DIM)),
        "attn_w_o": nrm((N_ATTN, HE, D), HE ** -0.5),
        "conv_w_pw1": nrm((N_CONV, D, 2 * CONV_CH), D ** -0.5),
        "conv_b_pw1": nrm((N_CONV, 2 * CONV_CH), 0.02),
        "conv_w_dw": nrm((N_CONV, CONV_WIDTH, CONV_CH), CONV_WIDTH ** -0.5),
        "conv_b_dw": nrm((N_CONV, CONV_CH), 0.02),
        "conv_ln_g": gain((N_CONV, CONV_CH)),
        "conv_ln_b": nrm((N_CONV, CONV_CH), 0.02),
        "conv_w_pw2": nrm((N_CONV, CONV_CH, D), CONV_CH ** -0.5),
        "conv_b_pw2": nrm((N_CONV, D), 0.02),
        "rwkv_mu": jax.random.uniform(next(ks), (N_RWKV, 6, D), jnp.float32),
        "rwkv_w_r": nrm((N_RWKV, D, D), D ** -0.5),
        "rwkv_w_k": nrm((N_RWKV, D, D), D ** -0.5),
        "rwkv_w_v": nrm((N_RWKV, D, D), D ** -0.5),
        "rwkv_w_o": nrm((N_RWKV, D, D), D ** -0.5),
        "rwkv_w0": jax.random.uniform(next(ks), (N_RWKV, D), jnp.float32, -5.0, 1.0),
        "rwkv_w_lora_a": nrm((N_RWKV, D, DECAY_LORA), D ** -0.5),
        "rwkv_w_lora_b": nrm((N_RWKV, DECAY_LORA, D), 0.1 * DECAY_LORA ** -0.5),
        "rwkv_a0": nrm((N_RWKV, D), 0.1),
        "rwkv_a_lora_a": nrm((N_RWKV, D, AAA_LORA), D ** -0.5),
        "rwkv_a_lora_b": nrm((N_RWKV, AAA_LORA, D), 0.1 * AAA_LORA ** -0.5),
        "rwkv_g_lora_a": nrm((N_RWKV, D, GATE_LORA), D ** -0.5),
        "rwkv_g_lora_b": nrm((N_RWKV, GATE_LORA, D), GATE_LORA ** -0.5),
        "rwkv_k_k": 0.85 + nrm((N_RWKV, D), 0.02),
        "rwkv_k_a": gain((N_RWKV, D)),
        "rwkv_r_k": nrm((N_RWKV, RWKV_HEADS, RWKV_HEAD_DIM), 0.1),
        "rwkv_ln_g": gain((N_RWKV, D)),
        "rwkv_ln_b": nrm((N_RWKV, D), 0.02),
    }


def reference(x, c, positions, ada_w, ada_b, norm_mix_g, norm_ffn_g, ffn_w_gate, ffn_w_up, ffn_w_down,
              attn_w_qkv, attn_q_gain, attn_k_gain, attn_w_o,
              conv_w_pw1, conv_b_pw1, conv_w_dw, conv_b_dw, conv_ln_g, conv_ln_b, conv_w_pw2, conv_b_pw2,
              rwkv_mu, rwkv_w_r, rwkv_w_k, rwkv_w_v, rwkv_w_o, rwkv_w0, rwkv_w_lora_a, rwkv_w_lora_b,
              rwkv_a0, rwkv_a_lora_a, rwkv_a_lora_b, rwkv_g_lora_a, rwkv_g_lora_b,
              rwkv_k_k, rwkv_k_a, rwkv_r_k, rwkv_ln_g, rwkv_ln_b):
    B, S, D = x.shape
    inv_freq = jnp.power(ROPE_THETA, -jnp.arange(0, ATTN_HEAD_DIM, 2, dtype=jnp.float32) / ATTN_HEAD_DIM)
    ang = positions.astype(jnp.float32)[..., None] * inv_freq
    cos = jnp.cos(ang)[:, :, None, None]
    sin = jnp.sin(ang)[:, :, None, None]
    c_act = jax.nn.silu(c)
    for i in range(DEPTH):
        mod = (c_act @ ada_w[i] + ada_b[i]).reshape(B, 6, D)
        shift_m, scale_m, gate_m, shift_f, scale_f, gate_f = (mod[:, m, None, :] for m in range(6))
        h = rms_norm(x, norm_mix_g[i]) * (1 + scale_m) + shift_m
        kind, j = i % N_MIXERS, i // N_MIXERS
        if kind == 0:
            y = dilated_attention_mixer(h, cos, sin, attn_w_qkv[j], attn_q_gain[j], attn_k_gain[j], attn_w_o[j])
        elif kind == 1:
            y = conformer_conv_mixer(h, conv_w_pw1[j], conv_b_pw1[j], conv_w_dw[j], conv_b_dw[j],
                                     conv_ln_g[j], conv_ln_b[j], conv_w_pw2[j], conv_b_pw2[j])
        else:
            y = rwkv7_mixer(h, rwkv_mu[j], rwkv_w_r[j], rwkv_w_k[j], rwkv_w_v[j], rwkv_w_o[j],
                            rwkv_w0[j], rwkv_w_lora_a[j], rwkv_w_lora_b[j],
                            rwkv_a0[j], rwkv_a_lora_a[j], rwkv_a_lora_b[j],
                            rwkv_g_lora_a[j], rwkv_g_lora_b[j],
                            rwkv_k_k[j], rwkv_k_a[j], rwkv_r_k[j], rwkv_ln_g[j], rwkv_ln_b[j])
        x = x + gate_m * y
        h = rms_norm(x, norm_ffn_g[i]) * (1 + scale_f) + shift_f
        x = x + gate_f * swiglu_ffn(h, ffn_w_gate[i], ffn_w_up[i], ffn_w_down[i])
    return x
```

```python
import numpy as np
import concourse.bass as bass
import concourse.mybir as mybir
from concourse.bass_utils import run_bass_kernel_spmd

F32 = mybir.dt.float32
F32R = mybir.dt.float32r
BF16 = mybir.dt.bfloat16
I32 = mybir.dt.int32
AF = mybir.ActivationFunctionType
ALU = mybir.AluOpType
AX = mybir.AxisListType

NCORES = 8


class Buf:
    __slots__ = ("name", "t", "last_w", "readers", "dsem", "dcnt", "excl")

    def __init__(self, name, t):
        self.name = name
        self.t = t
        self.last_w = None
        self.readers = []
        self.dsem = None
        self.dcnt = 0
        self.excl = False

    def __getitem__(self, idx):
        return self.t[idx]


class Op:
    __slots__ = ("eng", "fn", "deps", "signal", "ev_sem", "ev_val", "is_dma", "ndma")

    def __init__(self, eng, fn, is_dma=False, ndma=1):
        self.eng = eng
        self.fn = fn
        self.deps = []
        self.signal = False
        self.ev_sem = None
        self.ev_val = None
        self.is_dma = is_dma
        self.ndma = ndma


ENGS = ("pe", "dve", "act", "pool", "sp")


class Prog:
    def __init__(self, nc, stack):
        self.nc = nc
        self.stack = stack
        self.ops = {e: [] for e in ENGS}
        self.esem = {}
        for e in ENGS:
            self.esem[e] = stack.enter_context(nc.semaphore("es_" + e))
        self.nbuf = 0
        self.dma_ops = []
        self.all_sems = [self.esem[e] for e in ENGS]

    def sb(self, name, shape, dt=F32):
        t = self.stack.enter_context(self.nc.sbuf_tensor(name, list(shape), dt))
        return Buf(name, t)

    def ps(self, name, shape, dt=F32):
        t = self.stack.enter_context(self.nc.psum_tensor(name, list(shape), dt))
        b = Buf(name, t)
        b.excl = True
        return b

    def dram(self, name, ap):
        return Buf(name, ap)

    def view(self, name, t):
        return Buf(name, t)

    def _add_deps(self, op, reads, writes):
        deps = op.deps
        for b in reads:
            if b.last_w is not None:
                deps.append(b.last_w)
            if b.excl:
                for r in b.readers:
                    if r.eng != op.eng:
                        deps.append(r)
        for b in writes:
            if b.last_w is not None:
                deps.append(b.last_w)
            deps.extend(b.readers)
        for b in reads:
            b.readers.append(op)
        for b in writes:
            b.last_w = op
            b.readers = []

    def op(self, eng, fn, reads=(), writes=()):
        o = Op(eng, fn)
        self._add_deps(o, reads, writes)
        self.ops[eng].append(o)
        return o

    def dma(self, eng, fns, reads=(), writes=(), sem_buf=None):
        if not isinstance(fns, (list, tuple)):
            fns = [fns]
        if sem_buf is None:
            sem_buf = writes[0] if writes else reads[0]
        if sem_buf.dsem is None:
            sem_buf.dsem = self.stack.enter_context(self.nc.semaphore("ds_%s_%d" % (sem_buf.name, self.nbuf)))
            self.nbuf += 1
            self.all_sems.append(sem_buf.dsem)
        o = Op(eng, fns, is_dma=True, ndma=len(fns))
        self._add_deps(o, reads, writes)
        sem_buf.dcnt += 16 * len(fns)
        o.ev_sem = sem_buf.dsem
        o.ev_val = sem_buf.dcnt
        o.signal = True
        self.ops[eng].append(o)
        self.dma_ops.append(o)
        return o

    def emit(self):
        nc = self.nc
        fin = Op("sp", None)
        fin.deps = list(self.dma_ops)
        for e in ENGS:
            if self.ops[e] and e != "sp":
                last = self.ops[e][-1]
                fin.deps.append(last)
        self.ops["sp"].append(fin)
        for e in ENGS:
            for o in self.ops[e]:
                for d in o.deps:
                    if d.is_dma:
                        continue
                    if d.eng == "pe" and o.eng == "pe":
                        continue
                    d.signal = True
        for e in ENGS:
            k = 0
            for o in self.ops[e]:
                if o.is_dma:
                    continue
                if o.signal:
                    k += 1
                    o.ev_sem = self.esem[e]
                    o.ev_val = k
        handles = {"pe": "tensor", "dve": "vector", "act": "scalar", "pool": "gpsimd", "sp": "sync"}
        stats = {}
        sems = list(self.all_sems)
        with nc.Block() as b0:
            def clr(eng):
                for sm in sems:
                    eng.sem_clear(sm)
            b0.sync(clr)
        with nc.Block() as block:
            for e in ENGS:
                ops = self.ops[e]
                if not ops:
                    continue

                def body(eng, ops=ops, e=e):
                    seen = {}
                    nw = 0
                    for o in ops:
                        need = {}
                        for d in o.deps:
                            if d.eng == "pe" and e == "pe" and not d.is_dma:
                                continue
                            s = d.ev_sem
                            key = id(s)
                            if seen.get(key, 0) >= d.ev_val:
                                continue
                            if key not in need or need[key][1] < d.ev_val:
                                need[key] = (s, d.ev_val)
                        for key, (s, v) in need.items():
                            eng.wait_ge(s, v)
                            seen[key] = v
                            nw += 1
                        if o.fn is None:
                            continue
                        if o.is_dma:
                            for f in o.fn:
                                f(eng).then_inc(o.ev_sem, 16)
                        else:
                            ins = o.fn(eng)
                            if o.signal:
                                ins.then_inc(o.ev_sem, 1)
                    stats[e] = (len(ops), nw)

                getattr(block, handles[e])(body)
        with nc.Block() as b2:
            def clr2(eng):
                for sm in sems:
                    eng.sem_clear(sm)
            b2.sync(clr2)
        self.stats = stats
        return stats


D = 1024
KC = 8
FH = 2816
FC = 22
TT = 512
NORM_EPS = 1e-6
SLOT = 2048


def MM(P, out_ap, lhsT, rhs, start, stop, reads, writes):
    return P.op("pe", lambda e: e.matmul(out_ap, lhsT, rhs, start=start, stop=stop), reads=reads, writes=writes)


class WStream:
    def __init__(self, P, nstage=2, nring=4, lookahead=2, qeng="sp"):
        self.P = P
        self.stage = [P.sb("wst%d" % i, [128, SLOT], F32) for i in range(nstage)]
        self.ring = [P.sb("wrg%d" % i, [128, SLOT], F32R) for i in range(nring)]
        self.pieces = []
        self.req = 0
        self.look = lookahead
        self.qeng = qeng
        self.slot_of = {}
        self.cp = 0
        self.nr = 0

    def plan(self, ap, n, dest=None):
        self.pieces.append((ap, n, dest))
        return len(self.pieces) - 1

    def _request(self, i):
        P = self.P
        ap, n, dest = self.pieces[i]
        st = self.stage[i % len(self.stage)]
        if dest is not None:
            rgb, oap = dest
            src = st[:, 0:n]
            if len(oap.shape) == 3:
                src = src.rearrange("p (k t) -> p k t", k=oap.shape[1])
            P.dma(self.qeng, [lambda e, st=st, ap=ap, n=n: e.dma_start(out=st[:, 0:n], in_=ap)], writes=[st])
            if self.cp % 2 == 0:
                P.op("act", lambda e, src=src, oap=oap: e.copy(out=oap, in_=src), reads=[st], writes=[rgb])
            else:
                P.op("dve", lambda e, src=src, oap=oap: e.tensor_copy(out=oap, in_=src), reads=[st], writes=[rgb])
            self.cp += 1
            self.slot_of[i] = rgb
            return
        rg = self.ring[self.nr % len(self.ring)]
        self.nr += 1
        P.dma(self.qeng, [lambda e, st=st, ap=ap, n=n: e.dma_start(out=st[:, 0:n], in_=ap)], writes=[st])
        if self.cp % 2 == 0:
            P.op("act", lambda e, st=st, rg=rg, n=n: e.copy(out=rg[:, 0:n], in_=st[:, 0:n]), reads=[st], writes=[rg])
        else:
            P.op("dve", lambda e, st=st, rg=rg, n=n: e.tensor_copy(out=rg[:, 0:n], in_=st[:, 0:n]), reads=[st], writes=[rg])
        self.cp += 1
        self.slot_of[i] = rg

    def get(self, i):
        hi = min(len(self.pieces) - 1, i + self.look)
        while self.req <= hi:
            self._request(self.req)
            self.req += 1
        return self.slot_of[i]


def emit_mod(P, cT_d, adaw_d, adab_d, ms, nb, ps, wbufs):
    cT = P.sb("cT_sb", [128, KC, nb])
    cs = P.sb("c_sig", [128, KC, nb])
    adab = P.sb("adab_sb", [128, 6, KC])
    mod = P.sb("mod", [128, len(ms), KC, nb])
    P.dma("sp", [lambda e: e.dma_start(out=cT[:, :, :], in_=cT_d)], writes=[cT])
    P.dma("sp", [lambda e: e.dma_start(out=adab[:, :, :], in_=adab_d)], writes=[adab])
    P.op("act", lambda e: e.activation(out=cs[:, :, :], in_=cT[:, :, :], func=AF.Sigmoid), reads=[cT], writes=[cs])
    P.op("dve", lambda e: e.tensor_tensor(out=cT[:, :, :], in0=cT[:, :, :], in1=cs[:, :, :], op=ALU.mult), reads=[cT, cs], writes=[cT])
    n = 0
    wc = wbufs[0].t.shape[2]
    npc = 512 // wc
    for i, m in enumerate(ms):
        for half in range(2):
            for pc in range(npc):
                wb = wbufs[n % len(wbufs)]
                n += 1
                P.dma("sp", [lambda e, wb=wb, m=m, half=half, q=q, pc=pc: e.dma_start(out=wb[:, q * 4:(q + 1) * 4, :], in_=adaw_d[m, half, :, q * 4:(q + 1) * 4, pc * wc:(pc + 1) * wc]) for q in range(2)],
                      writes=[wb])
                for j in range(wc // 128):
                    jj = half * 4 + pc * (wc // 128) + j
                    for k in range(KC):
                        MM(P, ps[:, jj * nb:(jj + 1) * nb], wb[:, k, j * 128:(j + 1) * 128], cT[:, k, :], k == 0, k == KC - 1, [wb, cT], [ps])
        for b in range(nb):
            P.op("dve", lambda e, i=i, m=m, b=b: e.tensor_tensor(out=mod[:, i, :, b], in0=ps[:, b:KC * nb:nb], in1=adab[:, m, :], op=ALU.add),
                 reads=[ps, adab], writes=[mod])
    return mod


def emit_rmsnorm(P, xt, a_ap, s_ap, cbufs, h, sq, ones, ps_stat, rstd, T=TT, nk=KC, inv_n=1.0 / D, eps=NORM_EPS):
    P.op("act", lambda e: e.activation(out=sq[:, :, 0:T], in_=xt[:, :, 0:T], func=AF.Square), reads=[xt], writes=[sq])
    for k in range(nk):
        MM(P, ps_stat[:, 0:T], ones[:, :], sq[:, k, 0:T], k == 0, k == nk - 1, [ones, sq], [ps_stat])
    P.op("act", lambda e: e.activation(out=rstd[:, 0:T], in_=ps_stat[:, 0:T], func=AF.Sqrt, scale=inv_n, bias=eps_ap(P, eps)), reads=[ps_stat], writes=[rstd])
    P.op("dve", lambda e: e.reciprocal(out=rstd[:, 0:T], in_=rstd[:, 0:T]), reads=[rstd], writes=[rstd])
    for k in range(nk):
        P.op("dve", lambda e, k=k: e.tensor_tensor(out=sq[:, k, 0:T], in0=xt[:, k, 0:T], in1=rstd[:, 0:T], op=ALU.mult), reads=[xt, rstd], writes=[sq])
    for k in range(nk):
        P.op("act", lambda e, k=k: e.activation(out=h[:, k, 0:T], in_=sq[:, k, 0:T], func=AF.Identity, scale=a_ap(k), bias=s_ap(k)),
             reads=[sq] + list(cbufs), writes=[h])


_EPS = {}


def eps_ap(P, eps):
    return _EPS[(id(P), eps)][:, :]


def make_eps(P, eps):
    b = P.sb("eps%d" % len(_EPS), [128, 1])
    P.op("dve", lambda e: e.memset(b[:, :], eps), writes=[b])
    _EPS[(id(P), eps)] = b
    return b


def emit_ffn(P, ws, pieces, xt, h, act, sg, gate_fn, cbufs, psg, psu, pso):
    pg, pu, pd = pieces
    for j in range(FC):
        wg = ws.get(pg[j // 2])
        wu = ws.get(pu[j // 2])
        jo = (j % 2) * 128
        g_ps = psg[j % 2]
        u_ps = psu[j % 2]
        for k in range(KC):
            MM(P, g_ps[:, :], wg[:, k * 256 + jo:k * 256 + jo + 128], h[:, k, :], k == 0, k == KC - 1, [wg, h], [g_ps])
        for k in range(KC):
            MM(P, u_ps[:, :], wu[:, k * 256 + jo:k * 256 + jo + 128], h[:, k, :], k == 0, k == KC - 1, [wu, h], [u_ps])
        s_t = sg[j % 2]
        P.op("act", lambda e, s_t=s_t, g_ps=g_ps: e.activation(out=s_t[:, :], in_=g_ps[:, :], func=AF.Silu), reads=[g_ps], writes=[s_t])
        P.op("dve", lambda e, s_t=s_t, u_ps=u_ps, j=j: e.tensor_tensor(out=act[:, j, :], in0=s_t[:, :], in1=u_ps[:, :], op=ALU.mult),
             reads=[s_t, u_ps], writes=[act])
    for m in range(KC):
        o_ps = pso[m % 2]
        for j in range(FC):
            wd = ws.get(pd[2 * m + j // 11])
            jj = j % 11
            MM(P, o_ps[:, :], wd[:, jj * 128:(jj + 1) * 128], act[:, j, :], j == 0, j == FC - 1, [wd, act], [o_ps])
        P.op("dve", lambda e, m=m, o_ps=o_ps: e.scalar_tensor_tensor(out=xt[:, m, :], in0=o_ps[:, :], scalar=gate_fn(m), in1=xt[:, m, :], op0=ALU.mult, op1=ALU.add),
             reads=[o_ps, xt] + list(cbufs), writes=[xt])


def prep_ffn(wg, wu, wd):
    def gu(w):
        w = np.ascontiguousarray(w).reshape(KC, 128, FC // 2, 256)
        return np.ascontiguousarray(w.transpose(2, 1, 0, 3)).reshape(FC // 2, 128, KC * 256)
    wdp = np.ascontiguousarray(wd).reshape(2, 11, 128, KC, 128)
    wdp = np.ascontiguousarray(wdp.transpose(3, 0, 2, 1, 4)).reshape(2 * KC, 128, 11 * 128)
    return gu(wg), gu(wu), wdp


def prep_sq(w):
    w = np.ascontiguousarray(w).reshape(KC, 128, 4, 256)
    return np.ascontiguousarray(w.transpose(2, 1, 0, 3)).reshape(4, 128, KC * 256)


def prep_adaw(w):
    w = np.ascontiguousarray(w).reshape(KC, 128, 6, 2, 512)
    return np.ascontiguousarray(w.transpose(2, 3, 1, 0, 4))


def prep_vec(v):
    return np.ascontiguousarray(np.asarray(v).reshape(KC, 128).T)


def prep_adab(b):
    return np.ascontiguousarray(np.asarray(b).reshape(6, KC, 128).transpose(2, 0, 1))


def prep_c(c):
    nb = c.shape[0]
    return np.ascontiguousarray(np.asarray(c).reshape(nb, KC, 128).transpose(2, 1, 0))


def build_kb(NT, has_bias, conv=False):
    nc = bass.Bass("TRN2", target_bir_lowering=False)
    xT = nc.dram_tensor("xT", [D, NT], F32, kind="ExternalInput").ap()
    if not conv:
        yT = nc.dram_tensor("yT", [2 * (NT // TT), 128, 4 * TT], F32, kind="ExternalInput").ap()
    wo = nc.dram_tensor("wo", [4, 128, KC * 256], F32, kind="ExternalInput").ap()
    wg = nc.dram_tensor("wg", [FC // 2, 128, KC * 256], F32, kind="ExternalInput").ap()
    wu = nc.dram_tensor("wu", [FC // 2, 128, KC * 256], F32, kind="ExternalInput").ap()
    wd = nc.dram_tensor("wd", [2 * KC, 128, 11 * 128], F32, kind="ExternalInput").ap()
    cT = nc.dram_tensor("cT", [128, KC, 1], F32, kind="ExternalInput").ap()
    adaw = nc.dram_tensor("adaw", [6, 2, 128, KC, 512], F32, kind="ExternalInput").ap()
    adab = nc.dram_tensor("adab", [128, 6, KC], F32, kind="ExternalInput").ap()
    gf = nc.dram_tensor("gf", [128, KC], F32, kind="ExternalInput").ap()
    if has_bias:
        bo = nc.dram_tensor("bo", [128, KC], F32, kind="ExternalInput").ap()
    if conv:
        xh = nc.dram_tensor("xh", [D, 32], F32, kind="ExternalInput").ap()
        hon = nc.dram_tensor("hon", [128, 1], F32, kind="ExternalInput").ap()
        w1 = nc.dram_tensor("w1", [KC, 128, KC * 256], F32, kind="ExternalInput").ap()
        cvec = nc.dram_tensor("cvec", [128, 6, KC], F32, kind="ExternalInput").ap()
        wdw = nc.dram_tensor("wdw", [128, KC, 31], F32, kind="ExternalInput").ap()
    out = nc.dram_tensor("out", [D, NT], F32, kind="ExternalOutput").ap()
    from contextlib import ExitStack
    with ExitStack() as st:
        P = Prog(nc, st)
        make_eps(P, NORM_EPS)
        ones = P.sb("ones", [128, 128])
        P.op("dve", lambda e: e.memset(ones[:, :], 1.0), writes=[ones])
        ps_stat = P.ps("ps_stat", [128, 512])
        psg = [P.ps("psg%d" % i, [128, 512]) for i in range(2)]
        psu = [P.ps("psu%d" % i, [128, 512]) for i in range(2)]
        pso = [P.ps("pso%d" % i, [128, 512]) for i in range(2)]
        xts = [P.sb("xt%d" % i, [128, KC, TT]) for i in range(2)]
        sq = P.sb("sq", [128, KC, TT])
        h = P.sb("h", [128, KC, TT], F32R)
        act = P.sb("act", [128, FC, TT], F32R)
        sg = [P.sb("sg%d" % i, [128, TT]) for i in range(2)]
        rstd = P.sb("rstd", [128, TT])
        gfs = P.sb("gfs", [128, KC])
        P.dma("sp", [lambda e: e.dma_start(out=gfs[:, :], in_=gf)], writes=[gfs])
        if has_bias:
            bos = P.sb("bos", [128, KC])
            P.dma("sp", [lambda e: e.dma_start(out=bos[:, :], in_=bo)], writes=[bos])
        if conv:
            cv = P.sb("cvec_sb", [128, 6, KC])
            wdws = P.sb("wdw_sb", [128, KC, 31])
            hons = P.sb("hon_sb", [128, 1])
            P.dma("sp", [lambda e: e.dma_start(out=cv[:, :, :], in_=cvec)], writes=[cv])
            P.dma("sp", [lambda e: e.dma_start(out=wdws[:, :, :], in_=wdw)], writes=[wdws])
            P.dma("sp", [lambda e: e.dma_start(out=hons[:, :], in_=hon)], writes=[hons])
            u = P.sb("u", [128, KC, 32 + TT])
            acc = P.sb("acc", [128, KC, TT])
            mean = P.sb("mean", [128, TT])
            ps_stat2 = P.ps("ps_stat2", [128, 512])
        ms = [0, 1, 2, 3, 4, 5] if conv else [2, 3, 4, 5]
        mi = {m: i for i, m in enumerate(ms)}
        mod = emit_mod(P, cT, adaw, adab, ms, 1, ps_stat, xts)
        af = P.sb("af", [128, KC])
        P.op("dve", lambda e: e.scalar_tensor_tensor(out=af[:, :], in0=mod[:, mi[4], :, 0], scalar=1.0, in1=gfs[:, :], op0=ALU.add, op1=ALU.mult),
             reads=[mod, gfs], writes=[af])
        if conv:
            am = P.sb("am", [128, KC])
            P.op("dve", lambda e: e.scalar_tensor_tensor(out=am[:, :], in0=mod[:, mi[1], :, 0], scalar=1.0, in1=cv[:, 5, :], op0=ALU.add, op1=ALU.mult),
                 reads=[mod, cv], writes=[am])
        ws = WStream(P) if conv else WStream(P, nstage=4, nring=6, lookahead=3)
        ntile = NT // TT
        plan = []
        if conv:
            p1h = [ws.plan(w1[q], KC * 256) for q in range(KC)]
        for t in range(ntile):
            if conv:
                py = [ws.plan(w1[q], KC * 256) for q in range(KC)]
            else:
                py = [ws.plan(yT[2 * t + q], 4 * TT, dest=(h, h[:, q * 4:(q + 1) * 4, :])) for q in range(2)]
            po = [ws.plan(wo[q], KC * 256) for q in range(4)]
            pg, pu = [], []
            for q in range(FC // 2):
                pg.append(ws.plan(wg[q], KC * 256))
                pu.append(ws.plan(wu[q], KC * 256))
            pd = [ws.plan(wd[q], 11 * 128) for q in range(2 * KC)]
            plan.append((py, po, pg, pu, pd))

        def conv_front(xt, T, c0, p1):
            emit_rmsnorm(P, xt, lambda k: am[:, k:k + 1], lambda k: mod[:, mi[0], k, 0:1], [am, mod], h, sq, ones, ps_stat, rstd, T=T)
            for m in range(KC):
                w = ws.get(p1[m])
                a_ps = psg[m % 2]
                b_ps = psu[m % 2]
                for k in range(KC):
                    MM(P, a_ps[:, 0:T], w[:, k * 256:k * 256 + 128], h[:, k, 0:T], k == 0, k == KC - 1, [w, h], [a_ps])
                for k in range(KC):
                    MM(P, b_ps[:, 0:T], w[:, k * 256 + 128:k * 256 + 256], h[:, k, 0:T], k == 0, k == KC - 1, [w, h], [b_ps])
                s_t = sg[m % 2]
                P.op("act", lambda e, s_t=s_t, b_ps=b_ps, m=m: e.activation(out=s_t[:, 0:T], in_=b_ps[:, 0:T], func=AF.Sigmoid, bias=cv[:, 1, m:m + 1], scale=1.0),
                     reads=[b_ps, cv], writes=[s_t])
                P.op("dve", lambda e, s_t=s_t, a_ps=a_ps, m=m: e.scalar_tensor_tensor(out=u[:, m, c0:c0 + T], in0=a_ps[:, 0:T], scalar=cv[:, 0, m:m + 1], in1=s_t[:, 0:T], op0=ALU.add, op1=ALU.mult),
                     reads=[a_ps, s_t, cv], writes=[u])

        if conv:
            xt = xts[1]
            P.dma("sp", [lambda e, xt=xt, k=k: e.dma_start(out=xt[:, k, 0:32], in_=xh[k * 128:(k + 1) * 128, :]) for k in range(KC)], writes=[xt])
            conv_front(xt, 32, 0, p1h)
            P.op("dve", lambda e: e.tensor_scalar(out=u[:, :, 0:32], in0=u[:, :, 0:32], scalar1=hons[:, 0:1], scalar2=None, op0=ALU.mult), reads=[u, hons], writes=[u])
        for t in range(ntile):
            py, po, pg, pu, pd = plan[t]
            xt = xts[t % 2]
            P.dma("sp", [lambda e, xt=xt, t=t, k=k: e.dma_start(out=xt[:, k, :], in_=xT[k * 128:(k + 1) * 128, t * TT:(t + 1) * TT]) for k in range(KC)], writes=[xt])
            if conv:
                conv_front(xt, TT, 32, py)
                for m in range(KC):
                    P.op("dve", lambda e, m=m: e.tensor_scalar(out=acc[:, m, :], in0=u[:, m, 2:2 + TT], scalar1=wdws[:, m, 0:1], scalar2=cv[:, 2, m:m + 1], op0=ALU.mult, op1=ALU.add),
                         reads=[u, wdws, cv], writes=[acc])
                    for j in range(1, 31):
                        P.op("dve", lambda e, m=m, j=j: e.scalar_tensor_tensor(out=acc[:, m, :], in0=u[:, m, 2 + j:2 + j + TT], scalar=wdws[:, m, j:j + 1], in1=acc[:, m, :], op0=ALU.mult, op1=ALU.add),
                             reads=[u, wdws, acc], writes=[acc])
                P.op("act", lambda e: e.copy(out=u[:, :, 0:32], in_=u[:, :, TT:TT + 32]), reads=[u], writes=[u])
                P.op("act", lambda e: e.activation(out=sq[:, :, :], in_=acc[:, :, :], func=AF.Square), reads=[acc], writes=[sq])
                for k in range(KC):
                    MM(P, ps_stat[:, :], ones[:, :], acc[:, k, :], k == 0, k == KC - 1, [ones, acc], [ps_stat])
                for k in range(KC):
                    MM(P, ps_stat2[:, :], ones[:, :], sq[:, k, :], k == 0, k == KC - 1, [ones, sq], [ps_stat2])
                P.op("act", lambda e: e.mul(out=mean[:, :], in_=ps_stat[:, :], mul=1.0 / D), reads=[ps_stat], writes=[mean])
                P.op("dve", lambda e: e.tensor_tensor(out=sg[0][:, :], in0=mean[:, :], in1=mean[:, :], op=ALU.mult), reads=[mean], writes=[sg[0]])
                P.op("dve", lambda e: e.scalar_tensor_tensor(out=rstd[:, :], in0=ps_stat2[:, :], scalar=1.0 / D, in1=sg[0][:, :], op0=ALU.mult, op1=ALU.subtract),
                     reads=[ps_stat2, sg[0]], writes=[rstd])
                P.op("act", lambda e: e.activation(out=rstd[:, :], in_=rstd[:, :], func=AF.Sqrt, scale=1.0, bias=eps_ap(P, NORM_EPS)), reads=[rstd], writes=[rstd])
                P.op("dve", lambda e: e.reciprocal(out=rstd[:, :], in_=rstd[:, :]), reads=[rstd], writes=[rstd])
                for m in range(KC):
                    P.op("dve", lambda e, m=m: e.tensor_tensor(out=acc[:, m, :], in0=acc[:, m, :], in1=mean[:, :], op=ALU.subtract), reads=[acc, mean], writes=[acc])
                    P.op("dve", lambda e, m=m: e.tensor_tensor(out=acc[:, m, :], in0=acc[:, m, :], in1=rstd[:, :], op=ALU.mult), reads=[acc, rstd], writes=[acc])
                    P.op("act", lambda e, m=m: e.activation(out=h[:, m, :], in_=acc[:, m, :], func=AF.Silu, scale=cv[:, 3, m:m + 1], bias=cv[:, 4, m:m + 1]),
                         reads=[acc, cv], writes=[h])
            else:
                ws.get(py[0])
                ws.get(py[1])
            for m in range(KC):
                w = ws.get(po[m // 2])
                mo = (m % 2) * 128
                o_ps = pso[m % 2]
                for k in range(KC):
                    MM(P, o_ps[:, :], w[:, k * 256 + mo:k * 256 + mo + 128], h[:, k, :], k == 0, k == KC - 1, [w, h], [o_ps])
                if has_bias:
                    P.op("act", lambda e, m=m, o_ps=o_ps: e.activation(out=sg[0][:, :], in_=o_ps[:, :], func=AF.Identity, bias=bos[:, m:m + 1], scale=1.0),
                         reads=[o_ps, bos], writes=[sg[0]])
                    P.op("dve", lambda e, m=m, xt=xt: e.scalar_tensor_tensor(out=xt[:, m, :], in0=sg[0][:, :], scalar=mod[:, mi[2], m, 0:1], in1=xt[:, m, :], op0=ALU.mult, op1=ALU.add),
                         reads=[sg[0], xt, mod], writes=[xt])
                else:
                    P.op("dve", lambda e, m=m, xt=xt, o_ps=o_ps: e.scalar_tensor_tensor(out=xt[:, m, :], in0=o_ps[:, :], scalar=mod[:, mi[2], m, 0:1], in1=xt[:, m, :], op0=ALU.mult, op1=ALU.add),
                         reads=[o_ps, xt, mod], writes=[xt])
            emit_rmsnorm(P, xt, lambda k: af[:, k:k + 1], lambda k: mod[:, mi[3], k, 0:1], [af, mod], h, sq, ones, ps_stat, rstd)
            emit_ffn(P, ws, (pg, pu, pd), xt, h, act, sg, lambda m: mod[:, mi[5], m, 0:1], [mod], psg, psu, pso)
            P.dma("sp", [lambda e, xt=xt, t=t, k=k: e.dma_start(out=out[k * 128:(k + 1) * 128, t * TT:(t + 1) * TT], in_=xt[:, k, :]) for k in range(KC)], reads=[xt], sem_buf=xt)
        stats = P.emit()
    return nc, stats


def prep_w1(w):
    w = np.ascontiguousarray(w).reshape(KC, 128, 2, KC, 128)
    return np.ascontiguousarray(w.transpose(3, 1, 0, 2, 4)).reshape(KC, 128, KC * 256)


def act_view(P, act):
    return act


def prep_y_pieces(yT, NT):
    nt = NT // TT
    y = np.ascontiguousarray(yT).reshape(2, 4, 128, nt, TT)
    return np.ascontiguousarray(y.transpose(3, 0, 2, 1, 4)).reshape(2 * nt, 128, 4 * TT)


ST = 2048
NU = 16
DIL = (1, 4, 16)


def ka_consts():
    inv_freq = np.power(np.float32(10000.0), -np.arange(0, 64, 2, dtype=np.float32) / np.float32(64)).astype(np.float32)
    invf = np.tile(inv_freq, 4).reshape(128, 1).astype(np.float32)
    blockones = np.zeros((128, 128), np.float32)
    blockones[:64, :64] = 1
    blockones[64:, 64:] = 1
    rotT = np.zeros((128, 128), np.float32)
    for hb in (0, 64):
        for e in range(32):
            rotT[hb + e + 32, hb + e] = -1.0
            rotT[hb + e, hb + e + 32] = 1.0
    ident = np.eye(128, dtype=np.float32)
    j = np.arange(128)[:, None]
    i = np.arange(128)[None, :]
    masks = np.concatenate([(j >= i), (j <= i)], 1).astype(np.float32)
    shiftM = np.zeros((128, 128), np.float32)
    shiftM[64 + np.arange(64), np.arange(64)] = 1.0
    return dict(invf=invf, blockones=blockones, rotT=rotT, ident=ident, masks=masks, shiftM=shiftM)


def build_ka(S, LV=9, SUB=9):
    nc = bass.Bass("TRN2", target_bir_lowering=False)
    xT = nc.dram_tensor("xT", [D, S], F32, kind="ExternalInput").ap()
    posb = nc.dram_tensor("posb", [128, S], I32, kind="ExternalInput").ap()
    cT = nc.dram_tensor("cT", [128, KC, 1], F32, kind="ExternalInput").ap()
    adaw = nc.dram_tensor("adaw", [6, 2, 128, KC, 512], F32, kind="ExternalInput").ap()
    adab = nc.dram_tensor("adab", [128, 6, KC], F32, kind="ExternalInput").ap()
    gm = nc.dram_tensor("gm", [128, KC], F32, kind="ExternalInput").ap()
    wq = nc.dram_tensor("wq", [2, 9, 128, KC * 128], F32, kind="ExternalInput").ap()
    gains = nc.dram_tensor("gains", [128, 2], F32, kind="ExternalInput").ap()
    invf_d = nc.dram_tensor("invf", [128, 1], F32, kind="ExternalInput").ap()
    bo_d = nc.dram_tensor("blockones", [128, 128], F32, kind="ExternalInput").ap()
    rot_d = nc.dram_tensor("rotT", [128, 128], F32, kind="ExternalInput").ap()
    id_d = nc.dram_tensor("ident", [128, 128], F32, kind="ExternalInput").ap()
    mk_d = nc.dram_tensor("masks", [128, 256], F32, kind="ExternalInput").ap()
    sh_d = nc.dram_tensor("shiftM", [128, 128], F32, kind="ExternalInput").ap()
    oT = nc.dram_tensor("oT", [2, 2, 64, S], F32, kind="ExternalOutput").ap()
    from contextlib import ExitStack
    TWO_PI = 2.0 * np.pi
    C1 = 6.28125
    C2 = float(np.float32(TWO_PI - C1)) if False else 0.0019350051879882812
    C3 = float(TWO_PI - C1 - C2)
    with ExitStack() as st:
        P = Prog(nc, st)
        make_eps(P, NORM_EPS)

        def const(name, d, shape, dt=F32):
            b = P.sb(name + "_sb", shape, dt)
            P.dma("sp", [lambda e: e.dma_start(out=b[tuple(slice(None) for _ in shape)], in_=d)], writes=[b])
            return b
        invf = const("invf", invf_d, [128, 1])
        bones = const("bones", bo_d, [128, 128])
        rotT = const("rotT", rot_d, [128, 128])
        ident = const("ident", id_d, [128, 128])
        masks = const("masks", mk_d, [128, 256])
        shiftM = const("shiftM", sh_d, [128, 128])
        gms = const("gms", gm, [128, KC])
        gn = const("gn", gains, [128, 2])
        zt = P.sb("zt", [128, 512])
        P.op("dve", lambda e: e.memset(zt[:, :], 0.0), writes=[zt])
        ones = P.sb("ones", [128, 128], F32R)
        onesf = P.sb("onesf", [128, 128])
        P.op("dve", lambda e: e.memset(onesf[:, :], 1.0), writes=[onesf])
        P.op("dve", lambda e: e.tensor_copy(out=ones[:, :], in_=onesf[:, :]), reads=[onesf], writes=[ones])
        psp = [P.ps("psp%d" % i, [128, 512]) for i in range(2)]
        ps_stat = P.ps("ps_stat", [128, 512])
        pss = [P.ps("pss%d" % i, [128, 512]) for i in range(2)]
        psoh = [P.ps("pso%d" % i, [128, 512]) for i in range(2)]
        psm = P.ps("psm", [128, 512])
        ps_rot = psm
        xt = P.sb("xt", [128, KC, TT])
        h = P.sb("h", [128, KC, TT], F32R)
        W = P.sb("W", [128, 9, KC * 128], F32R)
        rstd = P.sb("rstd", [128, TT])
        posi = P.sb("posi", [128, TT], I32)
        ang = P.sb("ang", [128, TT])
        kf = P.sb("kf", [128, TT])
        ki = posi
        cosT = P.sb("cosT", [128, TT])
        sinT = P.sb("sinT", [128, TT])
        tg = P.sb("tg", [128, TT])
        ta = P.sb("ta", [128, TT])
        tb = kf
        qb = [[P.sb("q%d_%d" % (g, hh), [128, (4, 4, NU)[g], 128], F32R) for hh in range(2)] for g in range(3)]
        for g in range(3):
            for hh in range(2):
                for z4 in range((4, 4, NU)[g] // 4):
                    P.op("dve", lambda e, g=g, hh=hh, z4=z4: e.tensor_copy(out=qb[g][hh][:, z4 * 4:z4 * 4 + 4, :].rearrange("p u n -> p (u n)"), in_=zt[:, :]), reads=[zt], writes=[qb[g][hh]])
        nslot = (5, 8, 2 * NU)
        kb = [P.sb("k%d" % g, [128, nslot[g], 128], F32R) for g in range(3)]
        vb = [P.sb("v%d" % g, [128, nslot[g], 2, 128], BF16) for g in range(3)]
        vT = [P.sb("vT0", [128, TT]), P.sb("vT1", [128, TT]), P.sb("vT2", [128, ST])]
        accs = [P.sb("acc%d" % hh, [128, ST]) for hh in range(2)]
        et = [P.sb("et%d" % i, [128, 512]) for i in range(2)]
        pt = [P.sb("pt%d" % i, [128, 512], BF16) for i in range(2)]
        masks2 = P.sb("masks2", [128, 512])
        P.dma("sp", [lambda e, q=q: e.dma_start(out=masks2[:, q * 256:(q + 1) * 256], in_=mk_d) for q in range(2)], writes=[masks2])
        rden = P.sb("rden", [64, TT])
        for g in range(3):
            P.op("dve", lambda e, g=g: e.memset(vb[g][:, :, :, 64:128], 1.0), writes=[vb[g]])
        mod = emit_mod(P, cT, adaw, adab, [0, 1], 1, ps_stat, [xt])
        am = P.sb("am", [128, KC])
        P.op("dve", lambda e: e.scalar_tensor_tensor(out=am[:, :], in0=mod[:, 1, :, 0], scalar=1.0, in1=gms[:, :], op0=ALU.add, op1=ALU.mult),
             reads=[mod, gms], writes=[am])
        nst = S // ST
        cnt = {"pp": 0, "ss": 0, "ob": 0, "qk": 0}
        qk_sets = [(ta, tg, tb, rstd), (P.sb("ta2", [128, TT]), P.sb("tg2", [128, TT]), P.sb("tb2", [128, TT]), P.sb("rstd2", [128, TT]))]

        def attend(g, uq, cs, pv, colsf, first_group):
            sps = pss[cnt["ss"] % 2]
            e_t = et[cnt["ss"] % 2]
            p_t = pt[cnt["ss"] % 2]
            cnt["ss"] += 1
            kprev = pv if pv is not None else cs
            for hh in range(2):
                base = hh * 256
                MM(P, sps[:, base:base + 128], kb[g][:, kprev, :], qb[g][hh][:, uq, :], True, True, [kb[g], qb[g][hh]], [sps])
                MM(P, sps[:, base + 128:base + 256], kb[g][:, cs, :], qb[g][hh][:, uq, :], True, True, [kb[g], qb[g][hh]], [sps])
            P.op("act", lambda e, sps=sps, e_t=e_t: e.activation(out=e_t[:, :], in_=sps[:, :], func=AF.Exp, scale=0.125), reads=[sps], writes=[e_t])
            P.op("dve", lambda e, e_t=e_t, p_t=p_t: e.tensor_tensor(out=p_t[:, :], in0=e_t[:, :], in1=masks2[:, :], op=ALU.mult),
                 reads=[e_t, masks2], writes=[p_t])
            for hh in range(2):
                base = hh * 256
                pb_ = psoh[hh]
                oc = pb_[:, 0:128]
                if pv is not None:
                    MM(P, oc, vb[g][:, pv, hh, :], p_t[:, base:base + 128], True, False, [vb[g], p_t], [pb_])
                MM(P, oc, vb[g][:, cs, hh, :], p_t[:, base + 128:base + 256], pv is None, True, [vb[g], p_t], [pb_])
                if first_group:
                    P.op("dve", lambda e, oc=oc, o=colsf(accs[hh]): e.tensor_copy(out=o, in_=oc), reads=[pb_], writes=[accs[hh]])
                else:
                    P.op("dve", lambda e, oc=oc, o=colsf(accs[hh]): e.tensor_tensor(out=o, in0=oc, in1=o, op=ALU.add), reads=[pb_, accs[hh]], writes=[accs[hh]])

        for hp in range(2):
            for c0 in range(0, 9, 4):
                n = min(4, 9 - c0)
                P.dma("sp", [lambda e, hp=hp, c=c: e.dma_start(out=xt[:, 2 * (c % 4):2 * (c % 4) + 2, :], in_=wq[hp, c].rearrange("p (a b) -> p a b", a=2)) for c in range(c0, c0 + n)], writes=[xt])
                P.op("act", lambda e, c0=c0, n=n: e.copy(out=W[:, c0:c0 + n, :], in_=xt[:, 0:2 * n, :].rearrange("p (c a) b -> p c (a b)", a=2)), reads=[xt], writes=[W])
            for sti in range(nst):
                par = sti % 2
                cur0 = (0, 0, par * NU)
                for sub in range(ST // TT):
                    t0 = sti * ST + sub * TT
                    P.dma("sp", [lambda e, t0=t0, k=k: e.dma_start(out=xt[:, k, :], in_=xT[k * 128:(k + 1) * 128, t0:t0 + TT]) for k in range(KC)], writes=[xt])
                    P.dma("sp", [lambda e, t0=t0: e.dma_start(out=posi[:, :], in_=posb[:, t0:t0 + TT])], writes=[posi])
                    P.op("act", lambda e: e.activation(out=h[:, :, :], in_=xt[:, :, :], func=AF.Square), reads=[xt], writes=[h])
                    for k in range(KC):
                        MM(P, ps_stat[:, :], ones[:, :], h[:, k, :], k == 0, k == KC - 1, [ones, h], [ps_stat])
                    P.op("act", lambda e: e.activation(out=rstd[:, :], in_=ps_stat[:, :], func=AF.Sqrt, scale=1.0 / D, bias=eps_ap(P, NORM_EPS)), reads=[ps_stat], writes=[rstd])
                    P.op("dve", lambda e: e.reciprocal(out=rstd[:, :], in_=rstd[:, :]), reads=[rstd], writes=[rstd])
                    for k in range(KC):
                        P.op("dve", lambda e, k=k: e.tensor_tensor(out=xt[:, k, :], in0=xt[:, k, :], in1=rstd[:, :], op=ALU.mult), reads=[xt, rstd], writes=[xt])
                    for k in range(KC):
                        P.op("act", lambda e, k=k: e.activation(out=h[:, k, :], in_=xt[:, k, :], func=AF.Identity, scale=am[:, k:k + 1], bias=mod[:, 0, k, 0:1]),
                             reads=[xt, am, mod], writes=[h])
                    if LV < 2:
                        continue
                    P.op("dve", lambda e: e.tensor_copy(out=ang[:, :], in_=posi[:, :]), reads=[posi], writes=[ang])
                    P.op("dve", lambda e: e.tensor_scalar(out=ang[:, :], in0=ang[:, :], scalar1=invf[:, 0:1], scalar2=None, op0=ALU.mult), reads=[ang, invf], writes=[ang])
                    P.op("dve", lambda e: e.tensor_scalar(out=ki[:, :], in0=ang[:, :], scalar1=float(1.0 / TWO_PI), scalar2=None, op0=ALU.mult), reads=[ang], writes=[ki])
                    P.op("dve", lambda e: e.tensor_copy(out=kf[:, :], in_=ki[:, :]), reads=[ki], writes=[kf])
                    P.op("dve", lambda e: e.scalar_tensor_tensor(out=ang[:, :], in0=kf[:, :], scalar=-C1, in1=ang[:, :], op0=ALU.mult, op1=ALU.add), reads=[kf, ang], writes=[ang])
                    P.op("dve", lambda e: e.scalar_tensor_tensor(out=ang[:, :], in0=kf[:, :], scalar=-C2, in1=ang[:, :], op0=ALU.mult, op1=ALU.add), reads=[kf, ang], writes=[ang])
                    P.op("dve", lambda e: e.scalar_tensor_tensor(out=ang[:, :], in0=kf[:, :], scalar=-C3, in1=ang[:, :], op0=ALU.mult, op1=ALU.add), reads=[kf, ang], writes=[ang])
                    emit_wrap_sin(P, sinT, ang, 0.0, kf, ta)
                    emit_wrap_sin(P, cosT, ang, float(np.pi / 2), kf, ta)
                    if LV < 3:
                        continue
                    for g in range(3):
                        r = DIL[g]
                        for j in range(3):
                            cc = g * 3 + j
                            ps = psp[cnt["pp"] % 2]
                            cnt["pp"] += 1
                            for k in range(KC):
                                MM(P, ps[:, :], W[:, cc, k * 128:(k + 1) * 128], h[:, k, :], k == 0, k == KC - 1, [W, h], [ps])
                            if g == 0:
                                u0, nn = sub * 4, 4

                                def dst(buf, base):
                                    return buf[:, 0:4, :].rearrange("p u n -> p (u n)")

                                def srcv(a):
                                    return a
                            elif g == 1:
                                def dst(buf, base):
                                    return buf[:, 0:4, :]

                                def srcv(a):
                                    return a.rearrange("p (n r) -> p r n", r=4)
                            else:
                                def dst(buf, base):
                                    return buf[:, base:base + 16, sub * 32:(sub + 1) * 32]

                                def srcv(a):
                                    return a.rearrange("p (n r) -> p r n", r=16)
                            if j == 2:
                                if g < 2:
                                    P.op("act", lambda e, ps=ps, g=g: e.copy(out=vT[g][:, :], in_=ps[:, :]), reads=[ps], writes=[vT[g]])
                                else:
                                    P.op("act", lambda e, ps=ps, sub=sub: e.copy(out=vT[2][:, sub * TT:(sub + 1) * TT], in_=ps[:, :]), reads=[ps], writes=[vT[2]])
                                continue
                            if SUB < 1:
                                continue
                            tset = qk_sets[cnt["qk"] % 2]
                            cnt["qk"] += 1
                            q_ta, q_tg, q_tb, q_rstd = tset
                            P.op("act", lambda e, q_ta=q_ta, ps=ps: e.activation(out=q_ta[:, :], in_=ps[:, :], func=AF.Square), reads=[ps], writes=[q_ta])
                            MM(P, ps_stat[:, :], bones[:, :], q_ta[:, :], True, True, [bones, q_ta], [ps_stat])
                            P.op("dve", lambda e, q_ta=q_ta, q_tg=q_tg, ps=ps, j=j: e.tensor_scalar(out=q_tg[:, :], in0=ps[:, :], scalar1=gn[:, j:j + 1], scalar2=None, op0=ALU.mult), reads=[ps, gn, q_ta], writes=[q_tg])
                            MM(P, ps_rot[:, :], rotT[:, :], q_tg[:, :], True, True, [rotT, q_tg], [ps_rot])
                            P.op("act", lambda e, q_rstd=q_rstd: e.activation(out=q_rstd[:, :], in_=ps_stat[:, :], func=AF.Sqrt, scale=1.0 / 64, bias=eps_ap(P, NORM_EPS)), reads=[ps_stat], writes=[q_rstd])
                            P.op("dve", lambda e, q_rstd=q_rstd: e.reciprocal(out=q_rstd[:, :], in_=q_rstd[:, :]), reads=[q_rstd], writes=[q_rstd])
                            if SUB < 2:
                                continue
                            P.op("dve", lambda e, q_tg=q_tg: e.tensor_tensor(out=q_tg[:, :], in0=q_tg[:, :], in1=cosT[:, :], op=ALU.mult), reads=[q_tg, cosT], writes=[q_tg])
                            P.op("dve", lambda e, q_tb=q_tb: e.tensor_tensor(out=q_tb[:, :], in0=ps_rot[:, :], in1=sinT[:, :], op=ALU.mult), reads=[ps_rot, sinT], writes=[q_tb])
                            P.op("dve", lambda e, q_tg=q_tg, q_tb=q_tb: e.tensor_tensor(out=q_tg[:, :], in0=q_tg[:, :], in1=q_tb[:, :], op=ALU.add), reads=[q_tg, q_tb], writes=[q_tg])
                            if SUB < 3:
                                continue
                            if j == 1:
                                buf, base = kb[g], cur0[g]
                                P.op("dve", lambda e, q_tg=q_tg, q_rstd=q_rstd, o=dst(buf, base), a=srcv(q_tg[:, :]), b=srcv(q_rstd[:, :]): e.tensor_tensor(out=o, in0=a, in1=b, op=ALU.mult),
                                     reads=[q_tg, q_rstd], writes=[buf])
                            else:
                                for hh in range(2):
                                    buf = qb[g][hh]
                                    P.op("dve", lambda e, q_tg=q_tg, q_rstd=q_rstd, o=dst(buf, 0)[hh * 64:(hh + 1) * 64], a=srcv(q_tg[hh * 64:(hh + 1) * 64, :]), b=srcv(q_rstd[hh * 64:(hh + 1) * 64, :]): e.tensor_tensor(out=o, in0=a, in1=b, op=ALU.mult),
                                         reads=[q_tg, q_rstd], writes=[buf])
                    if LV < 4:
                        continue
                    for g in range(2):
                        for uu in range(4):
                            src = vT[g][:, uu * 128:(uu + 1) * 128] if g == 0 else vT[g][:, uu:TT:4]
                            P.op("pe", lambda e, uu=uu, src=src: e.transpose(out=psm[:, uu * 128:(uu + 1) * 128], in_=src, identity=ident[:, :]), reads=[vT[g], ident], writes=[psm])
                        P.op("act", lambda e, g=g, sub=sub: e.copy(out=vb[g][:, 0:4, :, 0:64], in_=psm[:, :].rearrange("p (u h f) -> p u h f", u=4, h=2)),
                             reads=[psm], writes=[vb[g]])
                    if LV < 5:
                        continue
                    first = (sti == 0 and sub == 0)
                    for u in range(4):
                        attend(0, u, u, (u - 1) if u > 0 else (None if first else 4), lambda a, u=u, sub=sub: a[:, (sub * 4 + u) * 128:(sub * 4 + u + 1) * 128], True)
                    for u in range(4):
                        attend(1, u, u, None if first else 4 + u, lambda a, u=u, sub=sub: a[:, sub * 512 + u:(sub + 1) * 512:4], False)
                    P.op("act", lambda e: e.copy(out=kb[0][:, 4, :], in_=kb[0][:, 3, :]), reads=[kb[0]], writes=[kb[0]])
                    P.op("act", lambda e: e.copy(out=vb[0][:, 4, :, :], in_=vb[0][:, 3, :, :]), reads=[vb[0]], writes=[vb[0]])
                    P.op("act", lambda e: e.copy(out=kb[1][:, 4:8, :], in_=kb[1][:, 0:4, :]), reads=[kb[1]], writes=[kb[1]])
                    P.op("act", lambda e: e.copy(out=vb[1][:, 4:8, :, :], in_=vb[1][:, 0:4, :, :]), reads=[vb[1]], writes=[vb[1]])
                if LV < 6:
                    continue
                for u4 in range(4):
                    for uu in range(4):
                        rho = u4 * 4 + uu
                        P.op("pe", lambda e, uu=uu, rho=rho: e.transpose(out=psm[:, uu * 128:(uu + 1) * 128], in_=vT[2][:, rho:ST:16], identity=ident[:, :]), reads=[vT[2], ident], writes=[psm])
                    P.op("act", lambda e, u4=u4, b0=cur0[2]: e.copy(out=vb[2][:, b0 + u4 * 4:b0 + u4 * 4 + 4, :, 0:64], in_=psm[:, :].rearrange("p (u h f) -> p u h f", u=4, h=2)),
                         reads=[psm], writes=[vb[2]])
                for u in range(NU):
                    attend(2, u, par * NU + u, ((1 - par) * NU + u) if sti > 0 else None, lambda a, u=u: a[:, u:ST:16], False)
                if LV < 7:
                    continue
                for hh in range(2):
                    for c4 in range(ST // TT):
                        MM(P, psm[:, :], shiftM[:, :], accs[hh][:, c4 * TT:(c4 + 1) * TT], True, True, [shiftM, accs[hh]], [psm])
                        P.op("dve", lambda e: e.reciprocal(out=rden[:, :], in_=psm[0:64, :]), reads=[psm], writes=[rden])
                        P.op("dve", lambda e, hh=hh, c4=c4: e.tensor_tensor(out=accs[hh][0:64, c4 * TT:(c4 + 1) * TT], in0=accs[hh][0:64, c4 * TT:(c4 + 1) * TT], in1=rden[:, :], op=ALU.mult),
                             reads=[accs[hh], rden], writes=[accs[hh]])
                        tcol = sti * ST + c4 * TT
                        P.dma("sp", [lambda e, hp=hp, hh=hh, tcol=tcol, c4=c4: e.dma_start(out=oT[hp, hh, :, tcol:tcol + TT], in_=accs[hh][0:64, c4 * TT:(c4 + 1) * TT])], reads=[accs[hh]], sem_buf=accs[hh])
        stats = P.emit()
    return nc, stats


def emit_wrap_sin(P, out, ang, shift, t1, t2):
    P.op("dve", lambda e: e.tensor_scalar(out=t1[:, :], in0=ang[:, :], scalar1=float(shift), scalar2=float(np.pi), op0=ALU.add, op1=ALU.is_gt), reads=[ang], writes=[t1])
    P.op("dve", lambda e: e.tensor_scalar(out=t2[:, :], in0=ang[:, :], scalar1=float(shift), scalar2=None, op0=ALU.add), reads=[ang], writes=[t2])
    P.op("dve", lambda e: e.scalar_tensor_tensor(out=t2[:, :], in0=t1[:, :], scalar=float(-2.0 * np.pi), in1=t2[:, :], op0=ALU.mult, op1=ALU.add), reads=[t1, t2], writes=[t2])
    P.op("dve", lambda e: e.tensor_scalar(out=t2[:, :], in0=t2[:, :], scalar1=float(-3.1415925), scalar2=float(3.1415925), op0=ALU.max, op1=ALU.min), reads=[t2], writes=[t2])
    P.op("act", lambda e: e.activation(out=out[:, :], in_=t2[:, :], func=AF.Sin), reads=[t2], writes=[out])


def prep_wq(w_qkv, bq, hp):
    s0 = 4 * bq + 2 * hp
    out = np.empty((9, 128, KC * 128), np.float32)
    for cc in range(9):
        cols = w_qkv[:, cc * 1024 + s0 * 64:cc * 1024 + s0 * 64 + 128]
        out[cc] = cols.reshape(KC, 128, 128).transpose(1, 0, 2).reshape(128, KC * 128)
    return out


TR = 256
CH = 64
NCH = TR // CH
C0 = float(np.exp(-0.5))
GN_EPS = 64e-5
QA, QR, QB, QK, QBH, QKH, QV, QP = range(8)


def kr_consts():
    identf = np.eye(128, dtype=np.float32)
    blockones = np.zeros((128, 128), np.float32)
    blockones[:64, :64] = 1
    blockones[64:, 64:] = 1
    s = np.arange(128)[:, None]
    t = np.arange(128)[None, :]
    su = (s < t).astype(np.float32)
    iu = (s <= t).astype(np.float32)
    masku4 = np.concatenate([su, iu, su, iu], 1)
    maskl = (s > t).astype(np.float32)
    resetm = np.ones((128, TR), np.float32)
    resetm[:, ::CH] = 0.0
    return dict(identf=identf, blockones=blockones, masku4=masku4, maskl=maskl, resetm=resetm)


def build_kr(S):
    nc = bass.Bass("TRN2", target_bir_lowering=False)
    xT = nc.dram_tensor("xT", [D, S], F32, kind="ExternalInput").ap()
    cT = nc.dram_tensor("cT", [128, KC, 1], F32, kind="ExternalInput").ap()
    adaw = nc.dram_tensor("adaw", [6, 2, 128, KC, 512], F32, kind="ExternalInput").ap()
    adab = nc.dram_tensor("adab", [128, 6, KC], F32, kind="ExternalInput").ap()
    gm = nc.dram_tensor("gm", [128, KC], F32, kind="ExternalInput").ap()
    wcat = nc.dram_tensor("wcat", [8, 128, KC * 128], F32, kind="ExternalInput").ap()
    mus_d = nc.dram_tensor("mus", [128, 6, KC], F32, kind="ExternalInput").ap()
    b2_d = nc.dram_tensor("b2", [3, 128, 256], F32, kind="ExternalInput").ap()
    vec_d = nc.dram_tensor("vec", [128, 2, 8], F32, kind="ExternalInput").ap()
    lg_d = nc.dram_tensor("lg8", [2, 128, NCH * 128], F32, kind="ExternalInput").ap()
    lb_d = nc.dram_tensor("lb8", [2, 128, NCH * 128], F32, kind="ExternalInput").ap()
    id_d = nc.dram_tensor("identf", [128, 128], F32, kind="ExternalInput").ap()
    bo_d = nc.dram_tensor("blockones", [128, 128], F32, kind="ExternalInput").ap()
    mu4_d = nc.dram_tensor("masku4", [128, 512], F32, kind="ExternalInput").ap()
    ml_d = nc.dram_tensor("maskl", [128, 128], F32, kind="ExternalInput").ap()
    rs_d = nc.dram_tensor("resetm", [128, TR], F32, kind="ExternalInput").ap()
    yo = nc.dram_tensor("yo", [2, 2, S, 64], F32, kind="ExternalOutput").ap()
    from contextlib import ExitStack
    with ExitStack() as st:
        P = Prog(nc, st)
        make_eps(P, NORM_EPS)
        make_eps(P, GN_EPS)

        def const(name, d, shape, dt=F32):
            b = P.sb(name + "_sb", shape, dt)
            P.dma("sp", [lambda e: e.dma_start(out=b[tuple(slice(None) for _ in shape)], in_=d)], writes=[b])
            return b
        identf = const("identf", id_d, [128, 128])
        bones = const("bones", bo_d, [128, 128])
        masku4 = const("masku4", mu4_d, [128, 512])
        maskl = const("maskl", ml_d, [128, 128])
        resetm = const("resetm", rs_d, [128, TR])
        gms = const("gms", gm, [128, KC])
        mus = const("mus", mus_d, [128, 6, KC])
        vec = const("vec", vec_d, [128, 2, 8])
        lg8 = [const("lg8_%d" % g, lg_d[g], [128, NCH * 128]) for g in range(2)]
        lb8 = [const("lb8_%d" % g, lb_d[g], [128, NCH * 128]) for g in range(2)]
        b2f = P.sb("b2f", [128, 3, 256])
        P.dma("sp", [lambda e, i=i: e.dma_start(out=b2f[:, i, :], in_=b2_d[i]) for i in range(3)], writes=[b2f])
        b2r = P.sb("b2r", [128, 3, 256], F32R)
        P.op("act", lambda e: e.copy(out=b2r[:, :, :], in_=b2f[:, :, :]), reads=[b2f], writes=[b2r])
        identR = P.sb("identR", [128, 128], F32R)
        P.op("act", lambda e: e.copy(out=identR[:, :], in_=identf[:, :]), reads=[identf], writes=[identR])
        onesf = P.sb("onesf", [128, 128])
        P.op("dve", lambda e: e.memset(onesf[:, :], 1.0), writes=[onesf])
        onesR = P.sb("onesR", [128, 128], F32R)
        P.op("dve", lambda e: e.tensor_copy(out=onesR[:, :], in_=onesf[:, :]), reads=[onesf], writes=[onesR])
        onem = P.sb("onem", [128, 6, KC])
        P.op("dve", lambda e: e.tensor_scalar(out=onem[:, :, :], in0=mus[:, :, :], scalar1=-1.0, scalar2=1.0, op0=ALU.mult, op1=ALU.add), reads=[mus], writes=[onem])
        pp = [P.ps("pp%d" % i, [128, 512]) for i in range(2)]
        ps_stat = P.ps("ps_stat", [128, 512])
        bA = [P.ps("bA%d" % i, [128, 512]) for i in range(2)]
        bX = [P.ps("bX%d" % i, [128, 512]) for i in range(2)]
        psT = P.ps("psT", [128, 512])
        xt = P.sb("xt", [128, KC, TR])
        h = P.sb("h", [128, KC, TR], F32R)
        hp = P.sb("hp", [128, KC, TR], F32R)
        Wa = P.sb("Wa", [128, 8, KC * 128], F32R)
        Wb = P.sb("Wb", [128, 8, KC * 128], F32R)
        rstd = P.sb("rstd", [128, TR])
        rT = [P.sb("rT%d" % g, [128, TR]) for g in range(2)]
        kT = [P.sb("kT%d" % g, [128, TR]) for g in range(2)]
        vT = [P.sb("vT%d" % g, [128, TR]) for g in range(2)]
        L1 = P.sb("L1", [128, TR], F32R)
        sgd = P.sb("sgd", [128, NCH, 2, CH], F32R)
        tmpn = ["sgw", "aT", "kk", "kkn", "kp", "ka", "cum", "gam", "gami", "game", "rev", "t1", "t2"]
        T_ = {n: P.sb("tm_" + n, [128, TR]) for n in tmpn}
        SRC = [P.sb("SRC%d" % g, [128, 8, TR]) for g in range(2)]
        gamc = [P.sb("gamc%d" % g, [128, NCH]) for g in range(2)]
        EXP = [P.sb("EXP%d" % g, [128, 8, 2, CH], F32R) for g in range(2)]
        NKM = [P.sb("NKM%d" % g, [128, 512], F32R) for g in range(2)]
        L0 = [P.sb("L0_%d" % g, [128, 128], F32R) for g in range(2)]
        XB = [P.sb("XB%d" % g, [128, 5, 128], F32R) for g in range(2)]
        NL = [[P.sb("NL%d_%d" % (g, i), [128, 256], F32R) for i in range(2)] for g in range(2)]
        PT = [P.sb("PT%d" % g, [128, 128]) for g in range(2)]
        Qc = [P.sb("Qc%d" % g, [128, 128]) for g in range(2)]
        OmT = [P.sb("OmT%d" % g, [128, 128]) for g in range(2)]
        Y0 = [P.sb("Y0_%d" % g, [128, 128]) for g in range(2)]
        coef = [P.sb("coef%d" % g, [128, NCH, 2]) for g in range(2)]
        Z = [[P.sb("Z%d_%d" % (g, i), [128, 128]) for i in range(2)] for g in range(2)]
        y8 = [P.sb("y8_%d" % g, [128, NCH, 128]) for g in range(2)]
        V8 = [P.sb("V8_%d" % g, [128, NCH, 128]) for g in range(2)]
        st8 = [P.sb("st8_%d" % g, [128, 4, NCH]) for g in range(2)]
        zt = P.sb("zt", [128, 1024])
        P.op("dve", lambda e: e.memset(zt[:, :], 0.0), writes=[zt])
        for g in range(2):
            P.op("dve", lambda e, g=g: e.memset(Z[g][0][:, :], 0.0), writes=[Z[g][0]])
            P.op("dve", lambda e, g=g: e.tensor_copy(out=EXP[g][:, :, :, :].rearrange("p q h c -> p (q h c)"), in_=zt[:, :]), reads=[zt], writes=[EXP[g]])
        P.op("dve", lambda e: e.tensor_copy(out=hp[:, :, 0:1], in_=zt[:, 0:KC].rearrange("p (k o) -> p k o", o=1)), reads=[zt], writes=[hp])
        mod = emit_mod(P, cT, adaw, adab, [0, 1], 1, ps_stat, [SRC[0]])
        am = P.sb("am", [128, KC])
        P.op("dve", lambda e: e.scalar_tensor_tensor(out=am[:, :], in0=mod[:, 1, :, 0], scalar=1.0, in1=gms[:, :], op0=ALU.add, op1=ALU.mult),
             reads=[mod, gms], writes=[am])
        MOF = {0: 0, 1: 0, 2: 2, 3: 2, 4: 3, 5: 3, 7: 5}
        for c0 in range(0, 8, 2):
            P.dma("sp", [lambda e, c=c: e.dma_start(out=xt[:, 4 * (c % 2):4 * (c % 2) + 4, :], in_=wcat[c].rearrange("p (a b) -> p a b", a=4)) for c in (c0, c0 + 1)], writes=[xt])
            for c in (c0, c0 + 1):
                src = xt[:, 4 * (c % 2):4 * (c % 2) + 4, :].rearrange("p a b -> p (a b)")
                for k in range(KC):
                    segs = [(0, 128, MOF[c])] if c != 6 else [(0, 64, 1), (64, 128, 4)]
                    for (lo, hi, m) in segs:
                        P.op("dve", lambda e, c=c, k=k, lo=lo, hi=hi, m=m, src=src: e.tensor_scalar(out=Wa[:, c, k * 128 + lo:k * 128 + hi], in0=src[:, k * 128 + lo:k * 128 + hi], scalar1=onem[:, m, k:k + 1], scalar2=None, op0=ALU.mult),
                             reads=[xt, onem], writes=[Wa])
                        P.op("act", lambda e, c=c, k=k, lo=lo, hi=hi, m=m, src=src: e.activation(out=Wb[:, c, k * 128 + lo:k * 128 + hi], in_=src[:, k * 128 + lo:k * 128 + hi], func=AF.Identity, scale=mus[:, m, k:k + 1]),
                             reads=[xt, mus], writes=[Wb])
        ntile = S // TR
        zpar = [0, 0]
        cnt = {"pp": 0}

        def nextpp():
            b = pp[cnt["pp"] % 2]
            cnt["pp"] += 1
            return b
        for ti in range(ntile):
            t0 = ti * TR
            P.dma("sp", [lambda e, t0=t0, k=k: e.dma_start(out=xt[:, k, :], in_=xT[k * 128:(k + 1) * 128, t0:t0 + TR]) for k in range(KC)], writes=[xt])
            P.op("act", lambda e: e.activation(out=h[:, :, :], in_=xt[:, :, :], func=AF.Square), reads=[xt], writes=[h])
            for k in range(KC):
                MM(P, ps_stat[:, 0:TR], onesR[:, :], h[:, k, :], k == 0, k == KC - 1, [onesR, h], [ps_stat])
            P.op("act", lambda e: e.activation(out=rstd[:, :], in_=ps_stat[:, 0:TR], func=AF.Sqrt, scale=1.0 / D, bias=eps_ap(P, NORM_EPS)), reads=[ps_stat], writes=[rstd])
            P.op("dve", lambda e: e.reciprocal(out=rstd[:, :], in_=rstd[:, :]), reads=[rstd], writes=[rstd])
            for k in range(KC):
                P.op("dve", lambda e, k=k: e.tensor_tensor(out=xt[:, k, :], in0=xt[:, k, :], in1=rstd[:, :], op=ALU.mult), reads=[xt, rstd], writes=[xt])
            for k in range(KC):
                P.op("act", lambda e, k=k: e.activation(out=h[:, k, :], in_=xt[:, k, :], func=AF.Identity, scale=am[:, k:k + 1], bias=mod[:, 0, k, 0:1]),
                     reads=[xt, am, mod], writes=[h])
            P.op("act", lambda e: e.copy(out=hp[:, :, 1:TR], in_=h[:, :, 0:TR - 1]), reads=[h], writes=[hp])
            pso = {}
            for cc in range(8):
                ps = nextpp()
                for k in range(KC):
                    MM(P, ps[:, 0:TR], Wa[:, cc, k * 128:(k + 1) * 128], h[:, k, :], k == 0, False, [Wa, h], [ps])
                for k in range(KC):
                    MM(P, ps[:, 0:TR], Wb[:, cc, k * 128:(k + 1) * 128], hp[:, k, :], False, k == KC - 1, [Wb, hp], [ps])
                if cc < 6:
                    dstb = (rT, kT, vT)[cc // 2][cc % 2]
                    P.op("act", lambda e, ps=ps, dstb=dstb: e.copy(out=dstb[:, :], in_=ps[:, 0:TR]), reads=[ps], writes=[dstb])
                elif cc == 6:
                    P.op("act", lambda e, ps=ps: e.activation(out=L1[0:64, :], in_=ps[0:64, 0:TR], func=AF.Tanh), reads=[ps], writes=[L1])
                    P.op("act", lambda e, ps=ps: e.copy(out=L1[64:128, :], in_=ps[64:128, 0:TR]), reads=[ps], writes=[L1])
                else:
                    for hh in range(2):
                        P.op("act", lambda e, ps=ps, hh=hh: e.activation(out=sgd[:, :, hh, :], in_=ps[:, 0:TR].rearrange("p (c t) -> p c t", c=NCH), func=AF.Sigmoid), reads=[ps], writes=[sgd])
            P.op("act", lambda e: e.copy(out=hp[:, :, 0:1], in_=h[:, :, TR - 1:TR]), reads=[h, hp], writes=[hp])
            for g in range(2):
                T = T_
                gc = slice(g * 128, (g + 1) * 128)
                ps = nextpp()
                MM(P, ps[:, 0:TR], b2r[:, 0, gc], L1[:, :], True, True, [b2r, L1], [ps])
                MM(P, ps[:, TR:2 * TR], b2r[:, 1, gc], L1[:, :], True, True, [b2r, L1], [ps])
                P.op("act", lambda e, ps=ps, g=g: e.activation(out=T["sgw"][:, :], in_=ps[:, 0:TR], func=AF.Sigmoid, bias=vec[:, g, 0:1], scale=1.0), reads=[ps, vec], writes=[T["sgw"]])
                P.op("act", lambda e, ps=ps, g=g: e.activation(out=T["aT"][:, :], in_=ps[:, TR:2 * TR], func=AF.Sigmoid, bias=vec[:, g, 1:2], scale=1.0), reads=[ps, vec], writes=[T["aT"]])
                P.op("dve", lambda e, g=g: e.tensor_scalar(out=T["kk"][:, :], in0=kT[g][:, :], scalar1=vec[:, g, 2:3], scalar2=None, op0=ALU.mult), reads=[kT[g], vec], writes=[T["kk"]])
                P.op("act", lambda e: e.activation(out=T["t1"][:, :], in_=T["kk"][:, :], func=AF.Square), reads=[T["kk"]], writes=[T["t1"]])
                MM(P, ps_stat[:, 0:TR], bones[:, :], T["t1"][:, :], True, True, [bones, T["t1"]], [ps_stat])
                P.op("act", lambda e: e.activation(out=T["t2"][:, :], in_=ps_stat[:, 0:TR], func=AF.Sqrt), reads=[ps_stat], writes=[T["t2"]])
                P.op("dve", lambda e: e.tensor_scalar(out=T["t2"][:, :], in0=T["t2"][:, :], scalar1=1e-12, scalar2=None, op0=ALU.max), reads=[T["t2"]], writes=[T["t2"]])
                P.op("dve", lambda e: e.reciprocal(out=T["t2"][:, :], in_=T["t2"][:, :]), reads=[T["t2"]], writes=[T["t2"]])
                P.op("dve", lambda e: e.tensor_tensor(out=T["kkn"][:, :], in0=T["kk"][:, :], in1=T["t2"][:, :], op=ALU.mult), reads=[T["kk"], T["t2"]], writes=[T["kkn"]])
                P.op("dve", lambda e, g=g: e.tensor_scalar(out=T["t1"][:, :], in0=T["aT"][:, :], scalar1=-1.0, scalar2=vec[:, g, 3:4], op0=ALU.add, op1=ALU.mult), reads=[T["aT"], vec], writes=[T["t1"]])
                P.op("dve", lambda e, g=g: e.scalar_tensor_tensor(out=T["kp"][:, :], in0=T["t1"][:, :], scalar=1.0, in1=kT[g][:, :], op0=ALU.add, op1=ALU.mult), reads=[T["t1"], kT[g]], writes=[T["kp"]])
                P.op("dve", lambda e, g=g: e.scalar_tensor_tensor(out=SRC[g][:, QP, :], in0=rT[g][:, :], scalar=vec[:, g, 4:5], in1=T["kp"][:, :], op0=ALU.mult, op1=ALU.mult), reads=[rT[g], vec, T["kp"]], writes=[SRC[g]])
                P.op("act", lambda e, g=g: e.copy(out=SRC[g][:, QV, :], in_=vT[g][:, :]), reads=[vT[g]], writes=[SRC[g]])
                P.op("dve", lambda e: e.tensor_tensor(out=T["ka"][:, :], in0=T["kkn"][:, :], in1=T["aT"][:, :], op=ALU.mult), reads=[T["kkn"], T["aT"]], writes=[T["ka"]])
                P.op("dve", lambda e: e.tensor_tensor_scan(out=T["cum"][:, :], data0=resetm[:, :], data1=T["sgw"][:, :], initial=0.0, op0=ALU.mult, op1=ALU.add), reads=[resetm, T["sgw"]], writes=[T["cum"]])
                P.op("act", lambda e: e.activation(out=T["gam"][:, :], in_=T["cum"][:, :], func=AF.Exp, scale=-C0), reads=[T["cum"]], writes=[T["gam"]])
                P.op("act", lambda e: e.activation(out=T["gami"][:, :], in_=T["cum"][:, :], func=AF.Exp, scale=C0), reads=[T["cum"]], writes=[T["gami"]])
                P.op("dve", lambda e: e.tensor_tensor(out=T["t1"][:, :], in0=T["cum"][:, :], in1=T["sgw"][:, :], op=ALU.subtract), reads=[T["cum"], T["sgw"]], writes=[T["t1"]])
                P.op("act", lambda e: e.activation(out=T["game"][:, :], in_=T["t1"][:, :], func=AF.Exp, scale=-C0), reads=[T["t1"]], writes=[T["game"]])
                for c in range(NCH):
                    P.op("dve", lambda e, c=c: e.tensor_scalar(out=T["t2"][:, c * CH:(c + 1) * CH], in0=T["cum"][:, c * CH:(c + 1) * CH], scalar1=T["cum"][:, c * CH + CH - 1:c * CH + CH], scalar2=None, op0=ALU.subtract),
                         reads=[T["cum"]], writes=[T["t2"]])
                P.op("act", lambda e: e.activation(out=T["rev"][:, :], in_=T["t2"][:, :], func=AF.Exp, scale=C0), reads=[T["t2"]], writes=[T["rev"]])
                P.op("act", lambda e, g=g: e.copy(out=gamc[g][:, :], in_=T["gam"][:, CH - 1:TR:CH]), reads=[T["gam"]], writes=[gamc[g]])
                P.op("dve", lambda e, g=g: e.scalar_tensor_tensor(out=SRC[g][:, QA, :], in0=T["game"][:, :], scalar=-1.0, in1=T["kkn"][:, :], op0=ALU.mult, op1=ALU.mult), reads=[T["game"], T["kkn"]], writes=[SRC[g]])
                P.op("dve", lambda e, g=g: e.tensor_tensor(out=SRC[g][:, QR, :], in0=rT[g][:, :], in1=T["gam"][:, :], op=ALU.mult), reads=[rT[g], T["gam"]], writes=[SRC[g]])
                P.op("dve", lambda e, g=g: e.tensor_tensor(out=SRC[g][:, QB, :], in0=T["ka"][:, :], in1=T["gami"][:, :], op=ALU.mult), reads=[T["ka"], T["gami"]], writes=[SRC[g]])
                P.op("dve", lambda e, g=g: e.tensor_tensor(out=SRC[g][:, QK, :], in0=T["kp"][:, :], in1=T["gami"][:, :], op=ALU.mult), reads=[T["kp"], T["gami"]], writes=[SRC[g]])
                P.op("dve", lambda e, g=g: e.tensor_tensor(out=SRC[g][:, QBH, :], in0=T["ka"][:, :], in1=T["rev"][:, :], op=ALU.mult), reads=[T["ka"], T["rev"]], writes=[SRC[g]])
                P.op("dve", lambda e, g=g: e.tensor_tensor(out=SRC[g][:, QKH, :], in0=T["kp"][:, :], in1=T["rev"][:, :], op=ALU.mult), reads=[T["kp"], T["rev"]], writes=[SRC[g]])
            GS = (0, 1)
            for c in range(NCH):
                cs = slice(c * CH, (c + 1) * CH)
                for g in GS:
                    P.op("act", lambda e, g=g, cs=cs: e.copy(out=EXP[g][0:64, :, 0, :], in_=SRC[g][0:64, :, cs]), reads=[SRC[g]], writes=[EXP[g]])
                    P.op("dve", lambda e, g=g, cs=cs: e.tensor_copy(out=EXP[g][64:128, :, 1, :], in_=SRC[g][64:128, :, cs]), reads=[SRC[g]], writes=[EXP[g]])

                def ex(g, q):
                    return EXP[g][:, q, :, :].rearrange("p h c -> p (h c)")

                def ex2(g, q):
                    return EXP[g][:, q:q + 2, :, :].rearrange("p q h c -> p (q h c)")
                for g in GS:
                    psAB, psLT, psXS, psB4 = bA[g], bX[g], bX[g], bA[g]
                    MM(P, psAB[:, 0:256], ex(g, QB), ex2(g, QA), True, True, [EXP[g]], [psAB])
                    MM(P, psAB[:, 256:512], ex(g, QK), ex2(g, QA), True, True, [EXP[g]], [psAB])
                    P.op("dve", lambda e, psAB=psAB, g=g: e.tensor_tensor(out=NKM[g][:, :], in0=psAB[:, :], in1=masku4[:, :], op=ALU.mult), reads=[psAB, masku4], writes=[NKM[g]])
                    MM(P, psLT[:, 0:128], ex(g, QA), ex(g, QB), True, True, [EXP[g]], [psLT])
                    P.op("dve", lambda e, psLT=psLT, g=g: e.tensor_tensor(out=L0[g][:, :], in0=psLT[:, 0:128], in1=maskl[:, :], op=ALU.mult), reads=[psLT, maskl], writes=[L0[g]])
                    for i, q in enumerate((QA, QBH, QKH, QV)):
                        MM(P, psT[:, i * 128:(i + 1) * 128], ex(g, q), identR[:, :], True, True, [EXP[g], identR], [psT])
                    P.op("act", lambda e, g=g: e.copy(out=XB[g][:, 1:5, :].rearrange("p a b -> p (a b)"), in_=psT[:, :]), reads=[psT], writes=[XB[g]])
                    MM(P, psLT[:, 128:256], NKM[g][:, 256:384], XB[g][:, 4, :], True, True, [NKM[g], XB[g]], [psLT])
                    P.op("act", lambda e, psLT=psLT, g=g: e.copy(out=XB[g][:, 0, :], in_=psLT[:, 128:256]), reads=[psLT], writes=[XB[g]])
                    P.op("act", lambda e, g=g, c=c: e.copy(out=V8[g][:, c, :], in_=XB[g][:, 4, :]), reads=[XB[g]], writes=[V8[g]])
                Ncur = {g: (NKM[g], NKM[g][:, 0:128]) for g in GS}
                Lcur = {g: (L0[g], L0[g][:, :]) for g in GS}
                for i in range(6):
                    for g in GS:
                        psXS = bX[g]
                        nb, nap = Ncur[g]
                        X = XB[g][:, 0:2, :].rearrange("p a b -> p (a b)")
                        MM(P, psXS[:, 0:256], nap, X, True, True, [nb, XB[g]], [psXS])
                        P.op("dve", lambda e, psXS=psXS, X=X: e.tensor_tensor(out=X, in0=psXS[:, 0:256], in1=X, op=ALU.add), reads=[psXS, XB[g]], writes=[XB[g]])
                        if i < 5:
                            lb_, lap = Lcur[g]
                            nl = NL[g][i % 2]
                            MM(P, psXS[:, 256:384], lap, nap, True, True, [lb_, nb], [psXS])
                            if i < 4:
                                MM(P, psXS[:, 384:512], nap, lap, True, True, [lb_, nb], [psXS])
                                P.op("act", lambda e, psXS=psXS, nl=nl: e.copy(out=nl[:, :], in_=psXS[:, 256:512]), reads=[psXS], writes=[nl])
                            else:
                                P.op("act", lambda e, psXS=psXS, nl=nl: e.copy(out=nl[:, 0:128], in_=psXS[:, 256:384]), reads=[psXS], writes=[nl])
                            Ncur[g] = (nl, nl[:, 0:128])
                            Lcur[g] = (nl, nl[:, 128:256])
                for g in GS:
                    psB4, psLT = bA[g], bX[g]
                    Wbd, U0, Bh, Kh, Vb = (XB[g][:, 1, :], XB[g][:, 0, :], XB[g][:, 2, :], XB[g][:, 3, :], XB[g][:, 4, :])
                    MrbT, MrkT = NKM[g][:, 128:256], NKM[g][:, 384:512]
                    MM(P, psB4[:, 0:128], Wbd, Bh, True, True, [XB[g]], [psB4])
                    MM(P, psB4[:, 128:256], Bh, U0, True, False, [XB[g]], [psB4])
                    MM(P, psB4[:, 128:256], Kh, Vb, False, True, [XB[g]], [psB4])
                    MM(P, psB4[:, 256:384], Wbd, MrbT, True, True, [XB[g], NKM[g]], [psB4])
                    MM(P, psB4[:, 384:512], MrbT, U0, True, False, [XB[g], NKM[g]], [psB4])
                    MM(P, psB4[:, 384:512], MrkT, Vb, False, True, [XB[g], NKM[g]], [psB4])
                    P.op("dve", lambda e, psB4=psB4, g=g, c=c: e.scalar_tensor_tensor(out=PT[g][:, :], in0=identf[:, :], scalar=gamc[g][:, c:c + 1], in1=psB4[:, 0:128], op0=ALU.mult, op1=ALU.add),
                         reads=[identf, gamc[g], psB4], writes=[PT[g]])
                    P.op("dve", lambda e, psB4=psB4, g=g: e.tensor_copy(out=Qc[g][:, :], in_=psB4[:, 128:256]), reads=[psB4], writes=[Qc[g]])
                    P.op("dve", lambda e, psB4=psB4, g=g: e.tensor_tensor(out=OmT[g][:, :], in0=psB4[:, 256:384], in1=ex(g, QR), op=ALU.add), reads=[psB4, EXP[g]], writes=[OmT[g]])
                    P.op("dve", lambda e, psB4=psB4, g=g: e.tensor_copy(out=Y0[g][:, :], in_=psB4[:, 384:512]), reads=[psB4], writes=[Y0[g]])
                    MM(P, psLT[:, 256:258], ex(g, QP), onesR[:, 0:2], True, True, [EXP[g], onesR], [psLT])
                    P.op("act", lambda e, psLT=psLT, g=g, c=c: e.copy(out=coef[g][:, c, :], in_=psLT[:, 256:258]), reads=[psLT], writes=[coef[g]])
                for g in GS:
                    Zc = Z[g][zpar[g]]
                    Zn = Z[g][1 - zpar[g]]
                    MM(P, ps_stat[:, 0:128], OmT[g][:, :], Zc[:, :], True, True, [OmT[g], Zc], [ps_stat])
                    MM(P, ps_stat[:, 128:256], PT[g][:, :], Zc[:, :], True, True, [PT[g], Zc], [ps_stat])
                    P.op("dve", lambda e, g=g, c=c: e.tensor_tensor(out=y8[g][:, c, :], in0=ps_stat[:, 0:128], in1=Y0[g][:, :], op=ALU.add), reads=[ps_stat, Y0[g]], writes=[y8[g]])
                    P.op("dve", lambda e, g=g, Zn=Zn: e.tensor_tensor(out=Zn[:, :], in0=ps_stat[:, 128:256], in1=Qc[g][:, :], op=ALU.add), reads=[ps_stat, Qc[g]], writes=[Zn])
                    zpar[g] = 1 - zpar[g]
            for g in GS:
                s8 = st8[g]
                sqv = SRC[g][:, 0:2, :].rearrange("p a (c f) -> p (a c) f", f=128)
                P.op("act", lambda e, g=g, sqv=sqv: e.activation(out=sqv, in_=y8[g][:, :, :], func=AF.Square), reads=[y8[g]], writes=[SRC[g]])
                P.op("dve", lambda e, g=g, s8=s8: e.tensor_reduce(out=s8[:, 0, :], in_=y8[g][:, :, :], axis=AX.X, op=ALU.add), reads=[y8[g]], writes=[s8])
                P.op("dve", lambda e, g=g, s8=s8, sqv=sqv: e.tensor_reduce(out=s8[:, 1, :], in_=sqv, axis=AX.X, op=ALU.add), reads=[SRC[g]], writes=[s8])
                P.op("dve", lambda e, s8=s8: e.tensor_scalar(out=s8[:, 0, :], in0=s8[:, 0, :], scalar1=1.0 / 64, scalar2=None, op0=ALU.mult), reads=[s8], writes=[s8])
                P.op("dve", lambda e, s8=s8: e.tensor_tensor(out=s8[:, 2, :], in0=s8[:, 0, :], in1=s8[:, 0, :], op=ALU.mult), reads=[s8], writes=[s8])
                P.op("dve", lambda e, s8=s8: e.scalar_tensor_tensor(out=s8[:, 3, :], in0=s8[:, 1, :], scalar=1.0 / 64, in1=s8[:, 2, :], op0=ALU.mult, op1=ALU.subtract), reads=[s8], writes=[s8])
                P.op("act", lambda e, s8=s8: e.activation(out=s8[:, 3, :], in_=s8[:, 3, :], func=AF.Sqrt, scale=1.0, bias=eps_ap(P, GN_EPS)), reads=[s8], writes=[s8])
                P.op("dve", lambda e, s8=s8: e.reciprocal(out=s8[:, 3, :], in_=s8[:, 3, :]), reads=[s8], writes=[s8])
                for c in range(NCH):
                    P.op("dve", lambda e, g=g, c=c, s8=s8: e.tensor_scalar(out=y8[g][:, c, :], in0=y8[g][:, c, :], scalar1=s8[:, 0, c:c + 1], scalar2=s8[:, 3, c:c + 1], op0=ALU.subtract, op1=ALU.mult),
                         reads=[y8[g], s8], writes=[y8[g]])
                yf = y8[g][:, :, :].rearrange("p c f -> p (c f)")
                P.op("dve", lambda e, g=g, yf=yf: e.tensor_tensor(out=yf, in0=yf, in1=lg8[g][:, :], op=ALU.mult), reads=[y8[g], lg8[g]], writes=[y8[g]])
                P.op("dve", lambda e, g=g, yf=yf: e.tensor_tensor(out=yf, in0=yf, in1=lb8[g][:, :], op=ALU.add), reads=[y8[g], lb8[g]], writes=[y8[g]])
                for c in range(NCH):
                    P.op("dve", lambda e, g=g, c=c: e.scalar_tensor_tensor(out=y8[g][:, c, :], in0=V8[g][:, c, :], scalar=coef[g][:, c, 0:1], in1=y8[g][:, c, :], op0=ALU.mult, op1=ALU.add),
                         reads=[V8[g], coef[g], y8[g]], writes=[y8[g]])
                psg_ = nextpp()
                for c in range(NCH):
                    MM(P, psg_[:, c * 128:(c + 1) * 128], sgd[:, c, :, :].rearrange("p h t -> p (h t)"), b2r[:, 2, g * 128:(g + 1) * 128], True, True, [sgd, b2r], [psg_])
                P.op("dve", lambda e, g=g, yf=yf, psg_=psg_: e.tensor_tensor(out=yf, in0=psg_[:, 0:NCH * 128], in1=yf, op=ALU.mult), reads=[psg_, y8[g]], writes=[y8[g]])
                P.dma("sp", [lambda e, g=g, hh=hh, t0=t0: e.dma_start(out=yo[g, hh, t0:t0 + TR, :].rearrange("(c t) f -> t c f", c=NCH), in_=y8[g][hh * 64:(hh + 1) * 64, :, hh * 64:(hh + 1) * 64]) for hh in range(2)],
                      reads=[y8[g]], sem_buf=y8[g])
        stats = P.emit()
    return nc, stats


def prep_kr_core(p, bq):
    ch0 = bq * 256
    f = np.float32

    def colchunk(w, c0, n=128):
        return np.ascontiguousarray(w[:, c0:c0 + n]).reshape(KC, 128, n).transpose(1, 0, 2)
    chunks = []
    for w in (p["w_r"], p["w_k"], p["w_v"]):
        for g in range(2):
            chunks.append(colchunk(w, ch0 + g * 128))
    chunks.append(np.concatenate([colchunk(p["wla"], 0, 64), colchunk(p["ala"], 0, 64)], 2))
    chunks.append(colchunk(p["gla"], 0, 128))
    wcat = np.ascontiguousarray(np.stack(chunks)).reshape(8, 128, KC * 128).astype(f)
    b2 = np.zeros((3, 128, 256), f)
    b2[0, 0:64] = p["wlb"][:, ch0:ch0 + 256]
    b2[1, 64:128] = p["alb"][:, ch0:ch0 + 256]
    b2[2] = p["glb"][:, ch0:ch0 + 256]
    vec = np.zeros((128, 2, 8), f)
    for g in range(2):
        sl = slice(ch0 + g * 128, ch0 + (g + 1) * 128)
        vec[:, g, 0] = p["w0"][sl]
        vec[:, g, 1] = p["a0"][sl]
        vec[:, g, 2] = p["k_k"][sl]
        vec[:, g, 3] = p["k_a"][sl]
        vec[:, g, 4] = p["r_k"].reshape(-1)[sl]
    lg8 = np.zeros((2, 128, NCH, 128), f)
    lb8 = np.zeros((2, 128, NCH, 128), f)
    for g in range(2):
        for hh in range(2):
            sl = slice(ch0 + g * 128 + hh * 64, ch0 + g * 128 + hh * 64 + 64)
            lg8[g, hh * 64:(hh + 1) * 64, :, hh * 64:(hh + 1) * 64] = p["ln_g"][sl][None, None, :]
            lb8[g, hh * 64:(hh + 1) * 64, :, hh * 64:(hh + 1) * 64] = p["ln_b"][sl][None, None, :]
    mus = np.ascontiguousarray(p["mu"].reshape(6, KC, 128).transpose(2, 0, 1)).astype(f)
    return dict(wcat=wcat, b2=b2, vec=vec, lg8=lg8.reshape(2, 128, NCH * 128), lb8=lb8.reshape(2, 128, NCH * 128), mus=mus)


_PROGS = {}


def _prog(key, builder):
    if key not in _PROGS:
        _PROGS[key] = builder()[0]
    return _PROGS[key]


def _run(nc, in_maps):
    res = run_bass_kernel_spmd(nc, in_maps, core_ids=list(range(NCORES)))
    return res.results


def _f32(a):
    return np.ascontiguousarray(np.asarray(a, dtype=np.float32))


def kernel(**inp):
    x = _f32(inp["x"])
    B, S, _ = x.shape
    NT = B * S // NCORES
    CPB = NCORES // B
    c = _f32(inp["c"])
    pos = np.ascontiguousarray(np.asarray(inp["positions"]).astype(np.int32, copy=False))
    ada_w = _f32(inp["ada_w"])
    ada_b = _f32(inp["ada_b"])
    xT = [np.ascontiguousarray(x[b].T) for b in range(B)]
    cTs = [prep_c(c[b:b + 1]) for b in range(B)]

    def tail_common(i):
        g, u, d = prep_ffn(_f32(inp["ffn_w_gate"][i]), _f32(inp["ffn_w_up"][i]), _f32(inp["ffn_w_down"][i]))
        return dict(wg=g, wu=u, wd=d, adaw=prep_adaw(ada_w[i]), adab=prep_adab(ada_b[i]), gf=prep_vec(_f32(inp["norm_ffn_g"][i])))

    def run_tail(i, yT, w_o):
        nc = _prog(("kb", NT), lambda: build_kb(NT, False))
        common = tail_common(i)
        common["wo"] = prep_sq(_f32(w_o))
        maps = []
        for cid in range(NCORES):
            b, q = cid // CPB, cid % CPB
            m = dict(common)
            m["xT"] = np.ascontiguousarray(xT[b][:, q * NT:(q + 1) * NT])
            m["yT"] = prep_y_pieces(np.ascontiguousarray(yT[b][:, q * NT:(q + 1) * NT]), NT)
            m["cT"] = cTs[b]
            maps.append(m)
        res = _run(nc, maps)
        for b in range(B):
            xT[b] = np.ascontiguousarray(np.concatenate([res[b * CPB + q]["out"] for q in range(CPB)], axis=1))

    def run_attn(i, j):
        nc = _prog(("ka", S), lambda: build_ka(S))
        common = dict(adaw=prep_adaw(ada_w[i]), adab=prep_adab(ada_b[i]), gm=prep_vec(_f32(inp["norm_mix_g"][i])))
        qg = _f32(inp["attn_q_gain"][j])
        kg = _f32(inp["attn_k_gain"][j])
        common["gains"] = np.ascontiguousarray(np.stack([np.tile(qg, 2), np.tile(kg, 2)], 1))
        common.update(ka_consts())
        wqkv = _f32(inp["attn_w_qkv"][j])
        maps = []
        for cid in range(NCORES):
            b, bq = cid // CPB, cid % CPB
            m = dict(common)
            m["xT"] = xT[b]
            m["cT"] = cTs[b]
            m["posb"] = np.ascontiguousarray(np.broadcast_to(pos[b][None, :], (128, S)))
            m["wq"] = np.stack([prep_wq(wqkv, bq, hp) for hp in range(2)])
            maps.append(m)
        res = _run(nc, maps)
        yT = []
        for b in range(B):
            yT.append(np.ascontiguousarray(np.concatenate([res[b * CPB + bq]["oT"].reshape(256, S) for bq in range(CPB)], axis=0)))
        return yT

    def run_conv(i, j):
        nc = _prog(("kc", NT), lambda: build_kb(NT, True, conv=True))
        common = tail_common(i)
        common["wo"] = prep_sq(_f32(inp["conv_w_pw2"][j]))
        common["bo"] = prep_vec(_f32(inp["conv_b_pw2"][j]))
        common["w1"] = prep_w1(_f32(inp["conv_w_pw1"][j]))
        b1 = _f32(inp["conv_b_pw1"][j])
        common["cvec"] = np.ascontiguousarray(np.stack([prep_vec(b1[:1024]), prep_vec(b1[1024:]), prep_vec(_f32(inp["conv_b_dw"][j])),
                                                        prep_vec(_f32(inp["conv_ln_g"][j])), prep_vec(_f32(inp["conv_ln_b"][j])),
                                                        prep_vec(_f32(inp["norm_mix_g"][i]))], 1))
        common["wdw"] = np.ascontiguousarray(_f32(inp["conv_w_dw"][j]).T.reshape(KC, 128, 31).transpose(1, 0, 2))
        maps = []
        for cid in range(NCORES):
            b, q = cid // CPB, cid % CPB
            m = dict(common)
            m["xT"] = np.ascontiguousarray(xT[b][:, q * NT:(q + 1) * NT])
            m["cT"] = cTs[b]
            if q == 0:
                m["xh"] = np.zeros((D, 32), np.float32)
                m["hon"] = np.zeros((128, 1), np.float32)
            else:
                m["xh"] = np.ascontiguousarray(xT[b][:, q * NT - 32:q * NT])
                m["hon"] = np.ones((128, 1), np.float32)
            maps.append(m)
        res = _run(nc, maps)
        for b in range(B):
            xT[b] = np.ascontiguousarray(np.concatenate([res[b * CPB + q]["out"] for q in range(CPB)], axis=1))

    def run_rwkv(i, j):
        nc = _prog(("kr", S), lambda: build_kr(S))
        common = dict(adaw=prep_adaw(ada_w[i]), adab=prep_adab(ada_b[i]), gm=prep_vec(_f32(inp["norm_mix_g"][i])))
        common.update(kr_consts())
        p = dict(mu=_f32(inp["rwkv_mu"][j]), w_r=_f32(inp["rwkv_w_r"][j]), w_k=_f32(inp["rwkv_w_k"][j]), w_v=_f32(inp["rwkv_w_v"][j]),
                 w0=_f32(inp["rwkv_w0"][j]), wla=_f32(inp["rwkv_w_lora_a"][j]), wlb=_f32(inp["rwkv_w_lora_b"][j]),
                 a0=_f32(inp["rwkv_a0"][j]), ala=_f32(inp["rwkv_a_lora_a"][j]), alb=_f32(inp["rwkv_a_lora_b"][j]),
                 gla=_f32(inp["rwkv_g_lora_a"][j]), glb=_f32(inp["rwkv_g_lora_b"][j]), k_k=_f32(inp["rwkv_k_k"][j]), k_a=_f32(inp["rwkv_k_a"][j]),
                 r_k=_f32(inp["rwkv_r_k"][j]), ln_g=_f32(inp["rwkv_ln_g"][j]), ln_b=_f32(inp["rwkv_ln_b"][j]))
        maps = []
        for cid in range(NCORES):
            b, bq = cid // CPB, cid % CPB
            m = dict(common)
            m["xT"] = xT[b]
            m["cT"] = cTs[b]
            m.update(prep_kr_core(p, bq))
            maps.append(m)
        res = _run(nc, maps)
        yT = []
        for b in range(B):
            ys = [res[b * CPB + bq]["yo"].transpose(2, 0, 1, 3).reshape(S, 256) for bq in range(CPB)]
            yT.append(np.ascontiguousarray(np.concatenate(ys, axis=1).T))
        return yT

    depth = ada_w.shape[0]
    for i in range(depth):
        kind, j = i % 3, i // 3
        if kind == 0:
            yT = run_attn(i, j)
            run_tail(i, yT, inp["attn_w_o"][j])
        elif kind == 1:
            run_conv(i, j)
        else:
            yT = run_rwkv(i, j)
            run_tail(i, yT, inp["rwkv_w_o"][j])
    out = np.stack([np.ascontiguousarray(xT[b].T) for b in range(B)]).astype(np.float32)
    return out
```

```python
import numpy as np
import concourse.bass as bass
import concourse.mybir as mybir
from concourse.bass_utils import run_bass_kernel_spmd

F32 = mybir.dt.float32
F32R = mybir.dt.float32r
BF16 = mybir.dt.bfloat16
I32 = mybir.dt.int32
AF = mybir.ActivationFunctionType
ALU = mybir.AluOpType
AX = mybir.AxisListType

NCORES = 8


class Buf:
    __slots__ = ("name", "t", "last_w", "readers", "dsem", "dcnt", "excl")

    def __init__(self, name, t):
        self.name = name
        self.t = t
        self.last_w = None
        self.readers = []
        self.dsem = None
        self.dcnt = 0
        self.excl = False

    def __getitem__(self, idx):
        return self.t[idx]


class Op:
    __slots__ = ("eng", "fn", "deps", "signal", "ev_sem", "ev_val", "is_dma", "ndma")

    def __init__(self, eng, fn, is_dma=False, ndma=1):
        self.eng = eng
        self.fn = fn
        self.deps = []
        self.signal = False
        self.ev_sem = None
        self.ev_val = None
        self.is_dma = is_dma
        self.ndma = ndma


ENGS = ("pe", "dve", "act", "pool", "sp")


class Prog:
    def __init__(self, nc, stack):
        self.nc = nc
        self.stack = stack
        self.ops = {e: [] for e in ENGS}
        self.esem = {}
        for e in ENGS:
            self.esem[e] = stack.enter_context(nc.semaphore("es_" + e))
        self.nbuf = 0
        self.dma_ops = []
        self.all_sems = [self.esem[e] for e in ENGS]

    def sb(self, name, shape, dt=F32):
        t = self.stack.enter_context(self.nc.sbuf_tensor(name, list(shape), dt))
        return Buf(name, t)

    def ps(self, name, shape, dt=F32):
        t = self.stack.enter_context(self.nc.psum_tensor(name, list(shape), dt))
        b = Buf(name, t)
        b.excl = True
        return b

    def dram(self, name, ap):
        return Buf(name, ap)

    def view(self, name, t):
        return Buf(name, t)

    def _add_deps(self, op, reads, writes):
        deps = op.deps
        for b in reads:
            if b.last_w is not None:
                deps.append(b.last_w)
            if b.excl:
                for r in b.readers:
                    if r.eng != op.eng:
                        deps.append(r)
        for b in writes:
            if b.last_w is not None:
                deps.append(b.last_w)
            deps.extend(b.readers)
        for b in reads:
            b.readers.append(op)
        for b in writes:
            b.last_w = op
            b.readers = []

    def op(self, eng, fn, reads=(), writes=()):
        o = Op(eng, fn)
        self._add_deps(o, reads, writes)
        self.ops[eng].append(o)
        return o

    def dma(self, eng, fns, reads=(), writes=(), sem_buf=None):
        if not isinstance(fns, (list, tuple)):
            fns = [fns]
        if sem_buf is None:
            sem_buf = writes[0] if writes else reads[0]
        if sem_buf.dsem is None:
            sem_buf.dsem = self.stack.enter_context(self.nc.semaphore("ds_%s_%d" % (sem_buf.name, self.nbuf)))
            self.nbuf += 1
            self.all_sems.append(sem_buf.dsem)
        o = Op(eng, fns, is_dma=True, ndma=len(fns))
        self._add_deps(o, reads, writes)
        sem_buf.dcnt += 16 * len(fns)
        o.ev_sem = sem_buf.dsem
        o.ev_val = sem_buf.dcnt
        o.signal = True
        self.ops[eng].append(o)
        self.dma_ops.append(o)
        return o

    def emit(self):
        nc = self.nc
        fin = Op("sp", None)
        fin.deps = list(self.dma_ops)
        for e in ENGS:
            if self.ops[e] and e != "sp":
                last = self.ops[e][-1]
                fin.deps.append(last)
        self.ops["sp"].append(fin)
        for e in ENGS:
            for o in self.ops[e]:
                for d in o.deps:
                    if d.is_dma:
                        continue
                    if d.eng == "pe" and o.eng == "pe":
                        continue
                    d.signal = True
        for e in ENGS:
            k = 0
            for o in self.ops[e]:
                if o.is_dma:
                    continue
                if o.signal:
                    k += 1
                    o.ev_sem = self.esem[e]
                    o.ev_val = k
        handles = {"pe": "tensor", "dve": "vector", "act": "scalar", "pool": "gpsimd", "sp": "sync"}
        stats = {}
        sems = list(self.all_sems)
        with nc.Block() as b0:
            def clr(eng):
                for sm in sems:
                    eng.sem_clear(sm)
            b0.sync(clr)
        with nc.Block() as block:
            for e in ENGS:
                ops = self.ops[e]
                if not ops:
                    continue

                def body(eng, ops=ops, e=e):
                    seen = {}
                    nw = 0
                    for o in ops:
                        need = {}
                        for d in o.deps:
                            if d.eng == "pe" and e == "pe" and not d.is_dma:
                                continue
                            s = d.ev_sem
                            key = id(s)
                            if seen.get(key, 0) >= d.ev_val:
                                continue
                            if key not in need or need[key][1] < d.ev_val:
                                need[key] = (s, d.ev_val)
                        for key, (s, v) in need.items():
                            eng.wait_ge(s, v)
                            seen[key] = v
                            nw += 1
                        if o.fn is None:
                            continue
                        if o.is_dma:
                            for f in o.fn:
                                f(eng).then_inc(o.ev_sem, 16)
                        else:
                            ins = o.fn(eng)
                            if o.signal:
                                ins.then_inc(o.ev_sem, 1)
                    stats[e] = (len(ops), nw)

                getattr(block, handles[e])(body)
        with nc.Block() as b2:
            def clr2(eng):
                for sm in sems:
                    eng.sem_clear(sm)
            b2.sync(clr2)
        self.stats = stats
        return stats


D = 1024
KC = 8
FH = 2816
FC = 22
TT = 512
NORM_EPS = 1e-6
SLOT = 2048


def MM(P, out_ap, lhsT, rhs, start, stop, reads, writes):
    return P.op("pe", lambda e: e.matmul(out_ap, lhsT, rhs, start=start, stop=stop), reads=reads, writes=writes)


class WStream:
    def __init__(self, P, nstage=2, nring=4, lookahead=2, qeng="sp"):
        self.P = P
        self.stage = [P.sb("wst%d" % i, [128, SLOT], F32) for i in range(nstage)]
        self.ring = [P.sb("wrg%d" % i, [128, SLOT], F32R) for i in range(nring)]
        self.pieces = []
        self.req = 0
        self.look = lookahead
        self.qeng = qeng
        self.slot_of = {}
        self.cp = 0
        self.nr = 0

    def plan(self, ap, n, dest=None):
        self.pieces.append((ap, n, dest))
        return len(self.pieces) - 1

    def _request(self, i):
        P = self.P
        ap, n, dest = self.pieces[i]
        st = self.stage[i % len(self.stage)]
        if dest is not None:
            rgb, oap = dest
            src = st[:, 0:n]
            if len(oap.shape) == 3:
                src = src.rearrange("p (k t) -> p k t", k=oap.shape[1])
            P.dma(self.qeng, [lambda e, st=st, ap=ap, n=n: e.dma_start(out=st[:, 0:n], in_=ap)], writes=[st])
            if self.cp % 2 == 0:
                P.op("act", lambda e, src=src, oap=oap: e.copy(out=oap, in_=src), reads=[st], writes=[rgb])
            else:
                P.op("dve", lambda e, src=src, oap=oap: e.tensor_copy(out=oap, in_=src), reads=[st], writes=[rgb])
            self.cp += 1
            self.slot_of[i] = rgb
            return
        rg = self.ring[self.nr % len(self.ring)]
        self.nr += 1
        P.dma(self.qeng, [lambda e, st=st, ap=ap, n=n: e.dma_start(out=st[:, 0:n], in_=ap)], writes=[st])
        if self.cp % 2 == 0:
            P.op("act", lambda e, st=st, rg=rg, n=n: e.copy(out=rg[:, 0:n], in_=st[:, 0:n]), reads=[st], writes=[rg])
        else:
            P.op("dve", lambda e, st=st, rg=rg, n=n: e.tensor_copy(out=rg[:, 0:n], in_=st[:, 0:n]), reads=[st], writes=[rg])
        self.cp += 1
        self.slot_of[i] = rg

    def get(self, i):
        hi = min(len(self.pieces) - 1, i + self.look)
        while self.req <= hi:
            self._request(self.req)
            self.req += 1
        return self.slot_of[i]


def emit_mod(P, cT_d, adaw_d, adab_d, ms, nb, ps, wbufs):
    cT = P.sb("cT_sb", [128, KC, nb])
    cs = P.sb("c_sig", [128, KC, nb])
    adab = P.sb("adab_sb", [128, 6, KC])
    mod = P.sb("mod", [128, len(ms), KC, nb])
    P.dma("sp", [lambda e: e.dma_start(out=cT[:, :, :], in_=cT_d)], writes=[cT])
    P.dma("sp", [lambda e: e.dma_start(out=adab[:, :, :], in_=adab_d)], writes=[adab])
    P.op("act", lambda e: e.activation(out=cs[:, :, :], in_=cT[:, :, :], func=AF.Sigmoid), reads=[cT], writes=[cs])
    P.op("dve", lambda e: e.tensor_tensor(out=cT[:, :, :], in0=cT[:, :, :], in1=cs[:, :, :], op=ALU.mult), reads=[cT, cs], writes=[cT])
    n = 0
    wc = wbufs[0].t.shape[2]
    npc = 512 // wc
    for i, m in enumerate(ms):
        for half in range(2):
            for pc in range(npc):
                wb = wbufs[n % len(wbufs)]
                n += 1
                P.dma("sp", [lambda e, wb=wb, m=m, half=half, q=q, pc=pc: e.dma_start(out=wb[:, q * 4:(q + 1) * 4, :], in_=adaw_d[m, half, :, q * 4:(q + 1) * 4, pc * wc:(pc + 1) * wc]) for q in range(2)],
                      writes=[wb])
                for j in range(wc // 128):
                    jj = half * 4 + pc * (wc // 128) + j
                    for k in range(KC):
                        MM(P, ps[:, jj * nb:(jj + 1) * nb], wb[:, k, j * 128:(j + 1) * 128], cT[:, k, :], k == 0, k == KC - 1, [wb, cT], [ps])
        for b in range(nb):
            P.op("dve", lambda e, i=i, m=m, b=b: e.tensor_tensor(out=mod[:, i, :, b], in0=ps[:, b:KC * nb:nb], in1=adab[:, m, :], op=ALU.add),
                 reads=[ps, adab], writes=[mod])
    return mod


def emit_rmsnorm(P, xt, a_ap, s_ap, cbufs, h, sq, ones, ps_stat, rstd, T=TT, nk=KC, inv_n=1.0 / D, eps=NORM_EPS):
    P.op("act", lambda e: e.activation(out=sq[:, :, 0:T], in_=xt[:, :, 0:T], func=AF.Square), reads=[xt], writes=[sq])
    for k in range(nk):
        MM(P, ps_stat[:, 0:T], ones[:, :], sq[:, k, 0:T], k == 0, k == nk - 1, [ones, sq], [ps_stat])
    P.op("act", lambda e: e.activation(out=rstd[:, 0:T], in_=ps_stat[:, 0:T], func=AF.Sqrt, scale=inv_n, bias=eps_ap(P, eps)), reads=[ps_stat], writes=[rstd])
    P.op("dve", lambda e: e.reciprocal(out=rstd[:, 0:T], in_=rstd[:, 0:T]), reads=[rstd], writes=[rstd])
    for k in range(nk):
        P.op("dve", lambda e, k=k: e.tensor_tensor(out=sq[:, k, 0:T], in0=xt[:, k, 0:T], in1=rstd[:, 0:T], op=ALU.mult), reads=[xt, rstd], writes=[sq])
    for k in range(nk):
        P.op("act", lambda e, k=k: e.activation(out=h[:, k, 0:T], in_=sq[:, k, 0:T], func=AF.Identity, scale=a_ap(k), bias=s_ap(k)),
             reads=[sq] + list(cbufs), writes=[h])


_EPS = {}


def eps_ap(P, eps):
    return _EPS[(id(P), eps)][:, :]


def make_eps(P, eps):
    b = P.sb("eps%d" % len(_EPS), [128, 1])
    P.op("dve", lambda e: e.memset(b[:, :], eps), writes=[b])
    _EPS[(id(P), eps)] = b
    return b


def emit_ffn(P, ws, pieces, xt, h, act, sg, gate_fn, cbufs, psg, psu, pso):
    pg, pu, pd = pieces
    for j in range(FC):
        wg = ws.get(pg[j // 2])
        wu = ws.get(pu[j // 2])
        jo = (j % 2) * 128
        g_ps = psg[j % 2]
        u_ps = psu[j % 2]
        for k in range(KC):
            MM(P, g_ps[:, :], wg[:, k * 256 + jo:k * 256 + jo + 128], h[:, k, :], k == 0, k == KC - 1, [wg, h], [g_ps])
        for k in range(KC):
            MM(P, u_ps[:, :], wu[:, k * 256 + jo:k * 256 + jo + 128], h[:, k, :], k == 0, k == KC - 1, [wu, h], [u_ps])
        s_t = sg[j % 2]
        P.op("act", lambda e, s_t=s_t, g_ps=g_ps: e.activation(out=s_t[:, :], in_=g_ps[:, :], func=AF.Silu), reads=[g_ps], writes=[s_t])
        P.op("dve", lambda e, s_t=s_t, u_ps=u_ps, j=j: e.tensor_tensor(out=act[:, j, :], in0=s_t[:, :], in1=u_ps[:, :], op=ALU.mult),
             reads=[s_t, u_ps], writes=[act])
    for m in range(KC):
        o_ps = pso[m % 2]
        for j in range(FC):
            wd = ws.get(pd[2 * m + j // 11])
            jj = j % 11
            MM(P, o_ps[:, :], wd[:, jj * 128:(jj + 1) * 128], act[:, j, :], j == 0, j == FC - 1, [wd, act], [o_ps])
        P.op("dve", lambda e, m=m, o_ps=o_ps: e.scalar_tensor_tensor(out=xt[:, m, :], in0=o_ps[:, :], scalar=gate_fn(m), in1=xt[:, m, :], op0=ALU.mult, op1=ALU.add),
             reads=[o_ps, xt] + list(cbufs), writes=[xt])


def prep_ffn(wg, wu, wd):
    def gu(w):
        w = np.ascontiguousarray(w).reshape(KC, 128, FC // 2, 256)
        return np.ascontiguousarray(w.transpose(2, 1, 0, 3)).reshape(FC // 2, 128, KC * 256)
    wdp = np.ascontiguousarray(wd).reshape(2, 11, 128, KC, 128)
    wdp = np.ascontiguousarray(wdp.transpose(3, 0, 2, 1, 4)).reshape(2 * KC, 128, 11 * 128)
    return gu(wg), gu(wu), wdp


def prep_sq(w):
    w = np.ascontiguousarray(w).reshape(KC, 128, 4, 256)
    return np.ascontiguousarray(w.transpose(2, 1, 0, 3)).reshape(4, 128, KC * 256)


def prep_adaw(w):
    w = np.ascontiguousarray(w).reshape(KC, 128, 6, 2, 512)
    return np.ascontiguousarray(w.transpose(2, 3, 1, 0, 4))


def prep_vec(v):
    return np.ascontiguousarray(np.asarray(v).reshape(KC, 128).T)


def prep_adab(b):
    return np.ascontiguousarray(np.asarray(b).reshape(6, KC, 128).transpose(2, 0, 1))


def prep_c(c):
    nb = c.shape[0]
    return np.ascontiguousarray(np.asarray(c).reshape(nb, KC, 128).transpose(2, 1, 0))


def build_kb(NT, has_bias, conv=False):
    nc = bass.Bass("TRN2", target_bir_lowering=False)
    xT = nc.dram_tensor("xT", [D, NT], F32, kind="ExternalInput").ap()
    if not conv:
        yT = nc.dram_tensor("yT", [2 * (NT // TT), 128, 4 * TT], F32, kind="ExternalInput").ap()
    wo = nc.dram_tensor("wo", [4, 128, KC * 256], F32, kind="ExternalInput").ap()
    wg = nc.dram_tensor("wg", [FC // 2, 128, KC * 256], F32, kind="ExternalInput").ap()
    wu = nc.dram_tensor("wu", [FC // 2, 128, KC * 256], F32, kind="ExternalInput").ap()
    wd = nc.dram_tensor("wd", [2 * KC, 128, 11 * 128], F32, kind="ExternalInput").ap()
    cT = nc.dram_tensor("cT", [128, KC, 1], F32, kind="ExternalInput").ap()
    adaw = nc.dram_tensor("adaw", [6, 2, 128, KC, 512], F32, kind="ExternalInput").ap()
    adab = nc.dram_tensor("adab", [128, 6, KC], F32, kind="ExternalInput").ap()
    gf = nc.dram_tensor("gf", [128, KC], F32, kind="ExternalInput").ap()
    if has_bias:
        bo = nc.dram_tensor("bo", [128, KC], F32, kind="ExternalInput").ap()
    if conv:
        xh = nc.dram_tensor("xh", [D, 32], F32, kind="ExternalInput").ap()
        hon = nc.dram_tensor("hon", [128, 1], F32, kind="ExternalInput").ap()
        w1 = nc.dram_tensor("w1", [KC, 128, KC * 256], F32, kind="ExternalInput").ap()
        cvec = nc.dram_tensor("cvec", [128, 6, KC], F32, kind="ExternalInput").ap()
        wdw = nc.dram_tensor("wdw", [128, KC, 31], F32, kind="ExternalInput").ap()
    out = nc.dram_tensor("out", [D, NT], F32, kind="ExternalOutput").ap()
    from contextlib import ExitStack
    with ExitStack() as st:
        P = Prog(nc, st)
        make_eps(P, NORM_EPS)
        ones = P.sb("ones", [128, 128])
        P.op("dve", lambda e: e.memset(ones[:, :], 1.0), writes=[ones])
        ps_stat = P.ps("ps_stat", [128, 512])
        psg = [P.ps("psg%d" % i, [128, 512]) for i in range(2)]
        psu = [P.ps("psu%d" % i, [128, 512]) for i in range(2)]
        pso = [P.ps("pso%d" % i, [128, 512]) for i in range(2)]
        xts = [P.sb("xt%d" % i, [128, KC, TT]) for i in range(2)]
        sq = P.sb("sq", [128, KC, TT])
        h = P.sb("h", [128, KC, TT], F32R)
        act = P.sb("act", [128, FC, TT], F32R)
        sg = [P.sb("sg%d" % i, [128, TT]) for i in range(2)]
        rstd = P.sb("rstd", [128, TT])
        gfs = P.sb("gfs", [128, KC])
        P.dma("sp", [lambda e: e.dma_start(out=gfs[:, :], in_=gf)], writes=[gfs])
        if has_bias:
            bos = P.sb("bos", [128, KC])
            P.dma("sp", [lambda e: e.dma_start(out=bos[:, :], in_=bo)], writes=[bos])
        if conv:
            cv = P.sb("cvec_sb", [128, 6, KC])
            wdws = P.sb("wdw_sb", [128, KC, 31])
            hons = P.sb("hon_sb", [128, 1])
            P.dma("sp", [lambda e: e.dma_start(out=cv[:, :, :], in_=cvec)], writes=[cv])
            P.dma("sp", [lambda e: e.dma_start(out=wdws[:, :, :], in_=wdw)], writes=[wdws])
            P.dma("sp", [lambda e: e.dma_start(out=hons[:, :], in_=hon)], writes=[hons])
            u = P.sb("u", [128, KC, 32 + TT])
            acc = P.sb("acc", [128, KC, TT])
            accv = [Buf("accv%d" % m, acc.t) for m in range(KC)]
            mean = P.sb("mean", [128, TT])
            ps_stat2 = P.ps("ps_stat2", [128, 512])
        ms = [0, 1, 2, 3, 4, 5] if conv else [2, 3, 4, 5]
        mi = {m: i for i, m in enumerate(ms)}
        mod = emit_mod(P, cT, adaw, adab, ms, 1, ps_stat, xts)
        af = P.sb("af", [128, KC])
        P.op("dve", lambda e: e.scalar_tensor_tensor(out=af[:, :], in0=mod[:, mi[4], :, 0], scalar=1.0, in1=gfs[:, :], op0=ALU.add, op1=ALU.mult),
             reads=[mod, gfs], writes=[af])
        if conv:
            am = P.sb("am", [128, KC])
            P.op("dve", lambda e: e.scalar_tensor_tensor(out=am[:, :], in0=mod[:, mi[1], :, 0], scalar=1.0, in1=cv[:, 5, :], op0=ALU.add, op1=ALU.mult),
                 reads=[mod, cv], writes=[am])
        ws = WStream(P) if conv else WStream(P, nstage=4, nring=6, lookahead=3)
        ntile = NT // TT
        plan = []
        if conv:
            p1h = [ws.plan(w1[q], KC * 256) for q in range(KC)]
        for t in range(ntile):
            if conv:
                py = [ws.plan(w1[q], KC * 256) for q in range(KC)]
            else:
                py = [ws.plan(yT[2 * t + q], 4 * TT, dest=(h, h[:, q * 4:(q + 1) * 4, :])) for q in range(2)]
            po = [ws.plan(wo[q], KC * 256) for q in range(4)]
            pg, pu = [], []
            for q in range(FC // 2):
                pg.append(ws.plan(wg[q], KC * 256))
                pu.append(ws.plan(wu[q], KC * 256))
            pd = [ws.plan(wd[q], 11 * 128) for q in range(2 * KC)]
            plan.append((py, po, pg, pu, pd))

        def conv_front(xt, T, c0, p1):
            emit_rmsnorm(P, xt, lambda k: am[:, k:k + 1], lambda k: mod[:, mi[0], k, 0:1], [am, mod], h, sq, ones, ps_stat, rstd, T=T)
            for m in range(KC):
                w = ws.get(p1[m])
                a_ps = psg[m % 2]
                b_ps = psu[m % 2]
                for k in range(KC):
                    MM(P, a_ps[:, 0:T], w[:, k * 256:k * 256 + 128], h[:, k, 0:T], k == 0, k == KC - 1, [w, h], [a_ps])
                for k in range(KC):
                    MM(P, b_ps[:, 0:T], w[:, k * 256 + 128:k * 256 + 256], h[:, k, 0:T], k == 0, k == KC - 1, [w, h], [b_ps])
                s_t = sg[m % 2]
                P.op("act", lambda e, s_t=s_t, b_ps=b_ps, m=m: e.activation(out=s_t[:, 0:T], in_=b_ps[:, 0:T], func=AF.Sigmoid, bias=cv[:, 1, m:m + 1], scale=1.0),
                     reads=[b_ps, cv], writes=[s_t])
                P.op("dve", lambda e, s_t=s_t, a_ps=a_ps, m=m: e.scalar_tensor_tensor(out=u[:, m, c0:c0 + T], in0=a_ps[:, 0:T], scalar=cv[:, 0, m:m + 1], in1=s_t[:, 0:T], op0=ALU.add, op1=ALU.mult),
                     reads=[a_ps, s_t, cv], writes=[u])

        if conv:
            xt = xts[1]
            P.dma("sp", [lambda e, xt=xt, k=k: e.dma_start(out=xt[:, k, 0:32], in_=xh[k * 128:(k + 1) * 128, :]) for k in range(KC)], writes=[xt])
            conv_front(xt, 32, 0, p1h)
            P.op("dve", lambda e: e.tensor_scalar(out=u[:, :, 0:32], in0=u[:, :, 0:32], scalar1=hons[:, 0:1], scalar2=None, op0=ALU.mult), reads=[u, hons], writes=[u])
        for t in range(ntile):
            py, po, pg, pu, pd = plan[t]
            xt = xts[t % 2]
            P.dma("sp", [lambda e, xt=xt, t=t, k=k: e.dma_start(out=xt[:, k, :], in_=xT[k * 128:(k + 1) * 128, t * TT:(t + 1) * TT]) for k in range(KC)], writes=[xt])
            if conv:
                conv_front(xt, TT, 32, py)
                for m in range(KC):
                    P.op("dve", lambda e, m=m: e.tensor_scalar(out=acc[:, m, :], in0=u[:, m, 2:2 + TT], scalar1=wdws[:, m, 0:1], scalar2=cv[:, 2, m:m + 1], op0=ALU.mult, op1=ALU.add),
                         reads=[u, wdws, cv], writes=[accv[m]])
                for j in range(1, 31):
                    for m in range(KC):
                        P.op("dve", lambda e, m=m, j=j: e.scalar_tensor_tensor(out=acc[:, m, :], in0=u[:, m, 2 + j:2 + j + TT], scalar=wdws[:, m, j:j + 1], in1=acc[:, m, :], op0=ALU.mult, op1=ALU.add),
                             reads=[u, wdws, accv[m]], writes=[accv[m]])
                P.op("act", lambda e: e.copy(out=u[:, :, 0:32], in_=u[:, :, TT:TT + 32]), reads=[u], writes=[u])
                P.op("act", lambda e: e.activation(out=sq[:, :, :], in_=acc[:, :, :], func=AF.Square), reads=list(accv), writes=[sq])
                for k in range(KC):
                    MM(P, ps_stat[:, :], ones[:, :], acc[:, k, :], k == 0, k == KC - 1, [ones, accv[k]], [ps_stat])
                for k in range(KC):
                    MM(P, ps_stat2[:, :], ones[:, :], sq[:, k, :], k == 0, k == KC - 1, [ones, sq], [ps_stat2])
                P.op("act", lambda e: e.mul(out=mean[:, :], in_=ps_stat[:, :], mul=1.0 / D), reads=[ps_stat], writes=[mean])
                P.op("dve", lambda e: e.tensor_tensor(out=sg[0][:, :], in0=mean[:, :], in1=mean[:, :], op=ALU.mult), reads=[mean], writes=[sg[0]])
                P.op("dve", lambda e: e.scalar_tensor_tensor(out=rstd[:, :], in0=ps_stat2[:, :], scalar=1.0 / D, in1=sg[0][:, :], op0=ALU.mult, op1=ALU.subtract),
                     reads=[ps_stat2, sg[0]], writes=[rstd])
                P.op("act", lambda e: e.activation(out=rstd[:, :], in_=rstd[:, :], func=AF.Sqrt, scale=1.0, bias=eps_ap(P, NORM_EPS)), reads=[rstd], writes=[rstd])
                P.op("dve", lambda e: e.reciprocal(out=rstd[:, :], in_=rstd[:, :]), reads=[rstd], writes=[rstd])
                for m in range(KC):
                    P.op("dve", lambda e, m=m: e.tensor_tensor(out=acc[:, m, :], in0=acc[:, m, :], in1=mean[:, :], op=ALU.subtract), reads=[accv[m], mean], writes=[accv[m]])
                    P.op("dve", lambda e, m=m: e.tensor_tensor(out=acc[:, m, :], in0=acc[:, m, :], in1=rstd[:, :], op=ALU.mult), reads=[accv[m], rstd], writes=[accv[m]])
                    P.op("act", lambda e, m=m: e.activation(out=h[:, m, :], in_=acc[:, m, :], func=AF.Silu, scale=cv[:, 3, m:m + 1], bias=cv[:, 4, m:m + 1]),
                         reads=[accv[m], cv], writes=[h])
            else:
                ws.get(py[0])
                ws.get(py[1])
            for m in range(KC):
                w = ws.get(po[m // 2])
                mo = (m % 2) * 128
                o_ps = pso[m % 2]
                for k in range(KC):
                    MM(P, o_ps[:, :], w[:, k * 256 + mo:k * 256 + mo + 128], h[:, k, :], k == 0, k == KC - 1, [w, h], [o_ps])
                if has_bias:
                    P.op("act", lambda e, m=m, o_ps=o_ps: e.activation(out=sg[0][:, :], in_=o_ps[:, :], func=AF.Identity, bias=bos[:, m:m + 1], scale=1.0),
                         reads=[o_ps, bos], writes=[sg[0]])
                    P.op("dve", lambda e, m=m, xt=xt: e.scalar_tensor_tensor(out=xt[:, m, :], in0=sg[0][:, :], scalar=mod[:, mi[2], m, 0:1], in1=xt[:, m, :], op0=ALU.mult, op1=ALU.add),
                         reads=[sg[0], xt, mod], writes=[xt])
                else:
                    P.op("dve", lambda e, m=m, xt=xt, o_ps=o_ps: e.scalar_tensor_tensor(out=xt[:, m, :], in0=o_ps[:, :], scalar=mod[:, mi[2], m, 0:1], in1=xt[:, m, :], op0=ALU.mult, op1=ALU.add),
                         reads=[o_ps, xt, mod], writes=[xt])
            emit_rmsnorm(P, xt, lambda k: af[:, k:k + 1], lambda k: mod[:, mi[3], k, 0:1], [af, mod], h, sq, ones, ps_stat, rstd)
            emit_ffn(P, ws, (pg, pu, pd), xt, h, act, sg, lambda m: mod[:, mi[5], m, 0:1], [mod], psg, psu, pso)
            P.dma("sp", [lambda e, xt=xt, t=t, k=k: e.dma_start(out=out[k * 128:(k + 1) * 128, t * TT:(t + 1) * TT], in_=xt[:, k, :]) for k in range(KC)], reads=[xt], sem_buf=xt)
        stats = P.emit()
    return nc, stats


def prep_w1(w):
    w = np.ascontiguousarray(w).reshape(KC, 128, 2, KC, 128)
    return np.ascontiguousarray(w.transpose(3, 1, 0, 2, 4)).reshape(KC, 128, KC * 256)


def act_view(P, act):
    return act


def prep_y_pieces(yT, NT):
    nt = NT // TT
    y = np.ascontiguousarray(yT).reshape(2, 4, 128, nt, TT)
    return np.ascontiguousarray(y.transpose(3, 0, 2, 1, 4)).reshape(2 * nt, 128, 4 * TT)


ST = 2048
NU = 16
DIL = (1, 4, 16)


def ka_consts():
    inv_freq = np.power(np.float32(10000.0), -np.arange(0, 64, 2, dtype=np.float32) / np.float32(64)).astype(np.float32)
    invf = np.tile(inv_freq, 4).reshape(128, 1).astype(np.float32)
    blockones = np.zeros((128, 128), np.float32)
    blockones[:64, :64] = 1
    blockones[64:, 64:] = 1
    rotT = np.zeros((128, 128), np.float32)
    for hb in (0, 64):
        for e in range(32):
            rotT[hb + e + 32, hb + e] = -1.0
            rotT[hb + e, hb + e + 32] = 1.0
    ident = np.eye(128, dtype=np.float32)
    j = np.arange(128)[:, None]
    i = np.arange(128)[None, :]
    masks = np.concatenate([(j >= i), (j <= i)], 1).astype(np.float32)
    shiftM = np.zeros((128, 128), np.float32)
    shiftM[64 + np.arange(64), np.arange(64)] = 1.0
    return dict(invf=invf, blockones=blockones, rotT=rotT, ident=ident, masks=masks, shiftM=shiftM)


def build_ka(S, LV=9, SUB=9):
    nc = bass.Bass("TRN2", target_bir_lowering=False)
    xT = nc.dram_tensor("xT", [D, S], F32, kind="ExternalInput").ap()
    posb = nc.dram_tensor("posb", [128, S], I32, kind="ExternalInput").ap()
    cT = nc.dram_tensor("cT", [128, KC, 1], F32, kind="ExternalInput").ap()
    adaw = nc.dram_tensor("adaw", [6, 2, 128, KC, 512], F32, kind="ExternalInput").ap()
    adab = nc.dram_tensor("adab", [128, 6, KC], F32, kind="ExternalInput").ap()
    gm = nc.dram_tensor("gm", [128, KC], F32, kind="ExternalInput").ap()
    wq = nc.dram_tensor("wq", [2, 9, 128, KC * 128], F32, kind="ExternalInput").ap()
    gains = nc.dram_tensor("gains", [128, 2], F32, kind="ExternalInput").ap()
    invf_d = nc.dram_tensor("invf", [128, 1], F32, kind="ExternalInput").ap()
    bo_d = nc.dram_tensor("blockones", [128, 128], F32, kind="ExternalInput").ap()
    rot_d = nc.dram_tensor("rotT", [128, 128], F32, kind="ExternalInput").ap()
    id_d = nc.dram_tensor("ident", [128, 128], F32, kind="ExternalInput").ap()
    mk_d = nc.dram_tensor("masks", [128, 256], F32, kind="ExternalInput").ap()
    sh_d = nc.dram_tensor("shiftM", [128, 128], F32, kind="ExternalInput").ap()
    oT = nc.dram_tensor("oT", [2, 2, 64, S], F32, kind="ExternalOutput").ap()
    from contextlib import ExitStack
    TWO_PI = 2.0 * np.pi
    C1 = 6.28125
    C2 = float(np.float32(TWO_PI - C1)) if False else 0.0019350051879882812
    C3 = float(TWO_PI - C1 - C2)
    with ExitStack() as st:
        P = Prog(nc, st)
        make_eps(P, NORM_EPS)

        def const(name, d, shape, dt=F32):
            b = P.sb(name + "_sb", shape, dt)
            P.dma("sp", [lambda e: e.dma_start(out=b[tuple(slice(None) for _ in shape)], in_=d)], writes=[b])
            return b
        invf = const("invf", invf_d, [128, 1])
        bones = const("bones", bo_d, [128, 128])
        rotT = const("rotT", rot_d, [128, 128])
        ident = const("ident", id_d, [128, 128])
        masks = const("masks", mk_d, [128, 256])
        shiftM = const("shiftM", sh_d, [128, 128])
        gms = const("gms", gm, [128, KC])
        gn = const("gn", gains, [128, 2])
        zt = P.sb("zt", [128, 512])
        P.op("dve", lambda e: e.memset(zt[:, :], 0.0), writes=[zt])
        ones = P.sb("ones", [128, 128], F32R)
        onesf = P.sb("onesf", [128, 128])
        P.op("dve", lambda e: e.memset(onesf[:, :], 1.0), writes=[onesf])
        P.op("dve", lambda e: e.tensor_copy(out=ones[:, :], in_=onesf[:, :]), reads=[onesf], writes=[ones])
        psp = [P.ps("psp%d" % i, [128, 512]) for i in range(2)]
        ps_stat = P.ps("ps_stat", [128, 512])
        pss = [P.ps("pss%d" % i, [128, 512]) for i in range(2)]
        psoh = [P.ps("pso%d" % i, [128, 512]) for i in range(2)]
        psm = P.ps("psm", [128, 512])
        ps_rot = psm
        xt = P.sb("xt", [128, KC, TT])
        h = P.sb("h", [128, KC, TT], F32R)
        W = P.sb("W", [128, 9, KC * 128], F32R)
        rstd = P.sb("rstd", [128, TT])
        posi = P.sb("posi", [128, TT], I32)
        ang = P.sb("ang", [128, TT])
        kf = P.sb("kf", [128, TT])
        ki = posi
        cosT = P.sb("cosT", [128, TT])
        sinT = P.sb("sinT", [128, TT])
        tg = P.sb("tg", [128, TT])
        ta = P.sb("ta", [128, TT])
        tb = kf
        qb = [[P.sb("q%d_%d" % (g, hh), [128, (4, 4, NU)[g], 128], F32R) for hh in range(2)] for g in range(3)]
        for g in range(3):
            for hh in range(2):
                for z4 in range((4, 4, NU)[g] // 4):
                    P.op("dve", lambda e, g=g, hh=hh, z4=z4: e.tensor_copy(out=qb[g][hh][:, z4 * 4:z4 * 4 + 4, :].rearrange("p u n -> p (u n)"), in_=zt[:, :]), reads=[zt], writes=[qb[g][hh]])
        nslot = (5, 8, 2 * NU)
        kb = [P.sb("k%d" % g, [128, nslot[g], 128], F32R) for g in range(3)]
        vb = [P.sb("v%d" % g, [128, nslot[g], 2, 128], BF16) for g in range(3)]
        vT = [P.sb("vT0", [128, TT]), P.sb("vT1", [128, TT]), P.sb("vT2", [128, ST])]
        accs = [P.sb("acc%d" % hh, [128, ST]) for hh in range(2)]
        et = [P.sb("et%d" % i, [128, 512]) for i in range(2)]
        pt = [P.sb("pt%d" % i, [128, 512], BF16) for i in range(2)]
        masks2 = P.sb("masks2", [128, 512])
        P.dma("sp", [lambda e, q=q: e.dma_start(out=masks2[:, q * 256:(q + 1) * 256], in_=mk_d) for q in range(2)], writes=[masks2])
        rden = P.sb("rden", [64, TT])
        for g in range(3):
            P.op("dve", lambda e, g=g: e.memset(vb[g][:, :, :, 64:128], 1.0), writes=[vb[g]])
        mod = emit_mod(P, cT, adaw, adab, [0, 1], 1, ps_stat, [xt])
        am = P.sb("am", [128, KC])
        P.op("dve", lambda e: e.scalar_tensor_tensor(out=am[:, :], in0=mod[:, 1, :, 0], scalar=1.0, in1=gms[:, :], op0=ALU.add, op1=ALU.mult),
             reads=[mod, gms], writes=[am])
        nst = S // ST
        cnt = {"pp": 0, "ss": 0, "ob": 0, "qk": 0}
        qk_sets = [(ta, tg, tb, rstd), (P.sb("ta2", [128, TT]), P.sb("tg2", [128, TT]), P.sb("tb2", [128, TT]), P.sb("rstd2", [128, TT]))]

        def attend(g, uq, cs, pv, colsf, first_group):
            sps = pss[cnt["ss"] % 2]
            e_t = et[cnt["ss"] % 2]
            p_t = pt[cnt["ss"] % 2]
            cnt["ss"] += 1
            kprev = pv if pv is not None else cs
            for hh in range(2):
                base = hh * 256
                MM(P, sps[:, base:base + 128], kb[g][:, kprev, :], qb[g][hh][:, uq, :], True, True, [kb[g], qb[g][hh]], [sps])
                MM(P, sps[:, base + 128:base + 256], kb[g][:, cs, :], qb[g][hh][:, uq, :], True, True, [kb[g], qb[g][hh]], [sps])
            P.op("act", lambda e, sps=sps, e_t=e_t: e.activation(out=e_t[:, :], in_=sps[:, :], func=AF.Exp, scale=0.125), reads=[sps], writes=[e_t])
            P.op("dve", lambda e, e_t=e_t, p_t=p_t: e.tensor_tensor(out=p_t[:, :], in0=e_t[:, :], in1=masks2[:, :], op=ALU.mult),
                 reads=[e_t, masks2], writes=[p_t])
            for hh in range(2):
                base = hh * 256
                pb_ = psoh[hh]
                oc = pb_[:, 0:128]
                if pv is not None:
                    MM(P, oc, vb[g][:, pv, hh, :], p_t[:, base:base + 128], True, False, [vb[g], p_t], [pb_])
                MM(P, oc, vb[g][:, cs, hh, :], p_t[:, base + 128:base + 256], pv is None, True, [vb[g], p_t], [pb_])
                if first_group:
                    P.op("dve", lambda e, oc=oc, o=colsf(accs[hh]): e.tensor_copy(out=o, in_=oc), reads=[pb_], writes=[accs[hh]])
                else:
                    P.op("dve", lambda e, oc=oc, o=colsf(accs[hh]): e.tensor_tensor(out=o, in0=oc, in1=o, op=ALU.add), reads=[pb_, accs[hh]], writes=[accs[hh]])

        for hp in range(2):
            for c0 in range(0, 9, 4):
                n = min(4, 9 - c0)
                P.dma("sp", [lambda e, hp=hp, c=c: e.dma_start(out=xt[:, 2 * (c % 4):2 * (c % 4) + 2, :], in_=wq[hp, c].rearrange("p (a b) -> p a b", a=2)) for c in range(c0, c0 + n)], writes=[xt])
                P.op("act", lambda e, c0=c0, n=n: e.copy(out=W[:, c0:c0 + n, :], in_=xt[:, 0:2 * n, :].rearrange("p (c a) b -> p c (a b)", a=2)), reads=[xt], writes=[W])
            for sti in range(nst):
                par = sti % 2
                cur0 = (0, 0, par * NU)
                for sub in range(ST // TT):
                    t0 = sti * ST + sub * TT
                    P.dma("sp", [lambda e, t0=t0, k=k: e.dma_start(out=xt[:, k, :], in_=xT[k * 128:(k + 1) * 128, t0:t0 + TT]) for k in range(KC)], writes=[xt])
                    P.dma("sp", [lambda e, t0=t0: e.dma_start(out=posi[:, :], in_=posb[:, t0:t0 + TT])], writes=[posi])
                    P.op("act", lambda e: e.activation(out=h[:, :, :], in_=xt[:, :, :], func=AF.Square), reads=[xt], writes=[h])
                    for k in range(KC):
                        MM(P, ps_stat[:, :], ones[:, :], h[:, k, :], k == 0, k == KC - 1, [ones, h], [ps_stat])
                    P.op("act", lambda e: e.activation(out=rstd[:, :], in_=ps_stat[:, :], func=AF.Sqrt, scale=1.0 / D, bias=eps_ap(P, NORM_EPS)), reads=[ps_stat], writes=[rstd])
                    P.op("dve", lambda e: e.reciprocal(out=rstd[:, :], in_=rstd[:, :]), reads=[rstd], writes=[rstd])
                    for k in range(KC):
                        P.op("dve", lambda e, k=k: e.tensor_tensor(out=xt[:, k, :], in0=xt[:, k, :], in1=rstd[:, :], op=ALU.mult), reads=[xt, rstd], writes=[xt])
                    for k in range(KC):
                        P.op("act", lambda e, k=k: e.activation(out=h[:, k, :], in_=xt[:, k, :], func=AF.Identity, scale=am[:, k:k + 1], bias=mod[:, 0, k, 0:1]),
                             reads=[xt, am, mod], writes=[h])
                    if LV < 2:
                        continue
                    P.op("dve", lambda e: e.tensor_copy(out=ang[:, :], in_=posi[:, :]), reads=[posi], writes=[ang])
                    P.op("dve", lambda e: e.tensor_scalar(out=ang[:, :], in0=ang[:, :], scalar1=invf[:, 0:1], scalar2=None, op0=ALU.mult), reads=[ang, invf], writes=[ang])
                    P.op("dve", lambda e: e.tensor_scalar(out=ki[:, :], in0=ang[:, :], scalar1=float(1.0 / TWO_PI), scalar2=None, op0=ALU.mult), reads=[ang], writes=[ki])
                    P.op("dve", lambda e: e.tensor_copy(out=kf[:, :], in_=ki[:, :]), reads=[ki], writes=[kf])
                    P.op("dve", lambda e: e.scalar_tensor_tensor(out=ang[:, :], in0=kf[:, :], scalar=-C1, in1=ang[:, :], op0=ALU.mult, op1=ALU.add), reads=[kf, ang], writes=[ang])
                    P.op("dve", lambda e: e.scalar_tensor_tensor(out=ang[:, :], in0=kf[:, :], scalar=-C2, in1=ang[:, :], op0=ALU.mult, op1=ALU.add), reads=[kf, ang], writes=[ang])
                    P.op("dve", lambda e: e.scalar_tensor_tensor(out=ang[:, :], in0=kf[:, :], scalar=-C3, in1=ang[:, :], op0=ALU.mult, op1=ALU.add), reads=[kf, ang], writes=[ang])
                    emit_wrap_sin(P, sinT, ang, 0.0, kf, ta)
                    emit_wrap_sin(P, cosT, ang, float(np.pi / 2), kf, ta)
                    if LV < 3:
                        continue
                    for g in range(3):
                        r = DIL[g]
                        for j in range(3):
                            cc = g * 3 + j
                            ps = psp[cnt["pp"] % 2]
                            cnt["pp"] += 1
                            for k in range(KC):
                                MM(P, ps[:, :], W[:, cc, k * 128:(k + 1) * 128], h[:, k, :], k == 0, k == KC - 1, [W, h], [ps])
                            if g == 0:
                                u0, nn = sub * 4, 4

                                def dst(buf, base):
                                    return buf[:, 0:4, :].rearrange("p u n -> p (u n)")

                                def srcv(a):
                                    return a
                            elif g == 1:
                                def dst(buf, base):
                                    return buf[:, 0:4, :]

                                def srcv(a):
                                    return a.rearrange("p (n r) -> p r n", r=4)
                            else:
                                def dst(buf, base):
                                    return buf[:, base:base + 16, sub * 32:(sub + 1) * 32]

                                def srcv(a):
                                    return a.rearrange("p (n r) -> p r n", r=16)
                            if j == 2:
                                if g < 2:
                                    P.op("act", lambda e, ps=ps, g=g: e.copy(out=vT[g][:, :], in_=ps[:, :]), reads=[ps], writes=[vT[g]])
                                else:
                                    P.op("act", lambda e, ps=ps, sub=sub: e.copy(out=vT[2][:, sub * TT:(sub + 1) * TT], in_=ps[:, :]), reads=[ps], writes=[vT[2]])
                                continue
                            if SUB < 1:
                                continue
                            tset = qk_sets[cnt["qk"] % 2]
                            cnt["qk"] += 1
                            q_ta, q_tg, q_tb, q_rstd = tset
                            P.op("act", lambda e, q_ta=q_ta, ps=ps: e.activation(out=q_ta[:, :], in_=ps[:, :], func=AF.Square), reads=[ps], writes=[q_ta])
                            MM(P, ps_stat[:, :], bones[:, :], q_ta[:, :], True, True, [bones, q_ta], [ps_stat])
                            P.op("dve", lambda e, q_ta=q_ta, q_tg=q_tg, ps=ps, j=j: e.tensor_scalar(out=q_tg[:, :], in0=ps[:, :], scalar1=gn[:, j:j + 1], scalar2=None, op0=ALU.mult), reads=[ps, gn, q_ta], writes=[q_tg])
                            MM(P, ps_rot[:, :], rotT[:, :], q_tg[:, :], True, True, [rotT, q_tg], [ps_rot])
                            P.op("act", lambda e, q_rstd=q_rstd: e.activation(out=q_rstd[:, :], in_=ps_stat[:, :], func=AF.Sqrt, scale=1.0 / 64, bias=eps_ap(P, NORM_EPS)), reads=[ps_stat], writes=[q_rstd])
                            P.op("dve", lambda e, q_rstd=q_rstd: e.reciprocal(out=q_rstd[:, :], in_=q_rstd[:, :]), reads=[q_rstd], writes=[q_rstd])
                            if SUB < 2:
                                continue
                            P.op("dve", lambda e, q_tg=q_tg: e.tensor_tensor(out=q_tg[:, :], in0=q_tg[:, :], in1=cosT[:, :], op=ALU.mult), reads=[q_tg, cosT], writes=[q_tg])
                            P.op("dve", lambda e, q_tb=q_tb: e.tensor_tensor(out=q_tb[:, :], in0=ps_rot[:, :], in1=sinT[:, :], op=ALU.mult), reads=[ps_rot, sinT], writes=[q_tb])
                            P.op("dve", lambda e, q_tg=q_tg, q_tb=q_tb: e.tensor_tensor(out=q_tg[:, :], in0=q_tg[:, :], in1=q_tb[:, :], op=ALU.add), reads=[q_tg, q_tb], writes=[q_tg])
                            if SUB < 3:
                                continue
                            if j == 1:
                                buf, base = kb[g], cur0[g]
                                P.op("dve", lambda e, q_tg=q_tg, q_rstd=q_rstd, o=dst(buf, base), a=srcv(q_tg[:, :]), b=srcv(q_rstd[:, :]): e.tensor_tensor(out=o, in0=a, in1=b, op=ALU.mult),
                                     reads=[q_tg, q_rstd], writes=[buf])
                            else:
                                for hh in range(2):
                                    buf = qb[g][hh]
                                    P.op("dve", lambda e, q_tg=q_tg, q_rstd=q_rstd, o=dst(buf, 0)[hh * 64:(hh + 1) * 64], a=srcv(q_tg[hh * 64:(hh + 1) * 64, :]), b=srcv(q_rstd[hh * 64:(hh + 1) * 64, :]): e.tensor_tensor(out=o, in0=a, in1=b, op=ALU.mult),
                                         reads=[q_tg, q_rstd], writes=[buf])
                    if LV < 4:
                        continue
                    for g in range(2):
                        for uu in range(4):
                            src = vT[g][:, uu * 128:(uu + 1) * 128] if g == 0 else vT[g][:, uu:TT:4]
                            P.op("pe", lambda e, uu=uu, src=src: e.transpose(out=psm[:, uu * 128:(uu + 1) * 128], in_=src, identity=ident[:, :]), reads=[vT[g], ident], writes=[psm])
                        P.op("act", lambda e, g=g, sub=sub: e.copy(out=vb[g][:, 0:4, :, 0:64], in_=psm[:, :].rearrange("p (u h f) -> p u h f", u=4, h=2)),
                             reads=[psm], writes=[vb[g]])
                    if LV < 5:
                        continue
                    first = (sti == 0 and sub == 0)
                    for u in range(4):
                        attend(0, u, u, (u - 1) if u > 0 else (None if first else 4), lambda a, u=u, sub=sub: a[:, (sub * 4 + u) * 128:(sub * 4 + u + 1) * 128], True)
                    for u in range(4):
                        attend(1, u, u, None if first else 4 + u, lambda a, u=u, sub=sub: a[:, sub * 512 + u:(sub + 1) * 512:4], False)
                    P.op("act", lambda e: e.copy(out=kb[0][:, 4, :], in_=kb[0][:, 3, :]), reads=[kb[0]], writes=[kb[0]])
                    P.op("act", lambda e: e.copy(out=vb[0][:, 4, :, :], in_=vb[0][:, 3, :, :]), reads=[vb[0]], writes=[vb[0]])
                    P.op("act", lambda e: e.copy(out=kb[1][:, 4:8, :], in_=kb[1][:, 0:4, :]), reads=[kb[1]], writes=[kb[1]])
                    P.op("act", lambda e: e.copy(out=vb[1][:, 4:8, :, :], in_=vb[1][:, 0:4, :, :]), reads=[vb[1]], writes=[vb[1]])
                if LV < 6:
                    continue
                for u4 in range(4):
                    for uu in range(4):
                        rho = u4 * 4 + uu
                        P.op("pe", lambda e, uu=uu, rho=rho: e.transpose(out=psm[:, uu * 128:(uu + 1) * 128], in_=vT[2][:, rho:ST:16], identity=ident[:, :]), reads=[vT[2], ident], writes=[psm])
                    P.op("act", lambda e, u4=u4, b0=cur0[2]: e.copy(out=vb[2][:, b0 + u4 * 4:b0 + u4 * 4 + 4, :, 0:64], in_=psm[:, :].rearrange("p (u h f) -> p u h f", u=4, h=2)),
                         reads=[psm], writes=[vb[2]])
                for u in range(NU):
                    attend(2, u, par * NU + u, ((1 - par) * NU + u) if sti > 0 else None, lambda a, u=u: a[:, u:ST:16], False)
                if LV < 7:
                    continue
                for hh in range(2):
                    for c4 in range(ST // TT):
                        MM(P, psm[:, :], shiftM[:, :], accs[hh][:, c4 * TT:(c4 + 1) * TT], True, True, [shiftM, accs[hh]], [psm])
                        P.op("dve", lambda e: e.reciprocal(out=rden[:, :], in_=psm[0:64, :]), reads=[psm], writes=[rden])
                        P.op("dve", lambda e, hh=hh, c4=c4: e.tensor_tensor(out=accs[hh][0:64, c4 * TT:(c4 + 1) * TT], in0=accs[hh][0:64, c4 * TT:(c4 + 1) * TT], in1=rden[:, :], op=ALU.mult),
                             reads=[accs[hh], rden], writes=[accs[hh]])
                        tcol = sti * ST + c4 * TT
                        P.dma("sp", [lambda e, hp=hp, hh=hh, tcol=tcol, c4=c4: e.dma_start(out=oT[hp, hh, :, tcol:tcol + TT], in_=accs[hh][0:64, c4 * TT:(c4 + 1) * TT])], reads=[accs[hh]], sem_buf=accs[hh])
        stats = P.emit()
    return nc, stats


def emit_wrap_sin(P, out, ang, shift, t1, t2):
    P.op("dve", lambda e: e.tensor_scalar(out=t1[:, :], in0=ang[:, :], scalar1=float(shift), scalar2=float(np.pi), op0=ALU.add, op1=ALU.is_gt), reads=[ang], writes=[t1])
    P.op("dve", lambda e: e.tensor_scalar(out=t2[:, :], in0=ang[:, :], scalar1=float(shift), scalar2=None, op0=ALU.add), reads=[ang], writes=[t2])
    P.op("dve", lambda e: e.scalar_tensor_tensor(out=t2[:, :], in0=t1[:, :], scalar=float(-2.0 * np.pi), in1=t2[:, :], op0=ALU.mult, op1=ALU.add), reads=[t1, t2], writes=[t2])
    P.op("dve", lambda e: e.tensor_scalar(out=t2[:, :], in0=t2[:, :], scalar1=float(-3.1415925), scalar2=float(3.1415925), op0=ALU.max, op1=ALU.min), reads=[t2], writes=[t2])
    P.op("act", lambda e: e.activation(out=out[:, :], in_=t2[:, :], func=AF.Sin), reads=[t2], writes=[out])


def prep_wq(w_qkv, bq, hp):
    s0 = 4 * bq + 2 * hp
    out = np.empty((9, 128, KC * 128), np.float32)
    for cc in range(9):
        cols = w_qkv[:, cc * 1024 + s0 * 64:cc * 1024 + s0 * 64 + 128]
        out[cc] = cols.reshape(KC, 128, 128).transpose(1, 0, 2).reshape(128, KC * 128)
    return out


TR = 256
CH = 64
NCH = TR // CH
C0 = float(np.exp(-0.5))
GN_EPS = 64e-5
QA, QR, QB, QK, QBH, QKH, QV, QP = range(8)


def kr_consts():
    identf = np.eye(128, dtype=np.float32)
    blockones = np.zeros((128, 128), np.float32)
    blockones[:64, :64] = 1
    blockones[64:, 64:] = 1
    s = np.arange(128)[:, None]
    t = np.arange(128)[None, :]
    su = (s < t).astype(np.float32)
    iu = (s <= t).astype(np.float32)
    masku4 = np.concatenate([su, iu, su, iu], 1)
    maskl = (s > t).astype(np.float32)
    resetm = np.ones((128, TR), np.float32)
    resetm[:, ::CH] = 0.0
    return dict(identf=identf, blockones=blockones, masku4=masku4, maskl=maskl, resetm=resetm)


def build_kr(S):
    nc = bass.Bass("TRN2", target_bir_lowering=False)
    xT = nc.dram_tensor("xT", [D, S], F32, kind="ExternalInput").ap()
    cT = nc.dram_tensor("cT", [128, KC, 1], F32, kind="ExternalInput").ap()
    adaw = nc.dram_tensor("adaw", [6, 2, 128, KC, 512], F32, kind="ExternalInput").ap()
    adab = nc.dram_tensor("adab", [128, 6, KC], F32, kind="ExternalInput").ap()
    gm = nc.dram_tensor("gm", [128, KC], F32, kind="ExternalInput").ap()
    wcat = nc.dram_tensor("wcat", [8, 128, KC * 128], F32, kind="ExternalInput").ap()
    mus_d = nc.dram_tensor("mus", [128, 6, KC], F32, kind="ExternalInput").ap()
    b2_d = nc.dram_tensor("b2", [3, 128, 256], F32, kind="ExternalInput").ap()
    vec_d = nc.dram_tensor("vec", [128, 2, 8], F32, kind="ExternalInput").ap()
    lg_d = nc.dram_tensor("lg8", [2, 128, NCH * 128], F32, kind="ExternalInput").ap()
    lb_d = nc.dram_tensor("lb8", [2, 128, NCH * 128], F32, kind="ExternalInput").ap()
    id_d = nc.dram_tensor("identf", [128, 128], F32, kind="ExternalInput").ap()
    bo_d = nc.dram_tensor("blockones", [128, 128], F32, kind="ExternalInput").ap()
    mu4_d = nc.dram_tensor("masku4", [128, 512], F32, kind="ExternalInput").ap()
    ml_d = nc.dram_tensor("maskl", [128, 128], F32, kind="ExternalInput").ap()
    rs_d = nc.dram_tensor("resetm", [128, TR], F32, kind="ExternalInput").ap()
    yo = nc.dram_tensor("yo", [2, 2, S, 64], F32, kind="ExternalOutput").ap()
    from contextlib import ExitStack
    with ExitStack() as st:
        P = Prog(nc, st)
        make_eps(P, NORM_EPS)
        make_eps(P, GN_EPS)

        def const(name, d, shape, dt=F32):
            b = P.sb(name + "_sb", shape, dt)
            P.dma("sp", [lambda e: e.dma_start(out=b[tuple(slice(None) for _ in shape)], in_=d)], writes=[b])
            return b
        identf = const("identf", id_d, [128, 128])
        bones = const("bones", bo_d, [128, 128])
        masku4 = const("masku4", mu4_d, [128, 512])
        maskl = const("maskl", ml_d, [128, 128])
        resetm = const("resetm", rs_d, [128, TR])
        gms = const("gms", gm, [128, KC])
        mus = const("mus", mus_d, [128, 6, KC])
        vec = const("vec", vec_d, [128, 2, 8])
        lg8 = [const("lg8_%d" % g, lg_d[g], [128, NCH * 128]) for g in range(2)]
        lb8 = [const("lb8_%d" % g, lb_d[g], [128, NCH * 128]) for g in range(2)]
        b2f = P.sb("b2f", [128, 3, 256])
        P.dma("sp", [lambda e, i=i: e.dma_start(out=b2f[:, i, :], in_=b2_d[i]) for i in range(3)], writes=[b2f])
        b2r = P.sb("b2r", [128, 3, 256], F32R)
        P.op("act", lambda e: e.copy(out=b2r[:, :, :], in_=b2f[:, :, :]), reads=[b2f], writes=[b2r])
        identR = P.sb("identR", [128, 128], F32R)
        P.op("act", lambda e: e.copy(out=identR[:, :], in_=identf[:, :]), reads=[identf], writes=[identR])
        onesf = P.sb("onesf", [128, 128])
        P.op("dve", lambda e: e.memset(onesf[:, :], 1.0), writes=[onesf])
        onesR = P.sb("onesR", [128, 128], F32R)
        P.op("dve", lambda e: e.tensor_copy(out=onesR[:, :], in_=onesf[:, :]), reads=[onesf], writes=[onesR])
        onem = P.sb("onem", [128, 6, KC])
        P.op("dve", lambda e: e.tensor_scalar(out=onem[:, :, :], in0=mus[:, :, :], scalar1=-1.0, scalar2=1.0, op0=ALU.mult, op1=ALU.add), reads=[mus], writes=[onem])
        pp = [P.ps("pp%d" % i, [128, 512]) for i in range(2)]
        ps_stat = P.ps("ps_stat", [128, 512])
        bA = [P.ps("bA%d" % i, [128, 512]) for i in range(2)]
        bX = [P.ps("bX%d" % i, [128, 512]) for i in range(2)]
        psT = P.ps("psT", [128, 512])
        xt = P.sb("xt", [128, KC, TR])
        h = P.sb("h", [128, KC, TR], F32R)
        hp = P.sb("hp", [128, KC, TR], F32R)
        Wa = P.sb("Wa", [128, 8, KC * 128], F32R)
        Wb = P.sb("Wb", [128, 8, KC * 128], F32R)
        rstd = P.sb("rstd", [128, TR])
        rT = [P.sb("rT%d" % g, [128, TR]) for g in range(2)]
        kT = [P.sb("kT%d" % g, [128, TR]) for g in range(2)]
        vT = [P.sb("vT%d" % g, [128, TR]) for g in range(2)]
        L1 = P.sb("L1", [128, TR], F32R)
        sgd = P.sb("sgd", [128, NCH, 2, CH], F32R)
        tmpn = ["sgw", "aT", "kk", "kkn", "kp", "ka", "cum", "gam", "gami", "game", "rev", "t1", "t2"]
        T_ = {n: P.sb("tm_" + n, [128, TR]) for n in tmpn}
        SRC = [P.sb("SRC%d" % g, [128, 8, TR]) for g in range(2)]
        gamc = [P.sb("gamc%d" % g, [128, NCH]) for g in range(2)]
        EXP = [P.sb("EXP%d" % g, [128, 8, 2, CH], F32R) for g in range(2)]
        NKM = [P.sb("NKM%d" % g, [128, 512], F32R) for g in range(2)]
        L0 = [P.sb("L0_%d" % g, [128, 128], F32R) for g in range(2)]
        XB = [P.sb("XB%d" % g, [128, 5, 128], F32R) for g in range(2)]
        NL = [[P.sb("NL%d_%d" % (g, i), [128, 256], F32R) for i in range(2)] for g in range(2)]
        PT = [P.sb("PT%d" % g, [128, 128]) for g in range(2)]
        Qc = [P.sb("Qc%d" % g, [128, 128]) for g in range(2)]
        OmT = [P.sb("OmT%d" % g, [128, 128]) for g in range(2)]
        Y0 = [P.sb("Y0_%d" % g, [128, 128]) for g in range(2)]
        coef = [P.sb("coef%d" % g, [128, NCH, 2]) for g in range(2)]
        Z = [[P.sb("Z%d_%d" % (g, i), [128, 128]) for i in range(2)] for g in range(2)]
        y8 = [P.sb("y8_%d" % g, [128, NCH, 128]) for g in range(2)]
        V8 = [P.sb("V8_%d" % g, [128, NCH, 128]) for g in range(2)]
        st8 = [P.sb("st8_%d" % g, [128, 4, NCH]) for g in range(2)]
        zt = P.sb("zt", [128, 1024])
        P.op("dve", lambda e: e.memset(zt[:, :], 0.0), writes=[zt])
        for g in range(2):
            P.op("dve", lambda e, g=g: e.memset(Z[g][0][:, :], 0.0), writes=[Z[g][0]])
            P.op("dve", lambda e, g=g: e.tensor_copy(out=EXP[g][:, :, :, :].rearrange("p q h c -> p (q h c)"), in_=zt[:, :]), reads=[zt], writes=[EXP[g]])
        P.op("dve", lambda e: e.tensor_copy(out=hp[:, :, 0:1], in_=zt[:, 0:KC].rearrange("p (k o) -> p k o", o=1)), reads=[zt], writes=[hp])
        mod = emit_mod(P, cT, adaw, adab, [0, 1], 1, ps_stat, [SRC[0]])
        am = P.sb("am", [128, KC])
        P.op("dve", lambda e: e.scalar_tensor_tensor(out=am[:, :], in0=mod[:, 1, :, 0], scalar=1.0, in1=gms[:, :], op0=ALU.add, op1=ALU.mult),
             reads=[mod, gms], writes=[am])
        MOF = {0: 0, 1: 0, 2: 2, 3: 2, 4: 3, 5: 3, 7: 5}
        for c0 in range(0, 8, 2):
            P.dma("sp", [lambda e, c=c: e.dma_start(out=xt[:, 4 * (c % 2):4 * (c % 2) + 4, :], in_=wcat[c].rearrange("p (a b) -> p a b", a=4)) for c in (c0, c0 + 1)], writes=[xt])
            for c in (c0, c0 + 1):
                src = xt[:, 4 * (c % 2):4 * (c % 2) + 4, :].rearrange("p a b -> p (a b)")
                for k in range(KC):
                    segs = [(0, 128, MOF[c])] if c != 6 else [(0, 64, 1), (64, 128, 4)]
                    for (lo, hi, m) in segs:
                        P.op("dve", lambda e, c=c, k=k, lo=lo, hi=hi, m=m, src=src: e.tensor_scalar(out=Wa[:, c, k * 128 + lo:k * 128 + hi], in0=src[:, k * 128 + lo:k * 128 + hi], scalar1=onem[:, m, k:k + 1], scalar2=None, op0=ALU.mult),
                             reads=[xt, onem], writes=[Wa])
                        P.op("act", lambda e, c=c, k=k, lo=lo, hi=hi, m=m, src=src: e.activation(out=Wb[:, c, k * 128 + lo:k * 128 + hi], in_=src[:, k * 128 + lo:k * 128 + hi], func=AF.Identity, scale=mus[:, m, k:k + 1]),
                             reads=[xt, mus], writes=[Wb])
        ntile = S // TR
        zpar = [0, 0]
        cnt = {"pp": 0}

        def nextpp():
            b = pp[cnt["pp"] % 2]
            cnt["pp"] += 1
            return b
        for ti in range(ntile):
            t0 = ti * TR
            P.dma("sp", [lambda e, t0=t0, k=k: e.dma_start(out=xt[:, k, :], in_=xT[k * 128:(k + 1) * 128, t0:t0 + TR]) for k in range(KC)], writes=[xt])
            P.op("act", lambda e: e.activation(out=h[:, :, :], in_=xt[:, :, :], func=AF.Square), reads=[xt], writes=[h])
            for k in range(KC):
                MM(P, ps_stat[:, 0:TR], onesR[:, :], h[:, k, :], k == 0, k == KC - 1, [onesR, h], [ps_stat])
            P.op("act", lambda e: e.activation(out=rstd[:, :], in_=ps_stat[:, 0:TR], func=AF.Sqrt, scale=1.0 / D, bias=eps_ap(P, NORM_EPS)), reads=[ps_stat], writes=[rstd])
            P.op("dve", lambda e: e.reciprocal(out=rstd[:, :], in_=rstd[:, :]), reads=[rstd], writes=[rstd])
            for k in range(KC):
                P.op("dve", lambda e, k=k: e.tensor_tensor(out=xt[:, k, :], in0=xt[:, k, :], in1=rstd[:, :], op=ALU.mult), reads=[xt, rstd], writes=[xt])
            for k in range(KC):
                P.op("act", lambda e, k=k: e.activation(out=h[:, k, :], in_=xt[:, k, :], func=AF.Identity, scale=am[:, k:k + 1], bias=mod[:, 0, k, 0:1]),
                     reads=[xt, am, mod], writes=[h])
            P.op("act", lambda e: e.copy(out=hp[:, :, 1:TR], in_=h[:, :, 0:TR - 1]), reads=[h], writes=[hp])
            pso = {}
            for cc in range(8):
                ps = nextpp()
                for k in range(KC):
                    MM(P, ps[:, 0:TR], Wa[:, cc, k * 128:(k + 1) * 128], h[:, k, :], k == 0, False, [Wa, h], [ps])
                for k in range(KC):
                    MM(P, ps[:, 0:TR], Wb[:, cc, k * 128:(k + 1) * 128], hp[:, k, :], False, k == KC - 1, [Wb, hp], [ps])
                if cc < 6:
                    dstb = (rT, kT, vT)[cc // 2][cc % 2]
                    P.op("act", lambda e, ps=ps, dstb=dstb: e.copy(out=dstb[:, :], in_=ps[:, 0:TR]), reads=[ps], writes=[dstb])
                elif cc == 6:
                    P.op("act", lambda e, ps=ps: e.activation(out=L1[0:64, :], in_=ps[0:64, 0:TR], func=AF.Tanh), reads=[ps], writes=[L1])
                    P.op("act", lambda e, ps=ps: e.copy(out=L1[64:128, :], in_=ps[64:128, 0:TR]), reads=[ps], writes=[L1])
                else:
                    for hh in range(2):
                        P.op("act", lambda e, ps=ps, hh=hh: e.activation(out=sgd[:, :, hh, :], in_=ps[:, 0:TR].rearrange("p (c t) -> p c t", c=NCH), func=AF.Sigmoid), reads=[ps], writes=[sgd])
            P.op("act", lambda e: e.copy(out=hp[:, :, 0:1], in_=h[:, :, TR - 1:TR]), reads=[h, hp], writes=[hp])
            for g in range(2):
                T = T_
                gc = slice(g * 128, (g + 1) * 128)
                ps = nextpp()
                MM(P, ps[:, 0:TR], b2r[:, 0, gc], L1[:, :], True, True, [b2r, L1], [ps])
                MM(P, ps[:, TR:2 * TR], b2r[:, 1, gc], L1[:, :], True, True, [b2r, L1], [ps])
                P.op("act", lambda e, ps=ps, g=g: e.activation(out=T["sgw"][:, :], in_=ps[:, 0:TR], func=AF.Sigmoid, bias=vec[:, g, 0:1], scale=1.0), reads=[ps, vec], writes=[T["sgw"]])
                P.op("act", lambda e, ps=ps, g=g: e.activation(out=T["aT"][:, :], in_=ps[:, TR:2 * TR], func=AF.Sigmoid, bias=vec[:, g, 1:2], scale=1.0), reads=[ps, vec], writes=[T["aT"]])
                P.op("dve", lambda e, g=g: e.tensor_scalar(out=T["kk"][:, :], in0=kT[g][:, :], scalar1=vec[:, g, 2:3], scalar2=None, op0=ALU.mult), reads=[kT[g], vec], writes=[T["kk"]])
                P.op("act", lambda e: e.activation(out=T["t1"][:, :], in_=T["kk"][:, :], func=AF.Square), reads=[T["kk"]], writes=[T["t1"]])
                MM(P, ps_stat[:, 0:TR], bones[:, :], T["t1"][:, :], True, True, [bones, T["t1"]], [ps_stat])
                P.op("act", lambda e: e.activation(out=T["t2"][:, :], in_=ps_stat[:, 0:TR], func=AF.Sqrt), reads=[ps_stat], writes=[T["t2"]])
                P.op("dve", lambda e: e.tensor_scalar(out=T["t2"][:, :], in0=T["t2"][:, :], scalar1=1e-12, scalar2=None, op0=ALU.max), reads=[T["t2"]], writes=[T["t2"]])
                P.op("dve", lambda e: e.reciprocal(out=T["t2"][:, :], in_=T["t2"][:, :]), reads=[T["t2"]], writes=[T["t2"]])
                P.op("dve", lambda e: e.tensor_tensor(out=T["kkn"][:, :], in0=T["kk"][:, :], in1=T["t2"][:, :], op=ALU.mult), reads=[T["kk"], T["t2"]], writes=[T["kkn"]])
                P.op("dve", lambda e, g=g: e.tensor_scalar(out=T["t1"][:, :], in0=T["aT"][:, :], scalar1=-1.0, scalar2=vec[:, g, 3:4], op0=ALU.add, op1=ALU.mult), reads=[T["aT"], vec], writes=[T["t1"]])
                P.op("dve", lambda e, g=g: e.scalar_tensor_tensor(out=T["kp"][:, :], in0=T["t1"][:, :], scalar=1.0, in1=kT[g][:, :], op0=ALU.add, op1=ALU.mult), reads=[T["t1"], kT[g]], writes=[T["kp"]])
                P.op("dve", lambda e, g=g: e.scalar_tensor_tensor(out=SRC[g][:, QP, :], in0=rT[g][:, :], scalar=vec[:, g, 4:5], in1=T["kp"][:, :], op0=ALU.mult, op1=ALU.mult), reads=[rT[g], vec, T["kp"]], writes=[SRC[g]])
                P.op("act", lambda e, g=g: e.copy(out=SRC[g][:, QV, :], in_=vT[g][:, :]), reads=[vT[g]], writes=[SRC[g]])
                P.op("dve", lambda e: e.tensor_tensor(out=T["ka"][:, :], in0=T["kkn"][:, :], in1=T["aT"][:, :], op=ALU.mult), reads=[T["kkn"], T["aT"]], writes=[T["ka"]])
                P.op("dve", lambda e: e.tensor_tensor_scan(out=T["cum"][:, :], data0=resetm[:, :], data1=T["sgw"][:, :], initial=0.0, op0=ALU.mult, op1=ALU.add), reads=[resetm, T["sgw"]], writes=[T["cum"]])
                P.op("act", lambda e: e.activation(out=T["gam"][:, :], in_=T["cum"][:, :], func=AF.Exp, scale=-C0), reads=[T["cum"]], writes=[T["gam"]])
                P.op("act", lambda e: e.activation(out=T["gami"][:, :], in_=T["cum"][:, :], func=AF.Exp, scale=C0), reads=[T["cum"]], writes=[T["gami"]])
                P.op("dve", lambda e: e.tensor_tensor(out=T["t1"][:, :], in0=T["cum"][:, :], in1=T["sgw"][:, :], op=ALU.subtract), reads=[T["cum"], T["sgw"]], writes=[T["t1"]])
                P.op("act", lambda e: e.activation(out=T["game"][:, :], in_=T["t1"][:, :], func=AF.Exp, scale=-C0), reads=[T["t1"]], writes=[T["game"]])
                for c in range(NCH):
                    P.op("dve", lambda e, c=c: e.tensor_scalar(out=T["t2"][:, c * CH:(c + 1) * CH], in0=T["cum"][:, c * CH:(c + 1) * CH], scalar1=T["cum"][:, c * CH + CH - 1:c * CH + CH], scalar2=None, op0=ALU.subtract),
                         reads=[T["cum"]], writes=[T["t2"]])
                P.op("act", lambda e: e.activation(out=T["rev"][:, :], in_=T["t2"][:, :], func=AF.Exp, scale=C0), reads=[T["t2"]], writes=[T["rev"]])
                P.op("act", lambda e, g=g: e.copy(out=gamc[g][:, :], in_=T["gam"][:, CH - 1:TR:CH]), reads=[T["gam"]], writes=[gamc[g]])
                P.op("dve", lambda e, g=g: e.scalar_tensor_tensor(out=SRC[g][:, QA, :], in0=T["game"][:, :], scalar=-1.0, in1=T["kkn"][:, :], op0=ALU.mult, op1=ALU.mult), reads=[T["game"], T["kkn"]], writes=[SRC[g]])
                P.op("dve", lambda e, g=g: e.tensor_tensor(out=SRC[g][:, QR, :], in0=rT[g][:, :], in1=T["gam"][:, :], op=ALU.mult), reads=[rT[g], T["gam"]], writes=[SRC[g]])
                P.op("dve", lambda e, g=g: e.tensor_tensor(out=SRC[g][:, QB, :], in0=T["ka"][:, :], in1=T["gami"][:, :], op=ALU.mult), reads=[T["ka"], T["gami"]], writes=[SRC[g]])
                P.op("dve", lambda e, g=g: e.tensor_tensor(out=SRC[g][:, QK, :], in0=T["kp"][:, :], in1=T["gami"][:, :], op=ALU.mult), reads=[T["kp"], T["gami"]], writes=[SRC[g]])
                P.op("dve", lambda e, g=g: e.tensor_tensor(out=SRC[g][:, QBH, :], in0=T["ka"][:, :], in1=T["rev"][:, :], op=ALU.mult), reads=[T["ka"], T["rev"]], writes=[SRC[g]])
                P.op("dve", lambda e, g=g: e.tensor_tensor(out=SRC[g][:, QKH, :], in0=T["kp"][:, :], in1=T["rev"][:, :], op=ALU.mult), reads=[T["kp"], T["rev"]], writes=[SRC[g]])
            GS = (0, 1)
            for c in range(NCH):
                cs = slice(c * CH, (c + 1) * CH)
                for g in GS:
                    P.op("act", lambda e, g=g, cs=cs: e.copy(out=EXP[g][0:64, :, 0, :], in_=SRC[g][0:64, :, cs]), reads=[SRC[g]], writes=[EXP[g]])
                    P.op("dve", lambda e, g=g, cs=cs: e.tensor_copy(out=EXP[g][64:128, :, 1, :], in_=SRC[g][64:128, :, cs]), reads=[SRC[g]], writes=[EXP[g]])

                def ex(g, q):
                    return EXP[g][:, q, :, :].rearrange("p h c -> p (h c)")

                def ex2(g, q):
                    return EXP[g][:, q:q + 2, :, :].rearrange("p q h c -> p (q h c)")
                for g in GS:
                    psAB, psLT, psXS, psB4 = bA[g], bX[g], bX[g], bA[g]
                    MM(P, psAB[:, 0:256], ex(g, QB), ex2(g, QA), True, True, [EXP[g]], [psAB])
                    MM(P, psAB[:, 256:512], ex(g, QK), ex2(g, QA), True, True, [EXP[g]], [psAB])
                    P.op("dve", lambda e, psAB=psAB, g=g: e.tensor_tensor(out=NKM[g][:, :], in0=psAB[:, :], in1=masku4[:, :], op=ALU.mult), reads=[psAB, masku4], writes=[NKM[g]])
                    MM(P, psLT[:, 0:128], ex(g, QA), ex(g, QB), True, True, [EXP[g]], [psLT])
                    P.op("dve", lambda e, psLT=psLT, g=g: e.tensor_tensor(out=L0[g][:, :], in0=psLT[:, 0:128], in1=maskl[:, :], op=ALU.mult), reads=[psLT, maskl], writes=[L0[g]])
                    for i, q in enumerate((QA, QBH, QKH, QV)):
                        MM(P, psT[:, i * 128:(i + 1) * 128], ex(g, q), identR[:, :], True, True, [EXP[g], identR], [psT])
                    P.op("act", lambda e, g=g: e.copy(out=XB[g][:, 1:5, :].rearrange("p a b -> p (a b)"), in_=psT[:, :]), reads=[psT], writes=[XB[g]])
                    MM(P, psLT[:, 128:256], NKM[g][:, 256:384], XB[g][:, 4, :], True, True, [NKM[g], XB[g]], [psLT])
                    P.op("act", lambda e, psLT=psLT, g=g: e.copy(out=XB[g][:, 0, :], in_=psLT[:, 128:256]), reads=[psLT], writes=[XB[g]])
                    P.op("act", lambda e, g=g, c=c: e.copy(out=V8[g][:, c, :], in_=XB[g][:, 4, :]), reads=[XB[g]], writes=[V8[g]])
                Ncur = {g: (NKM[g], NKM[g][:, 0:128]) for g in GS}
                Lcur = {g: (L0[g], L0[g][:, :]) for g in GS}
                for i in range(6):
                    for g in GS:
                        psXS = bX[g]
                        nb, nap = Ncur[g]
                        X = XB[g][:, 0:2, :].rearrange("p a b -> p (a b)")
                        MM(P, psXS[:, 0:256], nap, X, True, True, [nb, XB[g]], [psXS])
                        P.op("dve", lambda e, psXS=psXS, X=X: e.tensor_tensor(out=X, in0=psXS[:, 0:256], in1=X, op=ALU.add), reads=[psXS, XB[g]], writes=[XB[g]])
                        if i < 5:
                            lb_, lap = Lcur[g]
                            nl = NL[g][i % 2]
                            MM(P, psXS[:, 256:384], lap, nap, True, True, [lb_, nb], [psXS])
                            if i < 4:
                                MM(P, psXS[:, 384:512], nap, lap, True, True, [lb_, nb], [psXS])
                                P.op("act", lambda e, psXS=psXS, nl=nl: e.copy(out=nl[:, :], in_=psXS[:, 256:512]), reads=[psXS], writes=[nl])
                            else:
                                P.op("act", lambda e, psXS=psXS, nl=nl: e.copy(out=nl[:, 0:128], in_=psXS[:, 256:384]), reads=[psXS], writes=[nl])
                            Ncur[g] = (nl, nl[:, 0:128])
                            Lcur[g] = (nl, nl[:, 128:256])
                for g in GS:
                    psB4, psLT = bA[g], bX[g]
                    Wbd, U0, Bh, Kh, Vb = (XB[g][:, 1, :], XB[g][:, 0, :], XB[g][:, 2, :], XB[g][:, 3, :], XB[g][:, 4, :])
                    MrbT, MrkT = NKM[g][:, 128:256], NKM[g][:, 384:512]
                    MM(P, psB4[:, 0:128], Wbd, Bh, True, True, [XB[g]], [psB4])
                    MM(P, psB4[:, 128:256], Bh, U0, True, False, [XB[g]], [psB4])
                    MM(P, psB4[:, 128:256], Kh, Vb, False, True, [XB[g]], [psB4])
                    MM(P, psB4[:, 256:384], Wbd, MrbT, True, True, [XB[g], NKM[g]], [psB4])
                    MM(P, psB4[:, 384:512], MrbT, U0, True, False, [XB[g], NKM[g]], [psB4])
                    MM(P, psB4[:, 384:512], MrkT, Vb, False, True, [XB[g], NKM[g]], [psB4])
                    P.op("dve", lambda e, psB4=psB4, g=g, c=c: e.scalar_tensor_tensor(out=PT[g][:, :], in0=identf[:, :], scalar=gamc[g][:, c:c + 1], in1=psB4[:, 0:128], op0=ALU.mult, op1=ALU.add),
                         reads=[identf, gamc[g], psB4], writes=[PT[g]])
                    P.op("dve", lambda e, psB4=psB4, g=g: e.tensor_copy(out=Qc[g][:, :], in_=psB4[:, 128:256]), reads=[psB4], writes=[Qc[g]])
                    P.op("dve", lambda e, psB4=psB4, g=g: e.tensor_tensor(out=OmT[g][:, :], in0=psB4[:, 256:384], in1=ex(g, QR), op=ALU.add), reads=[psB4, EXP[g]], writes=[OmT[g]])
                    P.op("dve", lambda e, psB4=psB4, g=g: e.tensor_copy(out=Y0[g][:, :], in_=psB4[:, 384:512]), reads=[psB4], writes=[Y0[g]])
                    MM(P, psLT[:, 256:258], ex(g, QP), onesR[:, 0:2], True, True, [EXP[g], onesR], [psLT])
                    P.op("act", lambda e, psLT=psLT, g=g, c=c: e.copy(out=coef[g][:, c, :], in_=psLT[:, 256:258]), reads=[psLT], writes=[coef[g]])
                for g in GS:
                    Zc = Z[g][zpar[g]]
                    Zn = Z[g][1 - zpar[g]]
                    MM(P, ps_stat[:, 0:128], OmT[g][:, :], Zc[:, :], True, True, [OmT[g], Zc], [ps_stat])
                    MM(P, ps_stat[:, 128:256], PT[g][:, :], Zc[:, :], True, True, [PT[g], Zc], [ps_stat])
                    P.op("dve", lambda e, g=g, c=c: e.tensor_tensor(out=y8[g][:, c, :], in0=ps_stat[:, 0:128], in1=Y0[g][:, :], op=ALU.add), reads=[ps_stat, Y0[g]], writes=[y8[g]])
                    P.op("dve", lambda e, g=g, Zn=Zn: e.tensor_tensor(out=Zn[:, :], in0=ps_stat[:, 128:256], in1=Qc[g][:, :], op=ALU.add), reads=[ps_stat, Qc[g]], writes=[Zn])
                    zpar[g] = 1 - zpar[g]
            for g in GS:
                s8 = st8[g]
                sqv = SRC[g][:, 0:2, :].rearrange("p a (c f) -> p (a c) f", f=128)
                P.op("act", lambda e, g=g, sqv=sqv: e.activation(out=sqv, in_=y8[g][:, :, :], func=AF.Square), reads=[y8[g]], writes=[SRC[g]])
                P.op("dve", lambda e, g=g, s8=s8: e.tensor_reduce(out=s8[:, 0, :], in_=y8[g][:, :, :], axis=AX.X, op=ALU.add), reads=[y8[g]], writes=[s8])
                P.op("dve", lambda e, g=g, s8=s8, sqv=sqv: e.tensor_reduce(out=s8[:, 1, :], in_=sqv, axis=AX.X, op=ALU.add), reads=[SRC[g]], writes=[s8])
                P.op("dve", lambda e, s8=s8: e.tensor_scalar(out=s8[:, 0, :], in0=s8[:, 0, :], scalar1=1.0 / 64, scalar2=None, op0=ALU.mult), reads=[s8], writes=[s8])
                P.op("dve", lambda e, s8=s8: e.tensor_tensor(out=s8[:, 2, :], in0=s8[:, 0, :], in1=s8[:, 0, :], op=ALU.mult), reads=[s8], writes=[s8])
                P.op("dve", lambda e, s8=s8: e.scalar_tensor_tensor(out=s8[:, 3, :], in0=s8[:, 1, :], scalar=1.0 / 64, in1=s8[:, 2, :], op0=ALU.mult, op1=ALU.subtract), reads=[s8], writes=[s8])
                P.op("act", lambda e, s8=s8: e.activation(out=s8[:, 3, :], in_=s8[:, 3, :], func=AF.Sqrt, scale=1.0, bias=eps_ap(P, GN_EPS)), reads=[s8], writes=[s8])
                P.op("dve", lambda e, s8=s8: e.reciprocal(out=s8[:, 3, :], in_=s8[:, 3, :]), reads=[s8], writes=[s8])
                for c in range(NCH):
                    P.op("dve", lambda e, g=g, c=c, s8=s8: e.tensor_scalar(out=y8[g][:, c, :], in0=y8[g][:, c, :], scalar1=s8[:, 0, c:c + 1], scalar2=s8[:, 3, c:c + 1], op0=ALU.subtract, op1=ALU.mult),
                         reads=[y8[g], s8], writes=[y8[g]])
                yf = y8[g][:, :, :].rearrange("p c f -> p (c f)")
                P.op("dve", lambda e, g=g, yf=yf: e.tensor_tensor(out=yf, in0=yf, in1=lg8[g][:, :], op=ALU.mult), reads=[y8[g], lg8[g]], writes=[y8[g]])
                P.op("dve", lambda e, g=g, yf=yf: e.tensor_tensor(out=yf, in0=yf, in1=lb8[g][:, :], op=ALU.add), reads=[y8[g], lb8[g]], writes=[y8[g]])
                for c in range(NCH):
                    P.op("dve", lambda e, g=g, c=c: e.scalar_tensor_tensor(out=y8[g][:, c, :], in0=V8[g][:, c, :], scalar=coef[g][:, c, 0:1], in1=y8[g][:, c, :], op0=ALU.mult, op1=ALU.add),
                         reads=[V8[g], coef[g], y8[g]], writes=[y8[g]])
                psg_ = nextpp()
                for c in range(NCH):
                    MM(P, psg_[:, c * 128:(c + 1) * 128], sgd[:, c, :, :].rearrange("p h t -> p (h t)"), b2r[:, 2, g * 128:(g + 1) * 128], True, True, [sgd, b2r], [psg_])
                P.op("dve", lambda e, g=g, yf=yf, psg_=psg_: e.tensor_tensor(out=yf, in0=psg_[:, 0:NCH * 128], in1=yf, op=ALU.mult), reads=[psg_, y8[g]], writes=[y8[g]])
                P.dma("sp", [lambda e, g=g, hh=hh, t0=t0: e.dma_start(out=yo[g, hh, t0:t0 + TR, :].rearrange("(c t) f -> t c f", c=NCH), in_=y8[g][hh * 64:(hh + 1) * 64, :, hh * 64:(hh + 1) * 64]) for hh in range(2)],
                      reads=[y8[g]], sem_buf=y8[g])
        stats = P.emit()
    return nc, stats


def prep_kr_core(p, bq):
    ch0 = bq * 256
    f = np.float32

    def colchunk(w, c0, n=128):
        return np.ascontiguousarray(w[:, c0:c0 + n]).reshape(KC, 128, n).transpose(1, 0, 2)
    chunks = []
    for w in (p["w_r"], p["w_k"], p["w_v"]):
        for g in range(2):
            chunks.append(colchunk(w, ch0 + g * 128))
    chunks.append(np.concatenate([colchunk(p["wla"], 0, 64), colchunk(p["ala"], 0, 64)], 2))
    chunks.append(colchunk(p["gla"], 0, 128))
    wcat = np.ascontiguousarray(np.stack(chunks)).reshape(8, 128, KC * 128).astype(f)
    b2 = np.zeros((3, 128, 256), f)
    b2[0, 0:64] = p["wlb"][:, ch0:ch0 + 256]
    b2[1, 64:128] = p["alb"][:, ch0:ch0 + 256]
    b2[2] = p["glb"][:, ch0:ch0 + 256]
    vec = np.zeros((128, 2, 8), f)
    for g in range(2):
        sl = slice(ch0 + g * 128, ch0 + (g + 1) * 128)
        vec[:, g, 0] = p["w0"][sl]
        vec[:, g, 1] = p["a0"][sl]
        vec[:, g, 2] = p["k_k"][sl]
        vec[:, g, 3] = p["k_a"][sl]
        vec[:, g, 4] = p["r_k"].reshape(-1)[sl]
    lg8 = np.zeros((2, 128, NCH, 128), f)
    lb8 = np.zeros((2, 128, NCH, 128), f)
    for g in range(2):
        for hh in range(2):
            sl = slice(ch0 + g * 128 + hh * 64, ch0 + g * 128 + hh * 64 + 64)
            lg8[g, hh * 64:(hh + 1) * 64, :, hh * 64:(hh + 1) * 64] = p["ln_g"][sl][None, None, :]
            lb8[g, hh * 64:(hh + 1) * 64, :, hh * 64:(hh + 1) * 64] = p["ln_b"][sl][None, None, :]
    mus = np.ascontiguousarray(p["mu"].reshape(6, KC, 128).transpose(2, 0, 1)).astype(f)
    return dict(wcat=wcat, b2=b2, vec=vec, lg8=lg8.reshape(2, 128, NCH * 128), lb8=lb8.reshape(2, 128, NCH * 128), mus=mus)


_PROGS = {}


def _prog(key, builder):
    if key not in _PROGS:
        _PROGS[key] = builder()[0]
    return _PROGS[key]


def _run(nc, in_maps):
    res = run_bass_kernel_spmd(nc, in_maps, core_ids=list(range(NCORES)))
    return res.results


def _f32(a):
    return np.ascontiguousarray(np.asarray(a, dtype=np.float32))


def kernel(**inp):
    x = _f32(inp["x"])
    B, S, _ = x.shape
    NT = B * S // NCORES
    CPB = NCORES // B
    c = _f32(inp["c"])
    pos = np.ascontiguousarray(np.asarray(inp["positions"]).astype(np.int32, copy=False))
    ada_w = _f32(inp["ada_w"])
    ada_b = _f32(inp["ada_b"])
    xT = [np.ascontiguousarray(x[b].T) for b in range(B)]
    cTs = [prep_c(c[b:b + 1]) for b in range(B)]

    def tail_common(i):
        g, u, d = prep_ffn(_f32(inp["ffn_w_gate"][i]), _f32(inp["ffn_w_up"][i]), _f32(inp["ffn_w_down"][i]))
        return dict(wg=g, wu=u, wd=d, adaw=prep_adaw(ada_w[i]), adab=prep_adab(ada_b[i]), gf=prep_vec(_f32(inp["norm_ffn_g"][i])))

    def run_tail(i, yT, w_o):
        nc = _prog(("kb", NT), lambda: build_kb(NT, False))
        common = tail_common(i)
        common["wo"] = prep_sq(_f32(w_o))
        maps = []
        for cid in range(NCORES):
            b, q = cid // CPB, cid % CPB
            m = dict(common)
            m["xT"] = np.ascontiguousarray(xT[b][:, q * NT:(q + 1) * NT])
            m["yT"] = prep_y_pieces(np.ascontiguousarray(yT[b][:, q * NT:(q + 1) * NT]), NT)
            m["cT"] = cTs[b]
            maps.append(m)
        res = _run(nc, maps)
        for b in range(B):
            xT[b] = np.ascontiguousarray(np.concatenate([res[b * CPB + q]["out"] for q in range(CPB)], axis=1))

    def run_attn(i, j):
        nc = _prog(("ka", S), lambda: build_ka(S))
        common = dict(adaw=prep_adaw(ada_w[i]), adab=prep_adab(ada_b[i]), gm=prep_vec(_f32(inp["norm_mix_g"][i])))
        qg = _f32(inp["attn_q_gain"][j])
        kg = _f32(inp["attn_k_gain"][j])
        common["gains"] = np.ascontiguousarray(np.stack([np.tile(qg, 2), np.tile(kg, 2)], 1))
        common.update(ka_consts())
        wqkv = _f32(inp["attn_w_qkv"][j])
        maps = []
        for cid in range(NCORES):
            b, bq = cid // CPB, cid % CPB
            m = dict(common)
            m["xT"] = xT[b]
            m["cT"] = cTs[b]
            m["posb"] = np.ascontiguousarray(np.broadcast_to(pos[b][None, :], (128, S)))
            m["wq"] = np.stack([prep_wq(wqkv, bq, hp) for hp in range(2)])
            maps.append(m)
        res = _run(nc, maps)
        yT = []
        for b in range(B):
            yT.append(np.ascontiguousarray(np.concatenate([res[b * CPB + bq]["oT"].reshape(256, S) for bq in range(CPB)], axis=0)))
        return yT

    def run_conv(i, j):
        nc = _prog(("kc", NT), lambda: build_kb(NT, True, conv=True))
        common = tail_common(i)
        common["wo"] = prep_sq(_f32(inp["conv_w_pw2"][j]))
        common["bo"] = prep_vec(_f32(inp["conv_b_pw2"][j]))
        common["w1"] = prep_w1(_f32(inp["conv_w_pw1"][j]))
        b1 = _f32(inp["conv_b_pw1"][j])
        common["cvec"] = np.ascontiguousarray(np.stack([prep_vec(b1[:1024]), prep_vec(b1[1024:]), prep_vec(_f32(inp["conv_b_dw"][j])),
                                                        prep_vec(_f32(inp["conv_ln_g"][j])), prep_vec(_f32(inp["conv_ln_b"][j])),
                                                        prep_vec(_f32(inp["norm_mix_g"][i]))], 1))
        common["wdw"] = np.ascontiguousarray(_f32(inp["conv_w_dw"][j]).T.reshape(KC, 128, 31).transpose(1, 0, 2))
        maps = []
        for cid in range(NCORES):
            b, q = cid // CPB, cid % CPB
            m = dict(common)
            m["xT"] = np.ascontiguousarray(xT[b][:, q * NT:(q + 1) * NT])
            m["cT"] = cTs[b]
            if q == 0:
                m["xh"] = np.zeros((D, 32), np.float32)
                m["hon"] = np.zeros((128, 1), np.float32)
            else:
                m["xh"] = np.ascontiguousarray(xT[b][:, q * NT - 32:q * NT])
                m["hon"] = np.ones((128, 1), np.float32)
            maps.append(m)
        res = _run(nc, maps)
        for b in range(B):
            xT[b] = np.ascontiguousarray(np.concatenate([res[b * CPB + q]["out"] for q in range(CPB)], axis=1))

    def run_rwkv(i, j):
        nc = _prog(("kr", S), lambda: build_kr(S))
        common = dict(adaw=prep_adaw(ada_w[i]), adab=prep_adab(ada_b[i]), gm=prep_vec(_f32(inp["norm_mix_g"][i])))
        common.update(kr_consts())
        p = dict(mu=_f32(inp["rwkv_mu"][j]), w_r=_f32(inp["rwkv_w_r"][j]), w_k=_f32(inp["rwkv_w_k"][j]), w_v=_f32(inp["rwkv_w_v"][j]),
                 w0=_f32(inp["rwkv_w0"][j]), wla=_f32(inp["rwkv_w_lora_a"][j]), wlb=_f32(inp["rwkv_w_lora_b"][j]),
                 a0=_f32(inp["rwkv_a0"][j]), ala=_f32(inp["rwkv_a_lora_a"][j]), alb=_f32(inp["rwkv_a_lora_b"][j]),
                 gla=_f32(inp["rwkv_g_lora_a"][j]), glb=_f32(inp["rwkv_g_lora_b"][j]), k_k=_f32(inp["rwkv_k_k"][j]), k_a=_f32(inp["rwkv_k_a"][j]),
                 r_k=_f32(inp["rwkv_r_k"][j]), ln_g=_f32(inp["rwkv_ln_g"][j]), ln_b=_f32(inp["rwkv_ln_b"][j]))
        maps = []
        for cid in range(NCORES):
            b, bq = cid // CPB, cid % CPB
            m = dict(common)
            m["xT"] = xT[b]
            m["cT"] = cTs[b]
            m.update(prep_kr_core(p, bq))
            maps.append(m)
        res = _run(nc, maps)
        yT = []
        for b in range(B):
            ys = [res[b * CPB + bq]["yo"].transpose(2, 0, 1, 3).reshape(S, 256) for bq in range(CPB)]
            yT.append(np.ascontiguousarray(np.concatenate(ys, axis=1).T))
        return yT

    depth = ada_w.shape[0]
    for i in range(depth):
        kind, j = i % 3, i // 3
        if kind == 0:
            yT = run_attn(i, j)
            run_tail(i, yT, inp["attn_w_o"][j])
        elif kind == 1:
            run_conv(i, j)
        else:
            yT = run_rwkv(i, j)
            run_tail(i, yT, inp["rwkv_w_o"][j])
    out = np.stack([np.ascontiguousarray(xT[b].T) for b in range(B)]).astype(np.float32)
    return out
```
